# Optimizing a Trainium2 kernel written in Bass

```python
import math
import jax
import jax.numpy as jnp
from jax import lax
import numpy as np

D_MODEL = 2048
BATCH = 2
SEQ = 4096
DEPTH = 2

N_MIXERS = 2
ATT_HEADS = 16
ATT_HEAD_DIM = D_MODEL // ATT_HEADS
DILATED_PATTERNS = ((128, 1), (512, 4), (2048, 16))
ATT_BLOCK = 128
LSTM_HEADS = 4
LSTM_V_DIM = D_MODEL // LSTM_HEADS
LSTM_QK_DIM = LSTM_V_DIM // 2
LSTM_QK_WIDTH = LSTM_HEADS * LSTM_QK_DIM
LSTM_IN_WIDTH = 2 * LSTM_QK_WIDTH + 2 * D_MODEL + 2 * LSTM_HEADS
LSTM_CHUNK = 64
LSTM_CONV = 4
FFN_DIM = ((8 * D_MODEL // 3 + 255) // 256) * 256
FFN_CONV = 3
NORM_EPS = 1e-6

kernel_name = 'hybrid_dilated_attn_mlstm_convffn'


def rms_norm(x, g):
    xf = x.astype(jnp.float32)
    y = xf * lax.rsqrt(jnp.mean(xf * xf, axis=-1, keepdims=True) + NORM_EPS)
    return y * g.astype(jnp.float32)


def causal_dwconv(x, w, b):
    K = w.shape[0]
    S = x.shape[1]
    xp = jnp.pad(x, ((0, 0), (K - 1, 0), (0, 0)))
    y = b
    for j in range(K):
        y = y + w[j] * xp[:, j:j + S]
    return y


def dilated_branch(q, k, v, window, dil):
    B, H, S, hd = q.shape
    n_sub = -(-S // dil)
    nb = -(-n_sub // ATT_BLOCK)
    lp = nb * ATT_BLOCK
    pad = lp * dil - S

    def to_blocks(t):
        t = jnp.pad(t, ((0, 0), (0, 0), (0, pad), (0, 0)))
        t = t.reshape(B, H, lp, dil, hd).transpose(0, 1, 3, 2, 4)
        return t.reshape(B, H, dil, nb, ATT_BLOCK, hd)

    def with_prev(t):
        prev = jnp.pad(t[:, :, :, :-1], ((0, 0), (0, 0), (0, 0), (1, 0), (0, 0), (0, 0)))
        return jnp.concatenate([prev, t], axis=4)

    qb = to_blocks(q)
    kb = with_prev(to_blocks(k))
    vb = with_prev(to_blocks(v))
    s = jnp.einsum('bhrnid,bhrnjd->bhrnij', qb, kb)
    i = jnp.arange(ATT_BLOCK)[:, None]
    j = jnp.arange(2 * ATT_BLOCK)[None, :]
    dist = ATT_BLOCK + i - j
    band = (dist >= 0) & (dist <= window // dil)
    has_prev = (jnp.arange(nb) > 0)[:, None, None] | (j >= ATT_BLOCK)[None]
    valid = band[None] & has_prev
    s = jnp.where(valid, s, -jnp.inf)
    mx = jnp.max(s, axis=-1, keepdims=True)
    p = jnp.exp(s - mx)
    den = jnp.sum(p, axis=-1)
    o = jnp.einsum('bhrnij,bhrnjd->bhrnid', p, vb) / den[..., None]
    lse = mx[..., 0] + jnp.log(den)
    o = o.reshape(B, H, dil, lp, hd).transpose(0, 1, 3, 2, 4).reshape(B, H, lp * dil, hd)[:, :, :S]
    lse = lse.reshape(B, H, dil, lp).transpose(0, 1, 3, 2).reshape(B, H, lp * dil)[:, :, :S]
    return o, lse


def dilated_attention(x, norm_g, w_qkv, q_gain, k_gain, w_o):
    B, S, _ = x.shape
    h = rms_norm(x, norm_g).astype(x.dtype)
    z = (h @ w_qkv).reshape(B, S, 3, ATT_HEADS, ATT_HEAD_DIM)
    z = z.astype(jnp.float32).transpose(2, 0, 3, 1, 4)
    q = rms_norm(z[0], q_gain) * (ATT_HEAD_DIM ** -0.5)
    k = rms_norm(z[1], k_gain)
    v = z[2]
    outs = []
    lses = []
    for window, dil in DILATED_PATTERNS:
        o, l = dilated_branch(q, k, v, window, dil)
        outs.append(o)
        lses.append(l)
    wts = jax.nn.softmax(jnp.stack(lses), axis=0)
    o = jnp.einsum('gbhs,gbhsd->bshd', wts, jnp.stack(outs)).reshape(B, S, D_MODEL)
    return o.astype(x.dtype) @ w_o


def mlstm_mixer(x, norm_g, w_in, gate_bias, conv_w, conv_b, head_gain, w_out):
    B, S, _ = x.shape
    H, dk, dv, L = LSTM_HEADS, LSTM_QK_DIM, LSTM_V_DIM, LSTM_CHUNK
    QKW = LSTM_QK_WIDTH
    h = rms_norm(x, norm_g).astype(x.dtype)
    z = h @ w_in
    qk = jax.nn.silu(causal_dwconv(z[..., :2 * QKW], conv_w, conv_b)).astype(jnp.float32)
    q = qk[..., :QKW].reshape(B, S, H, dk).transpose(0, 2, 1, 3)
    k = qk[..., QKW:].reshape(B, S, H, dk).transpose(0, 2, 1, 3) * (dk ** -0.5)
    v = z[..., 2 * QKW:2 * QKW + D_MODEL].astype(jnp.float32).reshape(B, S, H, dv).transpose(0, 2, 1, 3)
    o_gate = jax.nn.sigmoid(z[..., 2 * QKW + D_MODEL:2 * QKW + 2 * D_MODEL].astype(jnp.float32))
    gates = (z[..., 2 * QKW + 2 * D_MODEL:].astype(jnp.float32) + gate_bias.astype(jnp.float32)).transpose(0, 2, 1)
    log_i = gates[:, :H]
    log_f = jax.nn.log_sigmoid(gates[:, H:])
    nc = S // L

    def chunks(t):
        return jnp.moveaxis(t.reshape((B, H, nc, L) + t.shape[3:]), 2, 0)

    causal = jnp.tril(jnp.ones((L, L), dtype=bool))

    def step(carry, inp):
        C, n, m = carry
        qc, kc, vc, lic, lfc = inp
        b = jnp.cumsum(lfc, axis=-1)
        D = jnp.where(causal, b[..., :, None] - b[..., None, :] + lic[..., None, :], -jnp.inf)
        g = b + m[..., None]
        m_t = jnp.maximum(g, jnp.max(D, axis=-1))
        P = jnp.exp(D - m_t[..., None])
        inter = jnp.exp(g - m_t)
        W = P * jnp.einsum('bhld,bhsd->bhls', qc, kc)
        num = inter[..., None] * jnp.einsum('bhld,bhde->bhle', qc, C) + jnp.einsum('bhls,bhse->bhle', W, vc)
        den = inter * jnp.einsum('bhld,bhd->bhl', qc, n) + jnp.sum(W, axis=-1)
        h_c = num / jnp.maximum(jnp.abs(den), jnp.exp(-m_t))[..., None]
        bL = b[..., -1]
        a = bL[..., None] - b + lic
        m_new = jnp.maximum(bL + m, jnp.max(a, axis=-1))
        decay = jnp.exp(bL + m - m_new)
        wts = jnp.exp(a - m_new[..., None])
        C_new = decay[..., None, None] * C + jnp.einsum('bhs,bhsd,bhse->bhde', wts, kc, vc)
        n_new = decay[..., None] * n + jnp.einsum('bhs,bhsd->bhd', wts, kc)
        return (C_new, n_new, m_new), h_c

    init = (jnp.zeros((B, H, dk, dv), jnp.float32), jnp.zeros((B, H, dk), jnp.float32), jnp.zeros((B, H), jnp.float32))
    _, hs = lax.scan(step, init, (chunks(q), chunks(k), chunks(v), chunks(log_i), chunks(log_f)))
    hs = jnp.moveaxis(hs, 0, 2).reshape(B, H, S, dv).transpose(0, 2, 1, 3)
    hs = rms_norm(hs, head_gain.reshape(H, dv)).reshape(B, S, D_MODEL) * o_gate
    return hs.astype(x.dtype) @ w_out


def conv_ffn(x, norm_g, w_up, conv_w, conv_b, w_down):
    h = rms_norm(x, norm_g).astype(x.dtype)
    u = causal_dwconv(h @ w_up, conv_w, conv_b)
    gate = u[..., :FFN_DIM]
    up = u[..., FFN_DIM:]
    return (jax.nn.silu(gate) * up) @ w_down


def setup_inputs(seed: int = 0) -> dict:
    key = jax.random.key(seed)
    ks = jax.random.split(key, 24)
    n_a = (DEPTH + 1) // 2
    n_b = DEPTH // 2
    res = (2 * DEPTH) ** -0.5
    f32 = jnp.float32

    def nrm(k, shape, scale):
        return scale * jax.random.normal(k, shape, f32)

    def gain(k, shape):
        return 1.0 + nrm(k, shape, 0.05)

    def conv_init(k, n, width, ch):
        ident = jnp.zeros((width, ch), f32).at[width - 1].set(1.0)
        return ident[None] + nrm(k, (n, width, ch), 0.3)

    x = jax.random.normal(ks[0], (BATCH, SEQ, D_MODEL), f32)
    lstm_gate_bias = jnp.concatenate([
        nrm(ks[8], (n_b, LSTM_HEADS), 0.1),
        jnp.linspace(3.0, 6.0, LSTM_HEADS, dtype=f32)[None] + nrm(ks[9], (n_b, LSTM_HEADS), 0.1)], axis=1)
    return {
        'x': x,
        'attn_norm': gain(ks[1], (n_a, D_MODEL)),
        'attn_w_qkv': nrm(ks[2], (n_a, D_MODEL, 3 * D_MODEL), D_MODEL ** -0.5),
        'attn_q_gain': gain(ks[3], (n_a, ATT_HEAD_DIM)),
        'attn_k_gain': gain(ks[4], (n_a, ATT_HEAD_DIM)),
        'attn_w_o': nrm(ks[5], (n_a, D_MODEL, D_MODEL), res * D_MODEL ** -0.5),
        'lstm_norm': gain(ks[6], (n_b, D_MODEL)),
        'lstm_w_in': nrm(ks[7], (n_b, D_MODEL, LSTM_IN_WIDTH), D_MODEL ** -0.5),
        'lstm_gate_bias': lstm_gate_bias,
        'lstm_conv_w': conv_init(ks[10], n_b, LSTM_CONV, 2 * LSTM_QK_WIDTH),
        'lstm_conv_b': nrm(ks[11], (n_b, 2 * LSTM_QK_WIDTH), 0.01),
        'lstm_head_gain': gain(ks[12], (n_b, D_MODEL)),
        'lstm_w_out': nrm(ks[13], (n_b, D_MODEL, D_MODEL), res * D_MODEL ** -0.5),
        'ffn_norm': gain(ks[14], (DEPTH, D_MODEL)),
        'ffn_w_up': nrm(ks[15], (DEPTH, D_MODEL, 2 * FFN_DIM), D_MODEL ** -0.5),
        'ffn_conv_w': conv_init(ks[16], DEPTH, FFN_CONV, 2 * FFN_DIM),
        'ffn_conv_b': nrm(ks[17], (DEPTH, 2 * FFN_DIM), 0.01),
        'ffn_w_down': nrm(ks[18], (DEPTH, FFN_DIM, D_MODEL), res * FFN_DIM ** -0.5),
    }


def reference(x, attn_norm, attn_w_qkv, attn_q_gain, attn_k_gain, attn_w_o,
              lstm_norm, lstm_w_in, lstm_gate_bias, lstm_conv_w, lstm_conv_b, lstm_head_gain, lstm_w_out,
              ffn_norm, ffn_w_up, ffn_conv_w, ffn_conv_b, ffn_w_down):
    for i in range(DEPTH):
        j = i // N_MIXERS
        if i % N_MIXERS == 0:
            x = x + dilated_attention(x, attn_norm[j], attn_w_qkv[j], attn_q_gain[j], attn_k_gain[j], attn_w_o[j])
        else:
            x = x + mlstm_mixer(x, lstm_norm[j], lstm_w_in[j], lstm_gate_bias[j], lstm_conv_w[j],
                                lstm_conv_b[j], lstm_head_gain[j], lstm_w_out[j])
        x = x + conv_ffn(x, ffn_norm[i], ffn_w_up[i], ffn_conv_w[i], ffn_conv_b[i], ffn_w_down[i])
    return x
```

```python
from contextlib import ExitStack
import os


import numpy as np
import ml_dtypes
import concourse.bass as bass
import concourse.mybir as mybir
from concourse.bass_utils import run_bass_kernel_spmd

F32 = mybir.dt.float32
BF16 = mybir.dt.bfloat16
AF = mybir.ActivationFunctionType
ALU = mybir.AluOpType
AX = mybir.AxisListType


class Tok:
    __slots__ = ("sem", "val")

    def __init__(self, sem, val):
        self.sem, self.val = sem, val


class Buf:
    def __init__(self, name=""):
        self.name = name
        self.w = {}
        self.r = {}


def _merge(d, tok):
    k = id(tok.sem)
    if k not in d or d[k].val < tok.val:
        d[k] = tok


class Eng:
    def __init__(self, h, sem, skip_self=False):
        self.h, self.sem, self.count, self.waited = h, sem, 0, {}
        self.skip_self = skip_self

    def wait(self, deps):
        for d in deps:
            if d is None:
                continue
            k = id(d.sem)
            if self.waited.get(k, 0) >= d.val:
                continue
            if getattr(self, "skip_self", False) and d.sem is self.sem:
                continue
            self.h.wait_ge(d.sem, d.val)
            self.waited[k] = d.val

    @staticmethod
    def _deps(reads, writes):
        deps = []
        for b in reads:
            deps.extend(b.w.values())
        for b in writes:
            deps.extend(b.w.values())
            deps.extend(b.r.values())
        return deps

    @staticmethod
    def _commit(tok, reads, writes):
        for b in reads:
            _merge(b.r, tok)
        for b in writes:
            _merge(b.w, tok)

    def op(self, fn, reads=(), writes=(), extra=()):
        self.wait(self._deps(reads, writes))
        self.wait(extra)
        inst = fn(self.h)
        self.count += 1
        inst.then_inc(self.sem, 1)
        tok = Tok(self.sem, self.count)
        self._commit(tok, reads, writes)
        return tok

    def group(self, fns, reads=(), writes=(), extra=()):
        self.wait(self._deps(reads, writes))
        self.wait(extra)
        inst = None
        for fn in fns:
            inst = fn(self.h)
        self.count += 1
        inst.then_inc(self.sem, 1)
        tok = Tok(self.sem, self.count)
        self._commit(tok, reads, writes)
        return tok


class DmaSem:
    def __init__(self, sem):
        self.sem, self.count = sem, 0


class DmaQ:
    def __init__(self, h):
        self.h, self.waited = h, {}

    wait = Eng.wait

    def dma(self, out, in_, dsem, reads=(), writes=(), extra=(), **kw):
        self.wait(Eng._deps(reads, writes))
        self.wait(extra)
        self.h.dma_start(out=out, in_=in_, **kw).then_inc(dsem.sem, 16)
        dsem.count += 16
        tok = Tok(dsem.sem, dsem.count)
        Eng._commit(tok, reads, writes)
        return tok


NDP = int(os.environ.get('NDP', 22))
SKIPDOWN = int(os.environ.get('SKIPDOWN', 0))
SKIPNORM = int(os.environ.get('SKIPNORM', 0))

NT = 1028
TILES = [(0, 4), (4, 516), (516, 1028)]
NPAIR = 44
NGRP = 11
FFN = 5632


class Kit:
    def __init__(self, nc, es):
        self.nc = nc
        sem = lambda n: es.enter_context(nc.semaphore(n))
        self.PE = Eng(nc.tensor, sem("s_pe"), skip_self=True)
        self.ACT = Eng(nc.scalar, sem("s_act"))
        self.DVE = Eng(nc.vector, sem("s_dve"))
        self.POOL = Eng(nc.gpsimd, sem("s_pool"))
        self.SP = DmaQ(nc.sync)
        self.GQ = DmaQ(nc.gpsimd)
        self.es = es
        self.banks = [es.enter_context(nc.psum_tensor(f"bank{i}", [128, 512], F32)) for i in range(7)]
        self.bankB = [Buf(f"bank{i}") for i in range(8)]
        self._nsem = 0

    def dsem(self):
        self._nsem += 1
        return DmaSem(self.es.enter_context(self.nc.semaphore(f"dsem{self._nsem}")))

    def barrier(self):
        toks = [Tok(e.sem, e.count) for e in (self.PE, self.ACT, self.DVE, self.POOL) if e.count > 0]
        toks += self._dtoks()
        for e in (self.PE, self.ACT, self.DVE, self.POOL, self.SP, self.GQ):
            e.wait(toks)

    def _dtoks(self):
        return [Tok(d.sem, d.count) for d in getattr(self, "_dsems", []) if d.count > 0]


def phase_of(nc, K, d, do_outproj, last, tag="of"):
    PE, ACT, DVE, POOL, SP, GQ = K.PE, K.ACT, K.DVE, K.POOL, K.SP, K.GQ
    with ExitStack() as es:
        sb = lambda name, shape, dt: es.enter_context(nc.sbuf_tensor(f"{tag}_{name}", shape, dt))
        xT = sb("xT", [128, 16, NT], F32)
        hT = sb("hT", [128, 16, NT], BF16)
        gb = sb("gb", [128, 2, 4, 1026], BF16)
        wb = sb("wb", [128, 3, 16, 256], BF16)
        wd = sb("wd", [128, 2, 4, 2048], BF16)
        ub = [sb(f"ub{i}", [128, NT], F32) for i in range(2)]
        yy = [sb(f"yy{i}", [128, NT], F32) for i in range(2)]
        sg = sb("sg", [128, NT], F32)
        cw = sb("cw", [128, 88, 3], F32)
        cb = sb("cb", [128, 88], F32)
        gn = sb("gn", [128, 16], F32)
        hv = sb("hv", [128, 1], F32)
        ones = sb("ones", [128, 128], BF16)
        rt = sb("rt", [128, 512], F32)
        rstd = sb("rstd", [128, 512], F32)

        xB = [Buf(f"x{j}") for j in range(16)]
        hB, gB, wbB, wdB = Buf("h"), [Buf(), Buf()], [Buf(), Buf(), Buf()], [Buf(), Buf()]
        ubB, yB, sgB = [Buf(), Buf()], [Buf(), Buf()], Buf()
        cB, onesB, rtB, rstdB = Buf("c"), Buf("ones"), Buf(), Buf()
        outB = Buf("out")
        ld_x, ld_c, st_o = K.dsem(), K.dsem(), K.dsem()
        ld_wb = [K.dsem() for _ in range(3)]
        ld_wd = [K.dsem() for _ in range(2)]
        bank7 = es.enter_context(nc.psum_tensor(f"{tag}_bank7", [128, 512], F32))
        bank, bankB = K.banks + [bank7], K.bankB
        U = [0, 1, 2, 3]
        Hb, D, Nb = 4, [5, 6], 7

        xin_v = d["xin"].rearrange("(c p) t -> p c t", p=128)
        for q in range(4):
            SP.dma(xT[:, 4 * q:4 * q + 4, :], xin_v[:, 4 * q:4 * q + 4, :], ld_x, writes=xB[4 * q:4 * q + 4])
        if do_outproj:
            SP.dma(hT[:], d["m"].rearrange("(c p) t -> p c t", p=128), ld_x, writes=[hB])
        for b_ in xB + [hB]:
            _merge(b_.w, Tok(ld_x.sem, ld_x.count))
        for (t, src) in ((cw, d["cw"]), (cb, d["cb"]), (gn, d["gn"]), (hv, d["hv"])):
            SP.dma(t[:], src, ld_c, writes=[cB])
        POOL.op(lambda h: h.memset(ones[:], 1.0), writes=[onesB])

        wb_n = [0]

        def load_wblk(src_cols):
            s = wb_n[0] % 3
            wb_n[0] += 1
            GQ.dma(wb[:, s], src_cols.rearrange("(c p) n -> p c n", p=128), ld_wb[s], writes=[wbB[s]])
            return s

        ring = {"u": 0, "d": 0}

        def next_bank(kind):
            lst = U if kind == "u" else D
            b = lst[ring[kind] % len(lst)]
            ring[kind] += 1
            return b

        def mm_group(bk, n, lhs_fn, rhs_fn, nch, reads):
            fns = []
            for c in range(nch):
                fns.append(lambda h, c=c: h.matmul(bank[bk][:, 0:n], lhs_fn(c), rhs_fn(c), start=(c == 0), stop=(c == nch - 1)))
            return PE.group(fns, reads=reads, writes=[bankB[bk]])

        if do_outproj:
            for blk in range(8):
                s = load_wblk(d["wo"][:, blk * 256:(blk + 1) * 256])
                for jj in range(2):
                    j = blk * 2 + jj
                    for (lo, hi) in TILES:
                        n = hi - lo
                        bk = Hb if n == 4 else next_bank("u")
                        mm_group(bk, n, lambda c: wb[:, s, c, jj * 128:(jj + 1) * 128], lambda c: hT[:, c, lo:hi], 16, [wbB[s], hB])
                        DVE.op(lambda h: h.tensor_tensor(out=xT[:, j, lo:hi], in0=bank[bk][:, 0:n], in1=xT[:, j, lo:hi], op=ALU.add),
                               reads=[bankB[bk]], writes=[xB[j]])

        sqv = wd[:, 0].rearrange("p a (b t) -> p (a b) t", t=512)
        for (lo, hi) in ([] if SKIPNORM else TILES):
            n = hi - lo
            ACT.op(lambda h: h.activation(out=sqv[:, :, 0:n], in_=xT[:, :, lo:hi], func=AF.Square), reads=xB, writes=[wdB[0]])
            mm_group(Nb, n, lambda c: ones[:], lambda c: sqv[:, c, 0:n], 16, [wdB[0], onesB])
            DVE.op(lambda h: h.tensor_scalar(out=rt[:, 0:n], in0=bank[Nb][:, 0:n], scalar1=1.0 / 2048.0, scalar2=1e-6, op0=ALU.mult, op1=ALU.add),
                   reads=[bankB[Nb]], writes=[rtB])
            ACT.op(lambda h: h.activation(out=rt[:, 0:n], in_=rt[:, 0:n], func=AF.Sqrt), reads=[rtB], writes=[rtB])
            DVE.op(lambda h: h.reciprocal(out=rstd[:, 0:n], in_=rt[:, 0:n]), reads=[rtB], writes=[rstdB])
            for c in range(16):
                DVE.op(lambda h: h.scalar_tensor_tensor(out=hT[:, c, lo:hi], in0=xT[:, c, lo:hi], scalar=gn[:, c:c + 1], in1=rstd[:, 0:n],
                                                        op0=ALU.mult, op1=ALU.mult),
                       reads=[xB[c], rstdB, cB], writes=[hB])

        def load_wd(grp):
            s = grp % 2
            src = d["wdn"][grp * 512:(grp + 1) * 512, :].rearrange("(a p) n -> p a n", p=128)
            GQ.dma(wd[:, s], src, ld_wd[s], writes=[wdB[s]])

        def down(grp):
            s = grp % 2
            if SKIPDOWN: return
            for dj in range(16):
                for (lo, hi) in TILES:
                    n = hi - lo
                    if n == 4:
                        if last:
                            continue
                        lo, n = 2, 2
                    glo = lo - 2
                    bk = Hb if n == 2 else next_bank("d")
                    mm_group(bk, n, lambda a: wd[:, s, a, dj * 128:(dj + 1) * 128], lambda a: gb[:, s, a, glo:glo + n], 4, [wdB[s], gB[s]])
                    DVE.op(lambda h: h.tensor_tensor(out=xT[:, dj, lo:lo + n], in0=bank[bk][:, 0:n], in1=xT[:, dj, lo:lo + n], op=ALU.add),
                           reads=[bankB[bk]], writes=[xB[dj]])

        for dp in range(NDP):
            if dp % 2 == 0:
                load_wd(dp // 2)
            slots = (load_wblk(d["wup"][:, dp * 256:(dp + 1) * 256]),
                     load_wblk(d["wup"][:, FFN + dp * 256:FFN + (dp + 1) * 256]))
            for jj in range(2):
                j = 2 * dp + jj
                grp, gi = j // 4, j % 4
                gs = grp % 2
                for kind in range(2):
                    s = slots[kind]
                    tl = kind * 44 + j
                    for (lo, hi) in TILES:
                        n = hi - lo
                        bk = Hb if n == 4 else next_bank("u")
                        mm_group(bk, n, lambda c: wb[:, s, c, jj * 128:(jj + 1) * 128], lambda c: hT[:, c, lo:hi], 16, [wbB[s], hB])
                        ACT.op(lambda h: h.activation(out=ub[kind][:, lo:hi], in_=bank[bk][:, 0:n], func=AF.Copy),
                               reads=[bankB[bk]], writes=[ubB[kind]])
                        ACT.op(lambda h: h.activation(out=yy[kind][:, lo:hi], in_=bank[bk][:, 0:n], func=AF.Identity,
                                                      bias=cb[:, tl:tl + 1], scale=cw[:, tl, 2:3]),
                               reads=[bankB[bk], cB], writes=[yB[kind]])
                    for tap, sh in ((1, 1), (0, 2)):
                        DVE.op(lambda h: h.scalar_tensor_tensor(out=yy[kind][:, 2:NT], in0=ub[kind][:, 2 - sh:NT - sh], scalar=cw[:, tl, tap:tap + 1],
                                                                in1=yy[kind][:, 2:NT], op0=ALU.mult, op1=ALU.add),
                               reads=[ubB[kind], yB[kind], cB], writes=[yB[kind]])
                ACT.op(lambda h: h.activation(out=sg[:, 2:NT], in_=yy[0][:, 2:NT], func=AF.Silu), reads=[yB[0]], writes=[sgB])
                DVE.op(lambda h: h.tensor_tensor(out=gb[:, gs, gi, :], in0=sg[:, 2:NT], in1=yy[1][:, 2:NT], op=ALU.mult),
                       reads=[sgB, yB[1]], writes=[gB[gs]])
                if gi == 3 and grp >= 1:
                    down(grp - 1)
        if NDP == 22: down(NGRP - 1)

        if not last:
            DVE.op(lambda h: h.tensor_scalar(out=xT[:, :, 0:4], in0=xT[:, :, 0:4], scalar1=hv[:, 0:1], scalar2=None, op0=ALU.mult),
                   reads=xB + [cB], writes=xB)
        xo_v = d["xout"].rearrange("(c p) t -> p c t", p=128)
        for q in range(4):
            SP.dma(xo_v[:, 4 * q:4 * q + 4, :], xT[:, 4 * q:4 * q + 4, :], st_o, reads=xB[4 * q:4 * q + 4], writes=[outB])
        SP.wait([Tok(st_o.sem, st_o.count)])
        toks = [Tok(e.sem, e.count) for e in (PE, ACT, DVE, POOL) if e.count > 0]
        toks += [Tok(s.sem, s.count) for s in [ld_x, ld_c, st_o] + ld_wb + ld_wd if s.count > 0]
        for e in (PE, ACT, DVE, POOL, SP, GQ):
            e.wait(toks)


TQ = 256
NTILE = 4096 // TQ
ND = 19


def attn_mask_np():
    j = np.arange(128)[:, None]
    i = np.arange(128)[None, :]
    M = np.zeros((128, ND * 128), np.float32)
    for d in range(-1, 18):
        dist = 128 * d + i - j
        m = np.zeros((128, 128), np.float32)
        for win, dil in ((128, 1), (512, 4), (2048, 16)):
            m += ((dist >= 0) & (dist <= win) & (dist % dil == 0)).astype(np.float32)
        M[:, (d + 1) * 128:(d + 2) * 128] = m
    return M.astype(ml_dtypes.bfloat16)


def phase_attn(nc, K, d, tag="at", ntile=NTILE):
    PE, ACT, DVE, POOL, SP, GQ = K.PE, K.ACT, K.DVE, K.POOL, K.SP, K.GQ
    bankB = K.bankB
    with ExitStack() as es:
        sb = lambda name, shape, dt: es.enter_context(nc.sbuf_tensor(f"{tag}_{name}", shape, dt))
        xt = sb("xt", [128, 16, TQ], F32)
        ht = sb("ht", [128, 2, 16, TQ], BF16)
        wq = sb("wq", [128, 3, 16, 512], BF16)
        kT = sb("kT", [128, 4, 4096], BF16)
        va = sb("va", [128, 32, 4, 130], BF16)
        qT = sb("qT", [128, 4, TQ], BF16)
        mk = sb("mk", [128, ND * 128], BF16)
        pT = sb("pT", [128, 3, TQ], BF16)
        sqs = sb("sqs", [128, TQ], BF16)
        rt = sb("rt", [128, TQ], F32)
        rstd = sb("rstd", [128, TQ], F32)
        rq = sb("rq", [128, TQ], F32)
        on = sb("on", [128, 2, 128], BF16)
        rden = sb("rden", [128, 2], F32)
        oT = sb("oT", [128, 2, 4, TQ], BF16)
        gn = sb("gn", [128, 16], F32)
        qkg = sb("qkg", [128, 2], F32)
        ones = sb("ones", [128, 128], BF16)
        ident = sb("ident", [128, 128], BF16)
        tp = es.enter_context(nc.psum_tensor(f"{tag}_tp", [128, 1024], BF16))
        bank = K.banks

        xtB, htB, wB, kB, vB, qB, cB = Buf(), [Buf(), Buf()], Buf(), Buf(), Buf(), Buf(), Buf()
        pB, sqsB, rtB, rstdB, rqB, onB, rdB, oTB = [Buf(), Buf(), Buf()], Buf(), Buf(), Buf(), Buf(), [Buf(), Buf()], [Buf(), Buf()], [Buf(), Buf()]
        onesB, outB, tpB = Buf(), Buf(), K.bankB[7]
        ld_x, ld_c, ld_w = K.dsem(), K.dsem(), K.dsem()
        st_o = [K.dsem(), K.dsem()]
        PR, NB, SR, OB = [0, 1], 2, [3, 4], [5, 6]
        ring = {"p": 0, "s": 0, "pt": 0}

        toks = []
        for (t, src) in ((gn[:], d["gn"]), (qkg[:, 0:1], d["qg"]), (qkg[:, 1:2], d["kg"]), (mk[:], d["mask"]), (ident[:], d["ident"])):
            SP.dma(t, src, ld_c, writes=[cB])
        for i in range(3):
            GQ.dma(wq[:, i], d["wqkv"][:, i * 512:(i + 1) * 512].rearrange("(c p) n -> p c n", p=128), ld_w, writes=[wB])
        POOL.op(lambda h: h.memset(ones[:], 1.0), writes=[onesB])
        POOL.op(lambda h: h.memset(va[:, :, :, 128:130], 1.0), writes=[vB])
        DVE.op(lambda h: h.tensor_scalar(out=qkg[:, 0:1], in0=qkg[:, 0:1], scalar1=128.0 ** -0.5, scalar2=None, op0=ALU.mult), reads=[cB], writes=[cB])

        xv = d["xfull"].rearrange("(c p) t -> p c t", p=128)
        ov = d["oT"].rearrange("(h p) t -> p h t", p=128)

        def mm_group(bk, n, lhs_fn, rhs_fn, nch, reads):
            fns = [(lambda h, c=c: h.matmul(bank[bk][:, 0:n], lhs_fn(c), rhs_fn(c), start=(c == 0), stop=(c == nch - 1))) for c in range(nch)]
            return PE.group(fns, reads=reads, writes=[bankB[bk]])

        def nb(kind, lst):
            b = lst[ring[kind] % len(lst)]
            ring[kind] += 1
            return b

        def rsqrt_from_bank(bk, n, scale, dst, dstB):
            DVE.op(lambda h: h.tensor_scalar(out=rt[:, 0:n], in0=bank[bk][:, 0:n], scalar1=scale, scalar2=1e-6, op0=ALU.mult, op1=ALU.add),
                   reads=[bankB[bk]], writes=[rtB])
            ACT.op(lambda h: h.activation(out=rt[:, 0:n], in_=rt[:, 0:n], func=AF.Ln), reads=[rtB], writes=[rtB])
            ACT.op(lambda h: h.activation(out=dst[:, 0:n], in_=rt[:, 0:n], func=AF.Exp, scale=-0.5), reads=[rtB], writes=[dstB])

        def load_norm(t):
            T0 = t * TQ
            hs = t % 2
            SP.dma(xt[:], xv[:, :, T0:T0 + TQ], ld_x, writes=[xtB])
            ACT.op(lambda h: h.activation(out=ht[:, hs], in_=xt[:], func=AF.Square), reads=[xtB], writes=[htB[hs]])
            mm_group(NB, TQ, lambda c: ones[:], lambda c: ht[:, hs, c, :], 16, [htB[hs], onesB])
            rsqrt_from_bank(NB, TQ, 1.0 / 2048.0, rstd, rstdB)
            for c in range(16):
                DVE.op(lambda h: h.scalar_tensor_tensor(out=ht[:, hs, c, :], in0=xt[:, c, :], scalar=gn[:, c:c + 1], in1=rstd[:], op0=ALU.mult, op1=ALU.mult),
                       reads=[xtB, rstdB, cB], writes=[htB[hs]])

        def qk_norm(f, bk, t):
            T0 = t * TQ
            isk, hh = f // 4, f % 4
            ACT.op(lambda h: h.activation(out=sqs[:], in_=bank[bk][:, 0:TQ], func=AF.Square), reads=[bankB[bk]], writes=[sqsB])
            mm_group(NB, TQ, lambda c: ones[:], lambda c: sqs[:], 1, [sqsB, onesB])
            rsqrt_from_bank(NB, TQ, 1.0 / 128.0, rq, rqB)
            dst = kT[:, hh, T0:T0 + TQ] if isk else qT[:, hh, :]
            DVE.op(lambda h: h.scalar_tensor_tensor(out=dst, in0=bank[bk][:, 0:TQ], scalar=qkg[:, isk:isk + 1], in1=rq[:], op0=ALU.mult, op1=ALU.mult),
                   reads=[bankB[bk], rqB, cB], writes=[kB if isk else qB])

        def proj(t):
            hs = t % 2
            prev = None
            for f in range(8):
                isk, hh = f // 4, f % 4
                bk = nb("p", PR)
                mm_group(bk, TQ, lambda c: wq[:, isk, c, hh * 128:(hh + 1) * 128], lambda c: ht[:, hs, c, :], 16, [wB, htB[hs]])
                if prev is not None:
                    qk_norm(prev[0], prev[1], t)
                prev = (f, bk)
            vb = []
            for a in range(TQ // 128):
                blk = t * (TQ // 128) + a
                bk = nb("p", PR)
                mm_group(bk, 512, lambda c: ht[:, hs, c, a * 128:(a + 1) * 128], lambda c: wq[:, 2, c, :], 16, [wB, htB[hs]])
                if prev is not None:
                    qk_norm(prev[0], prev[1], t)
                    prev = None
                ACT.op(lambda h: h.activation(out=va[:, blk, :, 0:128], in_=bank[bk][:, 0:512].rearrange("p (h e) -> p h e", h=4), func=AF.Copy),
                       reads=[bankB[bk]], writes=[vB])

        def attention(t):
            T0 = t * TQ
            os_ = t % 2
            kbs = list(range(max(0, 2 * t - 16), 2 * t + 2))
            steps = [(hh, kb) for hh in range(4) for kb in kbs]
            info = {}

            def front(i):
                hh, kb = steps[i]
                D = 2 * t - kb
                sbk = nb("s", SR)
                ps = ring["pt"] % 3
                ring["pt"] += 1
                info[i] = ps
                PE.group([lambda h: h.matmul(bank[sbk][:, 0:TQ], kT[:, hh, kb * 128:(kb + 1) * 128], qT[:, hh, :], start=True, stop=True)],
                         reads=[kB, qB], writes=[bankB[sbk]])
                ACT.op(lambda h: h.activation(out=pT[:, ps, :], in_=bank[sbk][:, 0:TQ], func=AF.Exp), reads=[bankB[sbk]], writes=[pB[ps]])
                DVE.op(lambda h: h.tensor_tensor(out=pT[:, ps, :], in0=pT[:, ps, :], in1=mk[:, (D + 1) * 128:(D + 3) * 128], op=ALU.mult),
                       reads=[pB[ps], cB], writes=[pB[ps]])

            def back(i):
                hh, kb = steps[i]
                D = 2 * t - kb
                ps = info.pop(i)
                for a in range(2):
                    dd = D + a
                    if dd < 0 or dd > 16:
                        continue
                    first = (kb == kbs[0]) or (dd == 16)
                    lastk = (dd == 0)
                    PE.group([lambda h: h.matmul(bank[OB[a]][:, 0:129], pT[:, ps, a * 128:(a + 1) * 128], va[:, kb, hh, 0:129], start=first, stop=lastk)],
                             reads=[pB[ps], vB], writes=[bankB[OB[a]]])
                if kb == kbs[-1]:
                    for a in range(2):
                        DVE.op(lambda h: h.reciprocal(out=rden[:, a:a + 1], in_=bank[OB[a]][:, 128:129]), reads=[bankB[OB[a]]], writes=[rdB[a]])
                        ACT.op(lambda h: h.activation(out=on[:, a, :], in_=bank[OB[a]][:, 0:128], func=AF.Copy, scale=rden[:, a:a + 1]),
                               reads=[bankB[OB[a]], rdB[a]], writes=[onB[a]])
                        PE.group([lambda h: h.transpose(tp[:, a * 128:(a + 1) * 128], on[:, a, :], ident[:])], reads=[onB[a], cB], writes=[tpB])
                        DVE.op(lambda h: h.tensor_copy(out=oT[:, os_, hh, a * 128:(a + 1) * 128], in_=tp[:, a * 128:(a + 1) * 128]), reads=[tpB], writes=[oTB[os_]])

            LA = 2
            n = len(steps)
            for i in range(min(LA, n)):
                front(i)
            for i in range(n):
                if i + LA < n:
                    front(i + LA)
                back(i)
            SP.dma(ov[:, :, T0:T0 + TQ], oT[:, os_], st_o[os_], reads=[oTB[os_]], writes=[outB])

        load_norm(0)
        for t in range(ntile):
            proj(t)
            if t + 1 < ntile:
                load_norm(t + 1)
            attention(t)
        SP.wait([Tok(s_.sem, s_.count) for s_ in st_o if s_.count > 0])
        toks = [Tok(e.sem, e.count) for e in (PE, ACT, DVE, POOL) if e.count > 0]
        toks += [Tok(s.sem, s.count) for s in [ld_x, ld_c, ld_w] + st_o if s.count > 0]
        for e in (PE, ACT, DVE, POOL, SP, GQ):
            e.wait(toks)


TQ = 256
NTILE = 4096 // TQ


def lstm_consts_np():
    s = np.arange(128)[:, None]
    l = np.arange(128)[None, :]
    tri = ((s // 64 == l // 64) & (s <= l)).astype(np.float32)
    sel = np.zeros((128, 2), np.float32)
    sel[:64, 0] = 1.0
    sel[64:, 1] = 1.0
    return tri, sel


def phase_lstm(nc, K, d, tag="ls", ntile=NTILE):
    PE, ACT, DVE, POOL, SP, GQ = K.PE, K.ACT, K.DVE, K.POOL, K.SP, K.GQ
    bankB = K.bankB
    with ExitStack() as es:
        sb = lambda name, shape, dt: es.enter_context(nc.sbuf_tensor(f"{tag}_{name}", shape, dt))
        xt = sb("xt", [128, 16, TQ], F32)
        ht = sb("ht", [128, 2, 16, TQ], BF16)
        win = sb("win", [128, 3, 16, 512], BF16)
        wg = sb("wg", [128, 16, 2], BF16)
        zq = sb("zq", [128, 4, TQ + 3], F32)
        acc = sb("acc", [128, 4, TQ], F32)
        qk = sb("qk", [128, 2, 4, TQ], BF16)
        ktm = sb("ktm", [128, 2, 2, 256], BF16)
        vp = sb("vp", [128, 2, 512], BF16)
        osg = sb("osg", [128, 2, 512], BF16)
        hb = sb("hb", [128, 512], F32)
        hjunk = sb("hjunk", [128, 512], BF16)
        htmp = sb("htmp", [128, 512], F32)
        hstm = sb("hstm", [128, 512], BF16)
        hsT = sb("hsT", [128, 2, 4, TQ], BF16)
        Sf = sb("Sf", [128, 2, 512], F32)
        Sb = sb("Sb", [128, 2, 512], BF16)
        nf = sb("nf", [128, 2], F32)
        nbf = sb("nbf", [128, 2], BF16)
        qts = sb("qts", [128, 2, 2, 64], BF16)
        WT = sb("WT", [128, 2, 128], BF16)
        gsb = sb("gsb", [128, 2, 2], F32)
        t1 = sb("t1", [128, 2, 1], F32)
        lf = sb("lf", [128, 2, 1], F32)
        lfm = sb("lfm", [128, 2, 2], F32)
        el = sb("el", [128, 2, 1], F32)
        amb = sb("amb", [128, 2, 1], F32)
        af = sb("af", [128, 2, 1], F32)
        abf = sb("abf", [128, 2, 1], BF16)
        dvec = sb("dvec", [128, 3, 2], F32)
        sm = sb("sm", [128, 8], F32)
        rt = sb("rt", [128, TQ], F32)
        rstd = sb("rstd", [128, TQ], F32)
        gn = sb("gn", [128, 16], F32)
        cw = sb("cw", [128, 4, 4], F32)
        cb = sb("cb", [128, 4], F32)
        hg = sb("hg", [128, 512], F32)
        gbias = sb("gbias", [128, 2], F32)
        tri = sb("tri", [128, 128], F32)
        onesf = sb("onesf", [128, 128], F32)
        sel = sb("sel", [128, 2], F32)
        ones = sb("ones", [128, 128], BF16)
        ident = sb("ident", [128, 128], BF16)
        tp = es.enter_context(nc.psum_tensor(f"{tag}_tp", [128, 1024], BF16))
        bank = K.banks

        B = lambda: Buf()
        xtB, htB, wB, cB, onesB = B(), [B(), B()], B(), B(), B()
        zqB, accB, qkB, ktmB = [B() for _ in range(4)], [B() for _ in range(4)], [B(), B()], [B(), B()]
        vpB, osgB, hbB, hjB, htmpB, hstmB, hsTB = [B(), B()], [B(), B()], B(), B(), B(), B(), [B(), B()]
        SfB, SbB, nfB, nbfB, qtsB, WTB = B(), B(), B(), B(), [B(), B()], [B(), B()]
        gB, smB, rtB, rstdB, outB = [B(), B()], B(), B(), B(), B()
        tpB = K.bankB[7]
        SMB = 2
        gateR, cumR, denR, dnR, sTR = B(), B(), B(), B(), B()
        PR, NUMB, DC = [0, 1], [3, 4], [5, 6]
        ld_x, ld_c, ld_w = K.dsem(), K.dsem(), K.dsem()
        st_o = [K.dsem(), K.dsem()]
        ring = {"p": 0}

        for (t, src) in ((gn[:], d["gn"]), (cw[:], d["cw"]), (cb[:], d["cb"]), (hg[:], d["hg"]), (gbias[:], d["gbias"]),
                         (tri[:], d["tri"]), (sel[:], d["sel"]), (ident[:], d["ident"])):
            SP.dma(t, src, ld_c, writes=[cB])
        for i in range(3):
            GQ.dma(win[:, i], d["win"][:, i * 512:(i + 1) * 512].rearrange("(c p) n -> p c n", p=128), ld_w, writes=[wB])
        GQ.dma(wg[:], d["wg"].rearrange("(c p) n -> p c n", p=128), ld_w, writes=[wB])
        POOL.op(lambda h: h.memset(ones[:], 1.0), writes=[onesB])
        POOL.op(lambda h: h.memset(onesf[:], 1.0), writes=[onesB])
        POOL.op(lambda h: h.memset(zq[:], 0.0), writes=zqB)
        POOL.op(lambda h: h.memset(Sf[:], 0.0), writes=[SfB])
        POOL.op(lambda h: h.memset(nf[:], 0.0), writes=[nfB])

        xv = d["xfull"].rearrange("(c p) t -> p c t", p=128)
        ov = d["hsT"].rearrange("(h p) t -> p h t", p=128)

        def mm_group(out_ap, lhs_fn, rhs_fn, nch, reads, writes):
            fns = [(lambda h, c=c: h.matmul(out_ap, lhs_fn(c), rhs_fn(c), start=(c == 0), stop=(c == nch - 1))) for c in range(nch)]
            return PE.group(fns, reads=reads, writes=writes)

        def nb(kind, lst):
            b = lst[ring[kind] % len(lst)]
            ring[kind] += 1
            return b

        def load_norm(t):
            T0 = t * TQ
            hs = t % 2
            SP.dma(xt[:], xv[:, :, T0:T0 + TQ], ld_x, writes=[xtB])
            ACT.op(lambda h: h.activation(out=ht[:, hs], in_=xt[:], func=AF.Square), reads=[xtB], writes=[htB[hs]])
            mm_group(bank[PR[0]][:, 0:TQ], lambda c: ones[:], lambda c: ht[:, hs, c, :], 16, [htB[hs], onesB], [bankB[PR[0]]])
            DVE.op(lambda h: h.tensor_scalar(out=rt[:], in0=bank[PR[0]][:, 0:TQ], scalar1=1.0 / 2048.0, scalar2=1e-6, op0=ALU.mult, op1=ALU.add),
                   reads=[bankB[PR[0]]], writes=[rtB])
            ACT.op(lambda h: h.activation(out=rt[:], in_=rt[:], func=AF.Ln), reads=[rtB], writes=[rtB])
            ACT.op(lambda h: h.activation(out=rstd[:], in_=rt[:], func=AF.Exp, scale=-0.5), reads=[rtB], writes=[rstdB])
            for c in range(16):
                DVE.op(lambda h: h.scalar_tensor_tensor(out=ht[:, hs, c, :], in0=xt[:, c, :], scalar=gn[:, c:c + 1], in1=rstd[:], op0=ALU.mult, op1=ALU.mult),
                       reads=[xtB, rstdB, cB], writes=[htB[hs]])

        def proj_qk(t):
            hs = t % 2
            qs = t % 2
            for f in range(4):
                bk = nb("p", PR)
                mm_group(bank[bk][:, 0:TQ], lambda c: win[:, 0, c, f * 128:(f + 1) * 128], lambda c: ht[:, hs, c, :], 16, [wB, htB[hs]], [bankB[bk]])
                ACT.op(lambda h: h.activation(out=zq[:, f, 3:3 + TQ], in_=bank[bk][:, 0:TQ], func=AF.Copy), reads=[bankB[bk]], writes=[zqB[f]])
                DVE.op(lambda h: h.tensor_scalar(out=acc[:, f, :], in0=zq[:, f, 3:3 + TQ], scalar1=cw[:, f, 3:4], scalar2=cb[:, f:f + 1], op0=ALU.mult, op1=ALU.add),
                       reads=[zqB[f], cB], writes=[accB[f]])
                for j in (1, 2, 3):
                    DVE.op(lambda h: h.scalar_tensor_tensor(out=acc[:, f, :], in0=zq[:, f, 3 - j:3 - j + TQ], scalar=cw[:, f, 3 - j:4 - j], in1=acc[:, f, :],
                                                            op0=ALU.mult, op1=ALU.add), reads=[zqB[f], accB[f], cB], writes=[accB[f]])
                DVE.op(lambda h: h.tensor_copy(out=zq[:, f, 0:3], in_=zq[:, f, TQ:TQ + 3]), reads=[zqB[f]], writes=[zqB[f]])
                ACT.op(lambda h: h.activation(out=acc[:, f, :], in_=acc[:, f, :], func=AF.Silu), reads=[accB[f]], writes=[accB[f]])
                DVE.op(lambda h: h.tensor_scalar(out=qk[:, qs, f, :], in0=acc[:, f, :], scalar1=(1.0 if f < 2 else 0.0625), scalar2=None, op0=ALU.mult),
                       reads=[accB[f]], writes=[qkB[qs]])
            for blk in range(2):
                for dkc in range(2):
                    col = (blk * 2 + dkc) * 128
                    PE.group([lambda h: h.transpose(tp[:, col:col + 128], qk[:, qs, 2 + dkc, blk * 128:(blk + 1) * 128], ident[:])],
                             reads=[qkB[qs], cB], writes=[tpB])
            DVE.op(lambda h: h.tensor_copy(out=ktm[:, qs].rearrange("p b k -> p (b k)"), in_=tp[:, 0:512]), reads=[tpB], writes=[ktmB[qs]])

        def block(t, blk, state):
            hs = t % 2
            qs = t % 2
            bs = blk
            tok = slice(blk * 128, (blk + 1) * 128)
            mm_group(bank[SMB][:, 0:2], lambda c: ht[:, hs, c, tok], lambda c: wg[:, c, :], 16, [wB, htB[hs]], [gateR])
            DVE.op(lambda h: h.tensor_tensor(out=gsb[:, bs, :], in0=bank[SMB][:, 0:2], in1=gbias[:], op=ALU.add), reads=[gateR, cB], writes=[gB[bs]])
            ACT.op(lambda h: h.activation(out=t1[:, bs, :], in_=gsb[:, bs, 1:2], func=AF.Exp, scale=-1.0), reads=[gB[bs]], writes=[gB[bs]])
            DVE.op(lambda h: h.tensor_scalar(out=t1[:, bs, :], in0=t1[:, bs, :], scalar1=1.0, scalar2=None, op0=ALU.add), reads=[gB[bs]], writes=[gB[bs]])
            ACT.op(lambda h: h.activation(out=t1[:, bs, :], in_=t1[:, bs, :], func=AF.Ln), reads=[gB[bs]], writes=[gB[bs]])
            DVE.op(lambda h: h.tensor_scalar(out=lf[:, bs, :], in0=t1[:, bs, :], scalar1=-1.0, scalar2=None, op0=ALU.mult), reads=[gB[bs]], writes=[gB[bs]])
            DVE.op(lambda h: h.tensor_scalar(out=lfm[:, bs, :], in0=sel[:], scalar1=lf[:, bs, 0:1], scalar2=None, op0=ALU.mult), reads=[gB[bs], cB], writes=[gB[bs]])
            PE.group([lambda h: h.matmul(bank[SMB][:, 4:5], tri[:], lf[:, bs, :], start=True, stop=True),
                      lambda h: h.matmul(bank[SMB][:, 6:8], onesf[:], lfm[:, bs, :], start=True, stop=True)],
                     reads=[gB[bs], cB, onesB], writes=[cumR])
            ACT.op(lambda h: h.activation(out=el[:, bs, :], in_=bank[SMB][:, 4:5], func=AF.Exp), reads=[cumR], writes=[gB[bs]])
            DVE.op(lambda h: h.tensor_tensor(out=amb[:, bs, :], in0=gsb[:, bs, 0:1], in1=bank[SMB][:, 4:5], op=ALU.subtract), reads=[cumR, gB[bs]], writes=[gB[bs]])
            ACT.op(lambda h: h.activation(out=af[:, bs, :], in_=amb[:, bs, :], func=AF.Exp), reads=[gB[bs]], writes=[gB[bs]])
            DVE.op(lambda h: h.tensor_copy(out=abf[:, bs, :], in_=af[:, bs, :]), reads=[gB[bs]], writes=[gB[bs]])
            ACT.op(lambda h: h.activation(out=dvec[:, 1 + bs, :], in_=bank[SMB][:, 6:8], func=AF.Exp), reads=[cumR], writes=[gB[bs]])
            bk = nb("p", PR)
            mm_group(bank[bk][:, 0:512], lambda c: ht[:, hs, c, tok], lambda c: win[:, 1, c, :], 16, [wB, htB[hs]], [bankB[bk]])
            ACT.op(lambda h: h.activation(out=vp[:, bs, :], in_=bank[bk][:, 0:512], func=AF.Copy, scale=af[:, bs, :]), reads=[bankB[bk], gB[bs]], writes=[vpB[bs]])
            bk = nb("p", PR)
            mm_group(bank[bk][:, 0:512], lambda c: ht[:, hs, c, tok], lambda c: win[:, 2, c, :], 16, [wB, htB[hs]], [bankB[bk]])
            ACT.op(lambda h: h.activation(out=osg[:, bs, :], in_=bank[bk][:, 0:512], func=AF.Sigmoid), reads=[bankB[bk]], writes=[osgB[bs]])
            mm_group(bank[SMB][:, 128:256], lambda c: qk[:, qs, 2 + c, tok], lambda c: qk[:, qs, c, tok], 2, [qkB[qs]], [sTR])
            DVE.op(lambda h: h.tensor_tensor(out=WT[:, bs, :], in0=bank[SMB][:, 128:256], in1=tri[:], op=ALU.mult), reads=[sTR, cB], writes=[WTB[bs]])
            nbk = NUMB[(t * 2 + blk) % 2]
            for half in range(2):
                c_idx = (t * 2 + blk) * 2 + half
                r0 = half * 64
                rows = slice(r0, r0 + 64)
                ctok = slice(blk * 128 + r0, blk * 128 + r0 + 64)
                dprev = dvec[:, 1 + bs, 0:1] if (half == 1 or c_idx == 0) else dvec[:, 0, 1:2]
                if c_idx > 0:
                    ACT.op(lambda h: h.activation(out=qts[:, half], in_=qk[:, qs, 0:2, ctok], func=AF.Copy, scale=dprev),
                           reads=[qkB[qs], gB[bs], gB[1 - bs]], writes=[qtsB[half]])
                fns = []
                if c_idx > 0:
                    for dkc in range(2):
                        fns.append(lambda h, dkc=dkc: h.matmul(bank[nbk][rows, 0:512], qts[:, half, dkc, :], Sb[:, dkc, :], start=(dkc == 0), stop=False))
                fns.append(lambda h: h.matmul(bank[nbk][rows, 0:512], WT[rows, bs, r0:r0 + 64], vp[rows, bs, :], start=(c_idx == 0), stop=True))
                PE.group(fns, reads=[qtsB[half], SbB, WTB[bs], vpB[bs]], writes=[bankB[nbk]])
                fns = []
                if c_idx > 0:
                    for dkc in range(2):
                        fns.append(lambda h, dkc=dkc: h.matmul(bank[SMB][rows, 8:9], qts[:, half, dkc, :], nbf[:, dkc:dkc + 1], start=(dkc == 0), stop=False))
                fns.append(lambda h: h.matmul(bank[SMB][rows, 8:9], WT[rows, bs, r0:r0 + 64], abf[rows, bs, :], start=(c_idx == 0), stop=True))
                PE.group(fns, reads=[qtsB[half], nbfB, WTB[bs], gB[bs]], writes=[denR])
                for dkc in range(2):
                    PE.group([lambda h: h.matmul(bank[DC[dkc]][:, 0:512], ktm[rows, qs, blk, dkc * 128:(dkc + 1) * 128], vp[rows, bs, :], start=True, stop=True)],
                             reads=[ktmB[qs], vpB[bs]], writes=[bankB[DC[dkc]]])
                PE.group([(lambda h, dkc=dkc: h.matmul(bank[SMB][:, 12 + dkc:13 + dkc], ktm[rows, qs, blk, dkc * 128:(dkc + 1) * 128], abf[rows, bs, :], start=True, stop=True))
                          for dkc in range(2)], reads=[ktmB[qs], gB[bs]], writes=[dnR])
                dc = dprev
                for dkc in range(2):
                    DVE.op(lambda h: h.scalar_tensor_tensor(out=Sf[:, dkc, :], in0=Sf[:, dkc, :], scalar=dc, in1=bank[DC[dkc]][:, 0:512], op0=ALU.mult, op1=ALU.add),
                           reads=[bankB[DC[dkc]], gB[bs], gB[1 - bs], SfB], writes=[SfB])
                DVE.op(lambda h: h.scalar_tensor_tensor(out=nf[:], in0=nf[:], scalar=dc, in1=bank[SMB][:, 12:14], op0=ALU.mult, op1=ALU.add),
                       reads=[dnR, gB[bs], gB[1 - bs], nfB], writes=[nfB])
                ACT.op(lambda h: h.activation(out=Sb[:], in_=Sf[:], func=AF.Copy), reads=[SfB], writes=[SbB])
                DVE.op(lambda h: h.tensor_copy(out=nbf[:], in_=nf[:]), reads=[nfB], writes=[nbfB])
            DVE.op(lambda h: h.tensor_copy(out=dvec[:, 0, :], in_=dvec[:, 1 + bs, :]), reads=[gB[bs]], writes=[gB[bs], gB[1 - bs]])
            DVE.op(lambda h: h.tensor_scalar(out=sm[:, 0:1], in0=bank[SMB][:, 8:9], scalar1=el[:, bs, :], scalar2=None, op0=ALU.mult),
                   reads=[denR, gB[bs]], writes=[smB])
            DVE.op(lambda h: h.tensor_scalar(out=sm[:, 6:7], in0=sm[:, 0:1], scalar1=-1.0, scalar2=1.0, op0=ALU.mult, op1=ALU.max), reads=[smB], writes=[smB])
            DVE.op(lambda h: h.tensor_tensor(out=sm[:, 0:1], in0=sm[:, 0:1], in1=sm[:, 6:7], op=ALU.max), reads=[smB], writes=[smB])
            DVE.op(lambda h: h.reciprocal(out=sm[:, 1:2], in_=sm[:, 0:1]), reads=[smB], writes=[smB])
            DVE.op(lambda h: h.tensor_tensor(out=sm[:, 2:3], in0=sm[:, 1:2], in1=el[:, bs, :], op=ALU.mult), reads=[smB, gB[bs]], writes=[smB])
            ACT.op(lambda h: h.activation(out=hb[:], in_=bank[nbk][:, 0:512], func=AF.Copy, scale=sm[:, 2:3]), reads=[bankB[nbk], smB], writes=[hbB])
            ACT.op(lambda h: h.activation(out=hjunk[:], in_=hb[:], func=AF.Square, accum_out=sm[:, 3:4]), reads=[hbB], writes=[hjB, smB])
            DVE.op(lambda h: h.tensor_scalar(out=sm[:, 4:5], in0=sm[:, 3:4], scalar1=1.0 / 512.0, scalar2=1e-6, op0=ALU.mult, op1=ALU.add), reads=[smB], writes=[smB])
            ACT.op(lambda h: h.activation(out=sm[:, 4:5], in_=sm[:, 4:5], func=AF.Ln), reads=[smB], writes=[smB])
            ACT.op(lambda h: h.activation(out=sm[:, 5:6], in_=sm[:, 4:5], func=AF.Exp, scale=-0.5), reads=[smB], writes=[smB])
            DVE.op(lambda h: h.scalar_tensor_tensor(out=htmp[:], in0=hb[:], scalar=sm[:, 5:6], in1=hg[:], op0=ALU.mult, op1=ALU.mult),
                   reads=[hbB, smB, cB], writes=[htmpB])
            DVE.op(lambda h: h.tensor_tensor(out=hstm[:], in0=htmp[:], in1=osg[:, bs, :], op=ALU.mult), reads=[htmpB, osgB[bs]], writes=[hstmB])
            os_ = t % 2
            for j in range(4):
                PE.group([lambda h: h.transpose(tp[:, 512 + j * 128:512 + (j + 1) * 128], hstm[:, j * 128:(j + 1) * 128], ident[:])],
                         reads=[hstmB, cB], writes=[tpB])
            DVE.op(lambda h: h.tensor_copy(out=hsT[:, os_, :, tok], in_=tp[:, 512:1024].rearrange("p (j t) -> p j t", j=4)), reads=[tpB], writes=[hsTB[os_]])

        load_norm(0)
        for t in range(ntile):
            proj_qk(t)
            for blk in range(2):
                block(t, blk, None)
            if t + 1 < ntile:
                load_norm(t + 1)
            SP.dma(ov[:, :, t * TQ:(t + 1) * TQ], hsT[:, t % 2], st_o[t % 2], reads=[hsTB[t % 2]], writes=[outB])
        SP.wait([Tok(s_.sem, s_.count) for s_ in st_o if s_.count > 0])
        toks = [Tok(e.sem, e.count) for e in (PE, ACT, DVE, POOL) if e.count > 0]
        toks += [Tok(s.sem, s.count) for s in [ld_x, ld_c, ld_w] + st_o if s.count > 0]
        for e in (PE, ACT, DVE, POOL, SP, GQ):
            e.wait(toks)


def _c(a):
    return np.ascontiguousarray(a)


def _build_attn():
    nc = bass.Bass("TRN2", target_bir_lowering=False)
    d = {}

    def inp(name, shape, dt=F32):
        d[name] = nc.dram_tensor(name, shape, dt, kind="ExternalInput").ap()
    inp("xfull", [2048, 4096]); inp("wqkv", [2048, 1536]); inp("gn", [128, 16]); inp("qg", [128, 1]); inp("kg", [128, 1])
    inp("mask", [128, ND * 128], BF16); inp("ident", [128, 128], BF16)
    d["oT"] = nc.dram_tensor("oT", [512, 4096], BF16, kind="ExternalOutput").ap()
    with ExitStack() as es:
        K = Kit(nc, es)
        phase_attn(nc, K, d)
    return nc


def _build_lstm():
    nc = bass.Bass("TRN2", target_bir_lowering=False)
    d = {}

    def inp(name, shape, dt=F32):
        d[name] = nc.dram_tensor(name, shape, dt, kind="ExternalInput").ap()
    inp("xfull", [2048, 4096]); inp("win", [2048, 1536]); inp("wg", [2048, 2]); inp("gn", [128, 16])
    inp("cw", [128, 4, 4]); inp("cb", [128, 4]); inp("hg", [128, 512]); inp("gbias", [128, 2])
    inp("tri", [128, 128]); inp("sel", [128, 2]); inp("ident", [128, 128], BF16)
    d["hsT"] = nc.dram_tensor("hsT", [512, 4096], BF16, kind="ExternalOutput").ap()
    with ExitStack() as es:
        K = Kit(nc, es)
        phase_lstm(nc, K, d)
    return nc


def _build_of(last):
    nc = bass.Bass("TRN2", target_bir_lowering=False)
    d = {}

    def inp(name, shape, dt=F32):
        d[name] = nc.dram_tensor(name, shape, dt, kind="ExternalInput").ap()
    inp("xin", [2048, NT]); inp("m", [2048, NT], BF16); inp("wo", [2048, 2048]); inp("wup", [2048, 2 * FFN]); inp("wdn", [FFN, 2048])
    inp("cw", [128, 88, 3]); inp("cb", [128, 88]); inp("gn", [128, 16]); inp("hv", [128, 1])
    d["xout"] = nc.dram_tensor("xout", [2048, NT], F32, kind="ExternalOutput").ap()
    with ExitStack() as es:
        K = Kit(nc, es)
        phase_of(nc, K, d, True, last)
    return nc


def _tok_shard(fullT, c):
    s0 = c * 1024
    out = np.zeros((fullT.shape[0], NT), fullT.dtype)
    lo = max(0, s0 - 4)
    out[:, 4 - (s0 - lo):] = fullT[:, lo:s0 + 1024]
    return out


def _of_inputs(z, L, xT_full, mT_full, wo):
    wup = _c(z["ffn_w_up"][L]); wdn = _c(z["ffn_w_down"][L])
    cw = _c(z["ffn_conv_w"][L].reshape(3, 88, 128).transpose(2, 1, 0))
    cb = _c(z["ffn_conv_b"][L].reshape(88, 128).T)
    gn = _c(z["ffn_norm"][L].reshape(16, 128).T)
    maps = []
    for core in range(8):
        b, c = core // 4, core % 4
        maps.append({"xin": _tok_shard(xT_full[b], c), "m": _tok_shard(mT_full[b], c), "wo": wo, "wup": wup, "wdn": wdn,
                     "cw": cw, "cb": cb, "gn": gn, "hv": np.full((128, 1), 0.0 if c == 0 else 1.0, np.float32)})
    return maps


def _lstm_inputs(z, xT_b, hd):
    w = z["lstm_w_in"][0]
    win = np.concatenate([w[:, hd * 256:(hd + 1) * 256], w[:, 1024 + hd * 256:1024 + (hd + 1) * 256],
                          w[:, 2048 + hd * 512:2048 + (hd + 1) * 512], w[:, 4096 + hd * 512:4096 + (hd + 1) * 512]], axis=1)
    wg = np.stack([w[:, 6144 + hd], w[:, 6148 + hd]], axis=1)
    cwf = z["lstm_conv_w"][0]; cbf = z["lstm_conv_b"][0]
    chans = np.concatenate([np.arange(hd * 256, (hd + 1) * 256), np.arange(1024 + hd * 256, 1024 + (hd + 1) * 256)])
    cw = cwf[:, chans].reshape(4, 4, 128).transpose(2, 1, 0)
    cb = cbf[chans].reshape(4, 128).T
    hg = np.broadcast_to(z["lstm_head_gain"][0][hd * 512:(hd + 1) * 512][None, :], (128, 512))
    gb = z["lstm_gate_bias"][0]
    gbias = np.broadcast_to(np.stack([gb[hd], gb[4 + hd]])[None, :], (128, 2))
    tri, sel = lstm_consts_np()
    return {"xfull": _c(xT_b), "win": _c(win), "wg": _c(wg), "gn": _c(z["lstm_norm"][0].reshape(16, 128).T), "cw": _c(cw), "cb": _c(cb),
            "hg": _c(hg), "gbias": _c(gbias), "tri": tri, "sel": sel, "ident": np.eye(128, dtype=np.float32).astype(ml_dtypes.bfloat16)}


def kernel(**inputs):
    z = {k: np.asarray(v, dtype=np.float32) for k, v in inputs.items()}
    x = z["x"]
    cores = list(range(8))
    xT = [_c(x[b].T) for b in range(2)]
    wqkv = z["attn_w_qkv"][0]
    gn = _c(z["attn_norm"][0].reshape(16, 128).T)
    qg = _c(z["attn_q_gain"][0].reshape(128, 1)); kg = _c(z["attn_k_gain"][0].reshape(128, 1))
    mask = attn_mask_np(); ident = np.eye(128, dtype=np.float32).astype(ml_dtypes.bfloat16)
    maps = []
    for core in cores:
        b, g = core // 4, core % 4
        w = np.concatenate([wqkv[:, i * 2048 + g * 512: i * 2048 + (g + 1) * 512] for i in range(3)], axis=1)
        maps.append({"xfull": xT[b], "wqkv": _c(w), "gn": gn, "qg": qg, "kg": kg, "mask": mask, "ident": ident})
    r1 = run_bass_kernel_spmd(_build_attn(), maps, core_ids=cores)
    oT = [np.concatenate([r1.results[b * 4 + g]["oT"] for g in range(4)], axis=0) for b in range(2)]
    r2 = run_bass_kernel_spmd(_build_of(False), _of_inputs(z, 0, xT, oT, _c(z["attn_w_o"][0])), core_ids=cores)
    x2T = [np.concatenate([r2.results[b * 4 + c]["xout"][:, 4:] for c in range(4)], axis=1) for b in range(2)]
    r3 = run_bass_kernel_spmd(_build_lstm(), [_lstm_inputs(z, x2T[core // 4], core % 4) for core in cores], core_ids=cores)
    hsT = [np.concatenate([r3.results[b * 4 + g]["hsT"] for g in range(4)], axis=0) for b in range(2)]
    r4 = run_bass_kernel_spmd(_build_of(True), _of_inputs(z, 1, x2T, hsT, _c(z["lstm_w_out"][0])), core_ids=cores)
    out = np.empty((2, 4096, 2048), np.float32)
    for core in cores:
        b, c = core // 4, core % 4
        out[b, c * 1024:(c + 1) * 1024, :] = r4.results[core]["xout"][:, 4:].T
    return out
```

```python
from contextlib import ExitStack
import os


import numpy as np
import ml_dtypes
import concourse.bass as bass
import concourse.mybir as mybir
from concourse.bass_utils import run_bass_kernel_spmd

F32 = mybir.dt.float32
BF16 = mybir.dt.bfloat16
AF = mybir.ActivationFunctionType
ALU = mybir.AluOpType
AX = mybir.AxisListType


class Tok:
    __slots__ = ("sem", "val")

    def __init__(self, sem, val):
        self.sem, self.val = sem, val


class Buf:
    def __init__(self, name=""):
        self.name = name
        self.w = {}
        self.r = {}


def _merge(d, tok):
    k = id(tok.sem)
    if k not in d or d[k].val < tok.val:
        d[k] = tok


class Eng:
    def __init__(self, h, sem, skip_self=False):
        self.h, self.sem, self.count, self.waited = h, sem, 0, {}
        self.skip_self = skip_self

    def wait(self, deps):
        for d in deps:
            if d is None:
                continue
            k = id(d.sem)
            if self.waited.get(k, 0) >= d.val:
                continue
            if getattr(self, "skip_self", False) and d.sem is self.sem:
                continue
            self.h.wait_ge(d.sem, d.val)
            self.waited[k] = d.val

    @staticmethod
    def _deps(reads, writes):
        deps = []
        for b in reads:
            deps.extend(b.w.values())
        for b in writes:
            deps.extend(b.w.values())
            deps.extend(b.r.values())
        return deps

    @staticmethod
    def _commit(tok, reads, writes):
        for b in reads:
            _merge(b.r, tok)
        for b in writes:
            _merge(b.w, tok)

    def op(self, fn, reads=(), writes=(), extra=()):
        self.wait(self._deps(reads, writes))
        self.wait(extra)
        inst = fn(self.h)
        self.count += 1
        inst.then_inc(self.sem, 1)
        tok = Tok(self.sem, self.count)
        self._commit(tok, reads, writes)
        return tok

    def group(self, fns, reads=(), writes=(), extra=()):
        self.wait(self._deps(reads, writes))
        self.wait(extra)
        inst = None
        for fn in fns:
            inst = fn(self.h)
        self.count += 1
        inst.then_inc(self.sem, 1)
        tok = Tok(self.sem, self.count)
        self._commit(tok, reads, writes)
        return tok


class DmaSem:
    def __init__(self, sem):
        self.sem, self.count = sem, 0


class DmaQ:
    def __init__(self, h):
        self.h, self.waited = h, {}

    wait = Eng.wait

    def dma(self, out, in_, dsem, reads=(), writes=(), extra=(), **kw):
        self.wait(Eng._deps(reads, writes))
        self.wait(extra)
        self.h.dma_start(out=out, in_=in_, **kw).then_inc(dsem.sem, 16)
        dsem.count += 16
        tok = Tok(dsem.sem, dsem.count)
        Eng._commit(tok, reads, writes)
        return tok


NDP = int(os.environ.get('NDP', 22))
SKIPDOWN = int(os.environ.get('SKIPDOWN', 0))
SKIPNORM = int(os.environ.get('SKIPNORM', 0))

NT = 1028
TILES = [(0, 4), (4, 516), (516, 1028)]
NPAIR = 44
NGRP = 11
FFN = 5632


class Kit:
    def __init__(self, nc, es):
        self.nc = nc
        sem = lambda n: es.enter_context(nc.semaphore(n))
        self.PE = Eng(nc.tensor, sem("s_pe"), skip_self=True)
        self.ACT = Eng(nc.scalar, sem("s_act"))
        self.DVE = Eng(nc.vector, sem("s_dve"))
        self.POOL = Eng(nc.gpsimd, sem("s_pool"))
        self.SP = DmaQ(nc.sync)
        self.GQ = DmaQ(nc.gpsimd)
        self.es = es
        self.banks = [es.enter_context(nc.psum_tensor(f"bank{i}", [128, 512], F32)) for i in range(7)]
        self.bankB = [Buf(f"bank{i}") for i in range(8)]
        self._nsem = 0

    def dsem(self):
        self._nsem += 1
        return DmaSem(self.es.enter_context(self.nc.semaphore(f"dsem{self._nsem}")))

    def barrier(self):
        toks = [Tok(e.sem, e.count) for e in (self.PE, self.ACT, self.DVE, self.POOL) if e.count > 0]
        toks += self._dtoks()
        for e in (self.PE, self.ACT, self.DVE, self.POOL, self.SP, self.GQ):
            e.wait(toks)

    def _dtoks(self):
        return [Tok(d.sem, d.count) for d in getattr(self, "_dsems", []) if d.count > 0]


def phase_of(nc, K, d, do_outproj, last, tag="of"):
    PE, ACT, DVE, POOL, SP, GQ = K.PE, K.ACT, K.DVE, K.POOL, K.SP, K.GQ
    with ExitStack() as es:
        sb = lambda name, shape, dt: es.enter_context(nc.sbuf_tensor(f"sb_{tag}_{name}", shape, dt))
        xT = sb("xT", [128, 16, NT], F32)
        hT = sb("hT", [128, 16, NT], BF16)
        gb = sb("gb", [128, 2, 4, 1026], BF16)
        wb = sb("wb", [128, 4, 16, 256], BF16)
        wd = sb("wd", [128, 2, 4, 2048], BF16)
        ub = [sb(f"ub{i}", [128, NT], F32) for i in range(2)]
        yy = [sb(f"yy{i}", [128, NT], F32) for i in range(2)]
        sg = sb("sg", [128, NT], F32)
        cw = sb("cw", [128, 88, 3], F32)
        cb = sb("cb", [128, 88], F32)
        gn = sb("gn", [128, 16], F32)
        hv = sb("hv", [128, 1], F32)
        ones = sb("ones", [128, 128], BF16)
        rt = sb("rt", [128, 512], F32)
        rstd = sb("rstd", [128, 512], F32)

        xB = [Buf(f"x{j}") for j in range(16)]
        hB, gB, wbB, wdB = Buf("h"), [Buf(), Buf()], [Buf(), Buf(), Buf(), Buf()], [Buf(), Buf()]
        ubB, yB, sgB = [Buf(), Buf()], [Buf(), Buf()], Buf()
        cB, onesB, rtB, rstdB = Buf("c"), Buf("ones"), Buf(), Buf()
        outB = Buf("out")
        ld_x, ld_c, st_o = K.dsem(), K.dsem(), K.dsem()
        ld_wb = [K.dsem() for _ in range(4)]
        ld_wd = [K.dsem() for _ in range(2)]
        bank7 = es.enter_context(nc.psum_tensor(f"{tag}_bank7", [128, 512], F32))
        bank, bankB = K.banks + [bank7], K.bankB
        U = [0, 1, 2, 3]
        Hb, D, Nb = 4, [5, 6], 7

        xin_v = d["xin"].rearrange("(c p) t -> p c t", p=128)
        for q in range(4):
            SP.dma(xT[:, 4 * q:4 * q + 4, :], xin_v[:, 4 * q:4 * q + 4, :], ld_x, writes=xB[4 * q:4 * q + 4])
        if do_outproj:
            m3 = d["m3"] if "m3" in d else d["m"].rearrange("(c p) t -> p c t", p=128)
            SP.dma(hT[:], m3, ld_x, writes=[hB], extra=d.get("m_dep", ()))
        for b_ in xB + [hB]:
            _merge(b_.w, Tok(ld_x.sem, ld_x.count))
        for (t, src) in ((cw, d["cw"]), (cb, d["cb"]), (gn, d["gn"]), (hv, d["hv"])):
            SP.dma(t[:], src, ld_c, writes=[cB])
        POOL.op(lambda h: h.memset(ones[:], 1.0), writes=[onesB])

        wb_n = [0]

        def load_wblk(src_cols):
            s = wb_n[0] % 4
            wb_n[0] += 1
            GQ.dma(wb[:, s], src_cols.rearrange("(c p) n -> p c n", p=128), ld_wb[s], writes=[wbB[s]])
            return s

        ring = {"u": 0, "d": 0}

        def next_bank(kind):
            lst = U if kind == "u" else D
            b = lst[ring[kind] % len(lst)]
            ring[kind] += 1
            return b

        def mm_group(bk, n, lhs_fn, rhs_fn, nch, reads):
            fns = []
            for c in range(nch):
                fns.append(lambda h, c=c: h.matmul(bank[bk][:, 0:n], lhs_fn(c), rhs_fn(c), start=(c == 0), stop=(c == nch - 1)))
            return PE.group(fns, reads=reads, writes=[bankB[bk]])

        if do_outproj:
            for blk in range(8):
                s = load_wblk(d["wo"][:, blk * 256:(blk + 1) * 256])
                for jj in range(2):
                    j = blk * 2 + jj
                    for (lo, hi) in TILES:
                        n = hi - lo
                        bk = Hb if n == 4 else next_bank("u")
                        mm_group(bk, n, lambda c: wb[:, s, c, jj * 128:(jj + 1) * 128], lambda c: hT[:, c, lo:hi], 16, [wbB[s], hB])
                        DVE.op(lambda h: h.tensor_tensor(out=xT[:, j, lo:hi], in0=bank[bk][:, 0:n], in1=xT[:, j, lo:hi], op=ALU.add),
                               reads=[bankB[bk]], writes=[xB[j]])

        sqv = wd[:, 0].rearrange("p a (b t) -> p (a b) t", t=512)
        for (lo, hi) in ([] if SKIPNORM else TILES):
            n = hi - lo
            ACT.op(lambda h: h.activation(out=sqv[:, :, 0:n], in_=xT[:, :, lo:hi], func=AF.Square), reads=xB, writes=[wdB[0]])
            mm_group(Nb, n, lambda c: ones[:], lambda c: sqv[:, c, 0:n], 16, [wdB[0], onesB])
            DVE.op(lambda h: h.tensor_scalar(out=rt[:, 0:n], in0=bank[Nb][:, 0:n], scalar1=1.0 / 2048.0, scalar2=1e-6, op0=ALU.mult, op1=ALU.add),
                   reads=[bankB[Nb]], writes=[rtB])
            ACT.op(lambda h: h.activation(out=rt[:, 0:n], in_=rt[:, 0:n], func=AF.Sqrt), reads=[rtB], writes=[rtB])
            DVE.op(lambda h: h.reciprocal(out=rstd[:, 0:n], in_=rt[:, 0:n]), reads=[rtB], writes=[rstdB])
            for c in range(16):
                DVE.op(lambda h: h.scalar_tensor_tensor(out=hT[:, c, lo:hi], in0=xT[:, c, lo:hi], scalar=gn[:, c:c + 1], in1=rstd[:, 0:n],
                                                        op0=ALU.mult, op1=ALU.mult),
                       reads=[xB[c], rstdB, cB], writes=[hB])

        def load_wd(grp):
            s = grp % 2
            src = d["wdn"][grp * 512:(grp + 1) * 512, :].rearrange("(a p) n -> p a n", p=128)
            GQ.dma(wd[:, s], src, ld_wd[s], writes=[wdB[s]])

        def down(grp):
            s = grp % 2
            if SKIPDOWN: return
            for dj in range(16):
                for (lo, hi) in TILES:
                    n = hi - lo
                    if n == 4:
                        if last:
                            continue
                        lo, n = 2, 2
                    glo = lo - 2
                    bk = Hb if n == 2 else next_bank("d")
                    mm_group(bk, n, lambda a: wd[:, s, a, dj * 128:(dj + 1) * 128], lambda a: gb[:, s, a, glo:glo + n], 4, [wdB[s], gB[s]])
                    DVE.op(lambda h: h.tensor_tensor(out=xT[:, dj, lo:lo + n], in0=bank[bk][:, 0:n], in1=xT[:, dj, lo:lo + n], op=ALU.add),
                           reads=[bankB[bk]], writes=[xB[dj]])

        for dp in range(NDP):
            if dp % 2 == 0:
                load_wd(dp // 2)
            slots = (load_wblk(d["wup"][:, dp * 256:(dp + 1) * 256]),
                     load_wblk(d["wup"][:, FFN + dp * 256:FFN + (dp + 1) * 256]))
            for jj in range(2):
                j = 2 * dp + jj
                grp, gi = j // 4, j % 4
                gs = grp % 2
                for kind in range(2):
                    s = slots[kind]
                    tl = kind * 44 + j
                    for (lo, hi) in TILES:
                        n = hi - lo
                        bk = Hb if n == 4 else next_bank("u")
                        mm_group(bk, n, lambda c: wb[:, s, c, jj * 128:(jj + 1) * 128], lambda c: hT[:, c, lo:hi], 16, [wbB[s], hB])
                        ACT.op(lambda h: h.activation(out=ub[kind][:, lo:hi], in_=bank[bk][:, 0:n], func=AF.Copy),
                               reads=[bankB[bk]], writes=[ubB[kind]])
                        ACT.op(lambda h: h.activation(out=yy[kind][:, lo:hi], in_=bank[bk][:, 0:n], func=AF.Identity,
                                                      bias=cb[:, tl:tl + 1], scale=cw[:, tl, 2:3]),
                               reads=[bankB[bk], cB], writes=[yB[kind]])
                    for tap, sh in ((1, 1), (0, 2)):
                        DVE.op(lambda h: h.scalar_tensor_tensor(out=yy[kind][:, 2:NT], in0=ub[kind][:, 2 - sh:NT - sh], scalar=cw[:, tl, tap:tap + 1],
                                                                in1=yy[kind][:, 2:NT], op0=ALU.mult, op1=ALU.add),
                               reads=[ubB[kind], yB[kind], cB], writes=[yB[kind]])
                ACT.op(lambda h: h.activation(out=sg[:, 2:NT], in_=yy[0][:, 2:NT], func=AF.Silu), reads=[yB[0]], writes=[sgB])
                DVE.op(lambda h: h.tensor_tensor(out=gb[:, gs, gi, :], in0=sg[:, 2:NT], in1=yy[1][:, 2:NT], op=ALU.mult),
                       reads=[sgB, yB[1]], writes=[gB[gs]])
                if gi == 3 and grp >= 1:
                    down(grp - 1)
        if NDP == 22: down(NGRP - 1)

        if not last:
            DVE.op(lambda h: h.tensor_scalar(out=xT[:, :, 0:4], in0=xT[:, :, 0:4], scalar1=hv[:, 0:1], scalar2=None, op0=ALU.mult),
                   reads=xB + [cB], writes=xB)
        if "xout" in d:
            xo_v = d["xout"].rearrange("(c p) t -> p c t", p=128)
            for q in range(4):
                SP.dma(xo_v[:, 4 * q:4 * q + 4, :], xT[:, 4 * q:4 * q + 4, :], st_o, reads=xB[4 * q:4 * q + 4], writes=[outB])
        if "xout_own" in d:
            xo_v = d["xout_own"].rearrange("(c p) t -> p c t", p=128)
            for q in range(4):
                SP.dma(xo_v[:, 4 * q:4 * q + 4, :], xT[:, 4 * q:4 * q + 4, 4:NT], st_o, reads=xB[4 * q:4 * q + 4], writes=[outB])
        SP.wait([Tok(st_o.sem, st_o.count)])
        toks = [Tok(e.sem, e.count) for e in (PE, ACT, DVE, POOL) if e.count > 0]
        toks += [Tok(s.sem, s.count) for s in [ld_x, ld_c, st_o] + ld_wb + ld_wd if s.count > 0]
        for e in (PE, ACT, DVE, POOL, SP, GQ):
            e.wait(toks)


PN = int(os.environ.get('POOLNORM', 1))
PS = int(os.environ.get('POOLSQ', 1))

TQ = 256
NTILE = 4096 // TQ
ND = 19


def attn_mask_np():
    j = np.arange(128)[:, None]
    i = np.arange(128)[None, :]
    M = np.zeros((128, ND * 128), np.float32)
    for d in range(-1, 18):
        dist = 128 * d + i - j
        m = np.zeros((128, 128), np.float32)
        for win, dil in ((128, 1), (512, 4), (2048, 16)):
            m += ((dist >= 0) & (dist <= win) & (dist % dil == 0)).astype(np.float32)
        M[:, (d + 1) * 128:(d + 2) * 128] = m
    return M.astype(ml_dtypes.bfloat16)


def phase_attn(nc, K, d, tag="at", ntile=NTILE):
    PE, ACT, DVE, POOL, SP, GQ = K.PE, K.ACT, K.DVE, K.POOL, K.SP, K.GQ
    bankB = K.bankB
    with ExitStack() as es:
        sb = lambda name, shape, dt: es.enter_context(nc.sbuf_tensor(f"sb_{tag}_{name}", shape, dt))
        xt = sb("xt", [128, 16, TQ], F32)
        ht = sb("ht", [128, 2, 16, TQ], BF16)
        wq = sb("wq", [128, 3, 16, 512], BF16)
        kT = sb("kT", [128, 4, 4096], BF16)
        va = sb("va", [128, 32, 4, 130], BF16)
        qT = sb("qT", [128, 4, TQ], BF16)
        mk = sb("mk", [128, ND * 128], BF16)
        pT = sb("pT", [128, 3, TQ], BF16)
        sqs = sb("sqs", [128, TQ], BF16)
        rt = sb("rt", [128, TQ], F32)
        rstd = sb("rstd", [128, TQ], F32)
        rq = sb("rq", [128, TQ], F32)
        on = sb("on", [128, 2, 128], BF16)
        rden = sb("rden", [128, 2], F32)
        oT = sb("oT", [128, 2, 4, TQ], BF16)
        gn = sb("gn", [128, 16], F32)
        qkg = sb("qkg", [128, 2], F32)
        ones = sb("ones", [128, 128], BF16)
        ident = sb("ident", [128, 128], BF16)
        tp = es.enter_context(nc.psum_tensor(f"{tag}_tp", [128, 1024], BF16))
        bank = K.banks

        xtB, htB, wB, kB, vB, qB, cB = Buf(), [Buf(), Buf()], Buf(), Buf(), Buf(), Buf(), Buf()
        pB, sqsB, rtB, rstdB, rqB, onB, rdB, oTB = [Buf(), Buf(), Buf()], Buf(), Buf(), Buf(), Buf(), [Buf(), Buf()], [Buf(), Buf()], [Buf(), Buf()]
        onesB, outB, tpB = Buf(), Buf(), K.bankB[7]
        ld_x, ld_c, ld_w = K.dsem(), K.dsem(), K.dsem()
        st_o = [K.dsem(), K.dsem()]
        PR, NB, SR, OB = [0, 1, 3, 4], 2, [3, 4], [5, 6]
        ring = {"p": 0, "s": 0, "pt": 0}

        toks = []
        for (t, src) in ((gn[:], d["gn"]), (qkg[:, 0:1], d["qg"]), (qkg[:, 1:2], d["kg"]), (mk[:], d["mask"]), (ident[:], d["ident"])):
            SP.dma(t, src, ld_c, writes=[cB])
        for i in range(3):
            GQ.dma(wq[:, i], d["wqkv"][:, i * 512:(i + 1) * 512].rearrange("(c p) n -> p c n", p=128), ld_w, writes=[wB])
        POOL.op(lambda h: h.memset(ones[:], 1.0), writes=[onesB])
        POOL.op(lambda h: h.memset(va[:, :, :, 128:130], 1.0), writes=[vB])
        if "opad" in d:
            POOL.op(lambda h: h.memset(on[:, 0, 0:16], 0.0), writes=[onB[0]])
            SP.dma(d["opad"].rearrange("(h p) t -> p h t", p=128), on[:, 0, 0:16].rearrange("p (h t) -> p h t", h=4), st_o[0], reads=[onB[0]], writes=[outB])
        DVE.op(lambda h: h.tensor_scalar(out=qkg[:, 0:1], in0=qkg[:, 0:1], scalar1=128.0 ** -0.5, scalar2=None, op0=ALU.mult), reads=[cB], writes=[cB])

        xv = d["xfull"].rearrange("(c p) t -> p c t", p=128)
        ov = d["oT"].rearrange("(h p) t -> p h t", p=128)

        def mm_group(bk, n, lhs_fn, rhs_fn, nch, reads):
            fns = [(lambda h, c=c: h.matmul(bank[bk][:, 0:n], lhs_fn(c), rhs_fn(c), start=(c == 0), stop=(c == nch - 1))) for c in range(nch)]
            return PE.group(fns, reads=reads, writes=[bankB[bk]])

        def nb(kind, lst):
            b = lst[ring[kind] % len(lst)]
            ring[kind] += 1
            return b

        def rsqrt_from_bank(bk, n, scale, dst, dstB):
            DVE.op(lambda h: h.tensor_scalar(out=rt[:, 0:n], in0=bank[bk][:, 0:n], scalar1=scale, scalar2=1e-6, op0=ALU.mult, op1=ALU.add),
                   reads=[bankB[bk]], writes=[rtB])
            ACT.op(lambda h: h.activation(out=rt[:, 0:n], in_=rt[:, 0:n], func=AF.Ln), reads=[rtB], writes=[rtB])
            ACT.op(lambda h: h.activation(out=dst[:, 0:n], in_=rt[:, 0:n], func=AF.Exp, scale=-0.5), reads=[rtB], writes=[dstB])

        raw = sb("raw", [128, 4, TQ], F32)
        sq2 = sb("sq2", [128, 2, TQ], BF16)
        oraw = sb("oraw", [128, 2, 2, 130], F32)
        rawB, sq2B, orawB = [Buf() for _ in range(4)], [Buf(), Buf()], [[Buf(), Buf()], [Buf(), Buf()]]
        sslots = [(0, 0), (1, 0), (3, 0), (4, 0)]
        sslotB = [bankB[0], bankB[1], bankB[3], bankB[4]]
        pT4 = sb("pT4", [128, 4, TQ], BF16)
        pB4 = [Buf() for _ in range(4)]

        def load_norm(t):
            T0 = t * TQ
            hs = t % 2
            SP.dma(xt[:], xv[:, :, T0:T0 + TQ], ld_x, writes=[xtB])
            if PN:
                POOL.op(lambda h: h.tensor_tensor(out=ht[:, hs], in0=xt[:], in1=xt[:], op=ALU.mult), reads=[xtB], writes=[htB[hs]])
            else:
                ACT.op(lambda h: h.activation(out=ht[:, hs], in_=xt[:], func=AF.Square), reads=[xtB], writes=[htB[hs]])
            mm_group(NB, TQ, lambda c: ones[:], lambda c: ht[:, hs, c, :], 16, [htB[hs], onesB])
            rsqrt_from_bank(NB, TQ, 1.0 / 2048.0, rstd, rstdB)
            for c in range(16):
                if PN:
                    POOL.op(lambda h: h.tensor_scalar(out=xt[:, c, :], in0=xt[:, c, :], scalar1=gn[:, c:c + 1], scalar2=1.0, op0=ALU.mult, op1=ALU.mult),
                            reads=[xtB, cB], writes=[xtB])
                    POOL.op(lambda h: h.tensor_tensor(out=ht[:, hs, c, :], in0=xt[:, c, :], in1=rstd[:], op=ALU.mult),
                            reads=[xtB, rstdB], writes=[htB[hs]])
                else:
                    DVE.op(lambda h: h.scalar_tensor_tensor(out=ht[:, hs, c, :], in0=xt[:, c, :], scalar=gn[:, c:c + 1], in1=rstd[:], op0=ALU.mult, op1=ALU.mult),
                           reads=[xtB, rstdB, cB], writes=[htB[hs]])

        def qk_norm_rest(f, t):
            T0 = t * TQ
            isk, hh = f // 4, f % 4
            rs, ss = f % 4, f % 2
            mm_group(NB, TQ, lambda c: ones[:], lambda c: sq2[:, ss, :], 1, [sq2B[ss], onesB])
            rsqrt_from_bank(NB, TQ, 1.0 / 128.0, rq, rqB)
            dst = kT[:, hh, T0:T0 + TQ] if isk else qT[:, hh, :]
            DVE.op(lambda h: h.scalar_tensor_tensor(out=dst, in0=raw[:, rs, :], scalar=qkg[:, isk:isk + 1], in1=rq[:], op0=ALU.mult, op1=ALU.mult),
                   reads=[rawB[rs], rqB, cB], writes=[kB if isk else qB])

        def proj(t):
            hs = t % 2
            for f in range(8):
                isk, hh = f // 4, f % 4
                rs, ss = f % 4, f % 2
                bk = nb("p", PR)
                mm_group(bk, TQ, lambda c: wq[:, isk, c, hh * 128:(hh + 1) * 128], lambda c: ht[:, hs, c, :], 16, [wB, htB[hs]])
                ACT.op(lambda h: h.activation(out=raw[:, rs, :], in_=bank[bk][:, 0:TQ], func=AF.Copy), reads=[bankB[bk]], writes=[rawB[rs]])
                if PS:
                    POOL.op(lambda h: h.tensor_tensor(out=sq2[:, ss, :], in0=raw[:, rs, :], in1=raw[:, rs, :], op=ALU.mult), reads=[rawB[rs]], writes=[sq2B[ss]])
                else:
                    ACT.op(lambda h: h.activation(out=sq2[:, ss, :], in_=raw[:, rs, :], func=AF.Square), reads=[rawB[rs]], writes=[sq2B[ss]])
                if f >= 1:
                    qk_norm_rest(f - 1, t)
            for a in range(TQ // 128):
                blk = t * (TQ // 128) + a
                bk = nb("p", PR)
                mm_group(bk, 512, lambda c: ht[:, hs, c, a * 128:(a + 1) * 128], lambda c: wq[:, 2, c, :], 16, [wB, htB[hs]])
                if a == 0:
                    qk_norm_rest(7, t)
                ACT.op(lambda h: h.activation(out=va[:, blk, :, 0:128], in_=bank[bk][:, 0:512].rearrange("p (h e) -> p h e", h=4), func=AF.Copy),
                       reads=[bankB[bk]], writes=[vB])

        def attention(t):
            T0 = t * TQ
            os_ = t % 2
            kbs = list(range(max(0, 2 * t - 16), 2 * t + 2))
            steps = [(hh, kb) for hh in range(4) for kb in kbs]
            info = {}
            pending = []

            def front(i):
                hh, kb = steps[i]
                D = 2 * t - kb
                si = ring["s"] % 4
                ring["s"] += 1
                sbk, sc0 = sslots[si]
                ps = ring["pt"] % 4
                ring["pt"] += 1
                info[i] = ps
                PE.group([lambda h: h.matmul(bank[sbk][:, sc0:sc0 + TQ], kT[:, hh, kb * 128:(kb + 1) * 128], qT[:, hh, :], start=True, stop=True)],
                         reads=[kB, qB], writes=[sslotB[si]])
                ACT.op(lambda h: h.activation(out=pT4[:, ps, :], in_=bank[sbk][:, sc0:sc0 + TQ], func=AF.Exp), reads=[sslotB[si]], writes=[pB4[ps]])
                DVE.op(lambda h: h.tensor_tensor(out=pT4[:, ps, :], in0=pT4[:, ps, :], in1=mk[:, (D + 1) * 128:(D + 3) * 128], op=ALU.mult),
                       reads=[pB4[ps], cB], writes=[pB4[ps]])

            def epi2(hh, par):
                for a in range(2):
                    DVE.op(lambda h: h.reciprocal(out=rden[:, a:a + 1], in_=oraw[:, par, a, 128:129]), reads=[orawB[par][a]], writes=[rdB[a]])
                    ACT.op(lambda h: h.activation(out=on[:, a, :], in_=oraw[:, par, a, 0:128], func=AF.Copy, scale=rden[:, a:a + 1]),
                           reads=[orawB[par][a], rdB[a]], writes=[onB[a]])
                    PE.group([lambda h: h.transpose(tp[:, a * 128:(a + 1) * 128], on[:, a, :], ident[:])], reads=[onB[a], cB], writes=[tpB])
                DVE.op(lambda h: h.tensor_copy(out=oT[:, os_, hh, :], in_=tp[:, 0:256]), reads=[tpB], writes=[oTB[os_]])

            def back(i):
                hh, kb = steps[i]
                D = 2 * t - kb
                ps = info.pop(i)
                for a in range(2):
                    dd = D + a
                    if dd < 0 or dd > 16:
                        continue
                    first = (kb == kbs[0]) or (dd == 16)
                    lastk = (dd == 0)
                    PE.group([lambda h: h.matmul(bank[OB[a]][:, 0:129], pT4[:, ps, a * 128:(a + 1) * 128], va[:, kb, hh, 0:129], start=first, stop=lastk)],
                             reads=[pB4[ps], vB], writes=[bankB[OB[a]]])
                if kb == kbs[-1]:
                    par = hh % 2
                    for a in range(2):
                        DVE.op(lambda h: h.tensor_copy(out=oraw[:, par, a, 0:129], in_=bank[OB[a]][:, 0:129]), reads=[bankB[OB[a]]], writes=[orawB[par][a]])
                    pending.append((i + min(4, 2 * len(kbs) - 1), hh, par))

            LA = 3
            n = len(steps)
            for i in range(min(LA, n)):
                front(i)
            for i in range(n):
                if i + LA < n:
                    front(i + LA)
                back(i)
                while pending and pending[0][0] <= i:
                    _, hh_, par_ = pending.pop(0)
                    epi2(hh_, par_)
            while pending:
                _, hh_, par_ = pending.pop(0)
                epi2(hh_, par_)
            SP.dma(ov[:, :, T0:T0 + TQ], oT[:, os_], st_o[os_], reads=[oTB[os_]], writes=[outB])

        load_norm(0)
        for t in range(ntile):
            proj(t)
            if t + 1 < ntile:
                load_norm(t + 1)
            attention(t)
        SP.wait([Tok(s_.sem, s_.count) for s_ in st_o if s_.count > 0])
        toks = [Tok(e.sem, e.count) for e in (PE, ACT, DVE, POOL) if e.count > 0]
        toks += [Tok(s.sem, s.count) for s in [ld_x, ld_c, ld_w] + st_o if s.count > 0]
        for e in (PE, ACT, DVE, POOL, SP, GQ):
            e.wait(toks)


TQ = 256
NTILE = 4096 // TQ


def lstm_consts_np():
    s = np.arange(128)[:, None]
    l = np.arange(128)[None, :]
    tri = ((s // 64 == l // 64) & (s <= l)).astype(np.float32)
    sel = np.zeros((128, 2), np.float32)
    sel[:64, 0] = 1.0
    sel[64:, 1] = 1.0
    return tri, sel


def phase_lstm(nc, K, d, tag="ls", ntile=NTILE):
    PE, ACT, DVE, POOL, SP, GQ = K.PE, K.ACT, K.DVE, K.POOL, K.SP, K.GQ
    bankB = K.bankB
    with ExitStack() as es:
        sb = lambda name, shape, dt: es.enter_context(nc.sbuf_tensor(f"sb_{tag}_{name}", shape, dt))
        xt = sb("xt", [128, 16, TQ], F32)
        ht = sb("ht", [128, 2, 16, TQ], BF16)
        win = sb("win", [128, 3, 16, 512], BF16)
        wg = sb("wg", [128, 16, 2], BF16)
        zq = sb("zq", [128, 4, TQ + 3], F32)
        acc = sb("acc", [128, 4, TQ], F32)
        qk = sb("qk", [128, 2, 4, TQ], BF16)
        ktm = sb("ktm", [128, 2, 2, 256], BF16)
        vp = sb("vp", [128, 2, 512], BF16)
        osg = sb("osg", [128, 2, 512], BF16)
        hb = sb("hb", [128, 512], F32)
        hjunk = sb("hjunk", [128, 512], BF16)
        htmp = sb("htmp", [128, 512], F32)
        hstm = sb("hstm", [128, 512], BF16)
        hsT = sb("hsT", [128, 2, 4, TQ], BF16)
        Sf = sb("Sf", [128, 2, 512], F32)
        Sb = sb("Sb", [128, 2, 512], BF16)
        nf = sb("nf", [128, 2], F32)
        nbf = sb("nbf", [128, 2], BF16)
        qts = sb("qts", [128, 2, 2, 64], BF16)
        WT = sb("WT", [128, 2, 128], BF16)
        gsb = sb("gsb", [128, 2, 2], F32)
        t1 = sb("t1", [128, 2, 1], F32)
        lf = sb("lf", [128, 2, 1], F32)
        lfm = sb("lfm", [128, 2, 2], F32)
        el = sb("el", [128, 2, 1], F32)
        amb = sb("amb", [128, 2, 1], F32)
        af = sb("af", [128, 2, 1], F32)
        abf = sb("abf", [128, 2, 1], BF16)
        dvec = sb("dvec", [128, 3, 2], F32)
        sm = sb("sm", [128, 8], F32)
        rt = sb("rt", [128, TQ], F32)
        rstd = sb("rstd", [128, TQ], F32)
        gn = sb("gn", [128, 16], F32)
        cw = sb("cw", [128, 4, 4], F32)
        cb = sb("cb", [128, 4], F32)
        hg = sb("hg", [128, 512], F32)
        gbias = sb("gbias", [128, 2], F32)
        tri = sb("tri", [128, 128], F32)
        onesf = sb("onesf", [128, 128], F32)
        sel = sb("sel", [128, 2], F32)
        ones = sb("ones", [128, 128], BF16)
        ident = sb("ident", [128, 128], BF16)
        tp = es.enter_context(nc.psum_tensor(f"{tag}_tp", [128, 1024], BF16))
        bank = K.banks

        B = lambda: Buf()
        xtB, htB, wB, cB, onesB = B(), [B(), B()], B(), B(), B()
        zqB, accB, qkB, ktmB = [B() for _ in range(4)], [B() for _ in range(4)], [B(), B()], [B(), B()]
        vpB, osgB, hbB, hjB, htmpB, hstmB, hsTB = [B(), B()], [B(), B()], B(), B(), B(), B(), [B(), B()]
        SfB, SbB, nfB, nbfB, qtsB, WTB = B(), B(), B(), B(), [B(), B()], [B(), B()]
        gB, smB, rtB, rstdB, outB = [B(), B()], B(), B(), B(), B()
        tpB = K.bankB[7]
        SMB = 2
        gateR, cumR, denR, dnR, sTR = B(), B(), B(), B(), B()
        PR, NUMB, DC = [0, 1], [3, 4], [5, 6]
        ld_x, ld_c, ld_w = K.dsem(), K.dsem(), K.dsem()
        st_o = [K.dsem(), K.dsem()]
        ring = {"p": 0}

        for (t, src) in ((gn[:], d["gn"]), (cw[:], d["cw"]), (cb[:], d["cb"]), (hg[:], d["hg"]), (gbias[:], d["gbias"]),
                         (tri[:], d["tri"]), (sel[:], d["sel"]), (ident[:], d["ident"])):
            SP.dma(t, src, ld_c, writes=[cB])
        for i in range(3):
            GQ.dma(win[:, i], d["win"][:, i * 512:(i + 1) * 512].rearrange("(c p) n -> p c n", p=128), ld_w, writes=[wB])
        GQ.dma(wg[:], d["wg"].rearrange("(c p) n -> p c n", p=128), ld_w, writes=[wB])
        POOL.op(lambda h: h.memset(ones[:], 1.0), writes=[onesB])
        POOL.op(lambda h: h.memset(onesf[:], 1.0), writes=[onesB])
        POOL.op(lambda h: h.memset(zq[:], 0.0), writes=zqB)
        POOL.op(lambda h: h.memset(Sf[:], 0.0), writes=[SfB])
        POOL.op(lambda h: h.memset(nf[:], 0.0), writes=[nfB])

        if "xtile" in d:
            xtile = d["xtile"]
        else:
            xv = d["xfull"].rearrange("(c p) t -> p c t", p=128)
            xtile = lambda t: xv[:, :, t * TQ:(t + 1) * TQ]
        if "opad" in d:
            POOL.op(lambda h: h.memset(hstm[:, 0:16], 0.0), writes=[hstmB])
            SP.dma(d["opad"].rearrange("(h p) t -> p h t", p=128), hstm[:, 0:16].rearrange("p (h t) -> p h t", h=4), st_o[0], reads=[hstmB], writes=[outB])
        ov = d["hsT"].rearrange("(h p) t -> p h t", p=128)

        def mm_group(out_ap, lhs_fn, rhs_fn, nch, reads, writes):
            fns = [(lambda h, c=c: h.matmul(out_ap, lhs_fn(c), rhs_fn(c), start=(c == 0), stop=(c == nch - 1))) for c in range(nch)]
            return PE.group(fns, reads=reads, writes=writes)

        def nb(kind, lst):
            b = lst[ring[kind] % len(lst)]
            ring[kind] += 1
            return b

        def load_norm(t):
            T0 = t * TQ
            hs = t % 2
            SP.dma(xt[:], xtile(t), ld_x, writes=[xtB], extra=d.get("x_dep", ()))
            ACT.op(lambda h: h.activation(out=ht[:, hs], in_=xt[:], func=AF.Square), reads=[xtB], writes=[htB[hs]])
            mm_group(bank[PR[0]][:, 0:TQ], lambda c: ones[:], lambda c: ht[:, hs, c, :], 16, [htB[hs], onesB], [bankB[PR[0]]])
            DVE.op(lambda h: h.tensor_scalar(out=rt[:], in0=bank[PR[0]][:, 0:TQ], scalar1=1.0 / 2048.0, scalar2=1e-6, op0=ALU.mult, op1=ALU.add),
                   reads=[bankB[PR[0]]], writes=[rtB])
            ACT.op(lambda h: h.activation(out=rt[:], in_=rt[:], func=AF.Ln), reads=[rtB], writes=[rtB])
            ACT.op(lambda h: h.activation(out=rstd[:], in_=rt[:], func=AF.Exp, scale=-0.5), reads=[rtB], writes=[rstdB])
            for c in range(16):
                DVE.op(lambda h: h.scalar_tensor_tensor(out=ht[:, hs, c, :], in0=xt[:, c, :], scalar=gn[:, c:c + 1], in1=rstd[:], op0=ALU.mult, op1=ALU.mult),
                       reads=[xtB, rstdB, cB], writes=[htB[hs]])

        def proj_qk(t):
            hs = t % 2
            qs = t % 2
            for f in range(4):
                bk = nb("p", PR)
                mm_group(bank[bk][:, 0:TQ], lambda c: win[:, 0, c, f * 128:(f + 1) * 128], lambda c: ht[:, hs, c, :], 16, [wB, htB[hs]], [bankB[bk]])
                ACT.op(lambda h: h.activation(out=zq[:, f, 3:3 + TQ], in_=bank[bk][:, 0:TQ], func=AF.Copy), reads=[bankB[bk]], writes=[zqB[f]])
                DVE.op(lambda h: h.tensor_scalar(out=acc[:, f, :], in0=zq[:, f, 3:3 + TQ], scalar1=cw[:, f, 3:4], scalar2=cb[:, f:f + 1], op0=ALU.mult, op1=ALU.add),
                       reads=[zqB[f], cB], writes=[accB[f]])
                for j in (1, 2, 3):
                    DVE.op(lambda h: h.scalar_tensor_tensor(out=acc[:, f, :], in0=zq[:, f, 3 - j:3 - j + TQ], scalar=cw[:, f, 3 - j:4 - j], in1=acc[:, f, :],
                                                            op0=ALU.mult, op1=ALU.add), reads=[zqB[f], accB[f], cB], writes=[accB[f]])
                DVE.op(lambda h: h.tensor_copy(out=zq[:, f, 0:3], in_=zq[:, f, TQ:TQ + 3]), reads=[zqB[f]], writes=[zqB[f]])
                ACT.op(lambda h: h.activation(out=acc[:, f, :], in_=acc[:, f, :], func=AF.Silu), reads=[accB[f]], writes=[accB[f]])
                DVE.op(lambda h: h.tensor_scalar(out=qk[:, qs, f, :], in0=acc[:, f, :], scalar1=(1.0 if f < 2 else 0.0625), scalar2=None, op0=ALU.mult),
                       reads=[accB[f]], writes=[qkB[qs]])
            for blk in range(2):
                for dkc in range(2):
                    col = (blk * 2 + dkc) * 128
                    PE.group([lambda h: h.transpose(tp[:, col:col + 128], qk[:, qs, 2 + dkc, blk * 128:(blk + 1) * 128], ident[:])],
                             reads=[qkB[qs], cB], writes=[tpB])
            DVE.op(lambda h: h.tensor_copy(out=ktm[:, qs].rearrange("p b k -> p (b k)"), in_=tp[:, 0:512]), reads=[tpB], writes=[ktmB[qs]])

        def block(t, blk, state):
            hs = t % 2
            qs = t % 2
            bs = blk
            tok = slice(blk * 128, (blk + 1) * 128)
            mm_group(bank[SMB][:, 0:2], lambda c: ht[:, hs, c, tok], lambda c: wg[:, c, :], 16, [wB, htB[hs]], [gateR])
            DVE.op(lambda h: h.tensor_tensor(out=gsb[:, bs, :], in0=bank[SMB][:, 0:2], in1=gbias[:], op=ALU.add), reads=[gateR, cB], writes=[gB[bs]])
            ACT.op(lambda h: h.activation(out=t1[:, bs, :], in_=gsb[:, bs, 1:2], func=AF.Exp, scale=-1.0), reads=[gB[bs]], writes=[gB[bs]])
            DVE.op(lambda h: h.tensor_scalar(out=t1[:, bs, :], in0=t1[:, bs, :], scalar1=1.0, scalar2=None, op0=ALU.add), reads=[gB[bs]], writes=[gB[bs]])
            ACT.op(lambda h: h.activation(out=t1[:, bs, :], in_=t1[:, bs, :], func=AF.Ln), reads=[gB[bs]], writes=[gB[bs]])
            DVE.op(lambda h: h.tensor_scalar(out=lf[:, bs, :], in0=t1[:, bs, :], scalar1=-1.0, scalar2=None, op0=ALU.mult), reads=[gB[bs]], writes=[gB[bs]])
            DVE.op(lambda h: h.tensor_scalar(out=lfm[:, bs, :], in0=sel[:], scalar1=lf[:, bs, 0:1], scalar2=None, op0=ALU.mult), reads=[gB[bs], cB], writes=[gB[bs]])
            PE.group([lambda h: h.matmul(bank[SMB][:, 4:5], tri[:], lf[:, bs, :], start=True, stop=True),
                      lambda h: h.matmul(bank[SMB][:, 6:8], onesf[:], lfm[:, bs, :], start=True, stop=True)],
                     reads=[gB[bs], cB, onesB], writes=[cumR])
            ACT.op(lambda h: h.activation(out=el[:, bs, :], in_=bank[SMB][:, 4:5], func=AF.Exp), reads=[cumR], writes=[gB[bs]])
            DVE.op(lambda h: h.tensor_tensor(out=amb[:, bs, :], in0=gsb[:, bs, 0:1], in1=bank[SMB][:, 4:5], op=ALU.subtract), reads=[cumR, gB[bs]], writes=[gB[bs]])
            ACT.op(lambda h: h.activation(out=af[:, bs, :], in_=amb[:, bs, :], func=AF.Exp), reads=[gB[bs]], writes=[gB[bs]])
            DVE.op(lambda h: h.tensor_copy(out=abf[:, bs, :], in_=af[:, bs, :]), reads=[gB[bs]], writes=[gB[bs]])
            ACT.op(lambda h: h.activation(out=dvec[:, 1 + bs, :], in_=bank[SMB][:, 6:8], func=AF.Exp), reads=[cumR], writes=[gB[bs]])
            bk = nb("p", PR)
            mm_group(bank[bk][:, 0:512], lambda c: ht[:, hs, c, tok], lambda c: win[:, 1, c, :], 16, [wB, htB[hs]], [bankB[bk]])
            ACT.op(lambda h: h.activation(out=vp[:, bs, :], in_=bank[bk][:, 0:512], func=AF.Copy, scale=af[:, bs, :]), reads=[bankB[bk], gB[bs]], writes=[vpB[bs]])
            bk = nb("p", PR)
            mm_group(bank[bk][:, 0:512], lambda c: ht[:, hs, c, tok], lambda c: win[:, 2, c, :], 16, [wB, htB[hs]], [bankB[bk]])
            ACT.op(lambda h: h.activation(out=osg[:, bs, :], in_=bank[bk][:, 0:512], func=AF.Sigmoid), reads=[bankB[bk]], writes=[osgB[bs]])
            mm_group(bank[SMB][:, 128:256], lambda c: qk[:, qs, 2 + c, tok], lambda c: qk[:, qs, c, tok], 2, [qkB[qs]], [sTR])
            DVE.op(lambda h: h.tensor_tensor(out=WT[:, bs, :], in0=bank[SMB][:, 128:256], in1=tri[:], op=ALU.mult), reads=[sTR, cB], writes=[WTB[bs]])
            nbk = NUMB[(t * 2 + blk) % 2]
            for half in range(2):
                c_idx = (t * 2 + blk) * 2 + half
                r0 = half * 64
                rows = slice(r0, r0 + 64)
                ctok = slice(blk * 128 + r0, blk * 128 + r0 + 64)
                dprev = dvec[:, 1 + bs, 0:1] if (half == 1 or c_idx == 0) else dvec[:, 0, 1:2]
                if c_idx > 0:
                    ACT.op(lambda h: h.activation(out=qts[:, half], in_=qk[:, qs, 0:2, ctok], func=AF.Copy, scale=dprev),
                           reads=[qkB[qs], gB[bs], gB[1 - bs]], writes=[qtsB[half]])
                fns = []
                if c_idx > 0:
                    for dkc in range(2):
                        fns.append(lambda h, dkc=dkc: h.matmul(bank[nbk][rows, 0:512], qts[:, half, dkc, :], Sb[:, dkc, :], start=(dkc == 0), stop=False))
                fns.append(lambda h: h.matmul(bank[nbk][rows, 0:512], WT[rows, bs, r0:r0 + 64], vp[rows, bs, :], start=(c_idx == 0), stop=True))
                PE.group(fns, reads=[qtsB[half], SbB, WTB[bs], vpB[bs]], writes=[bankB[nbk]])
                fns = []
                if c_idx > 0:
                    for dkc in range(2):
                        fns.append(lambda h, dkc=dkc: h.matmul(bank[SMB][rows, 8:9], qts[:, half, dkc, :], nbf[:, dkc:dkc + 1], start=(dkc == 0), stop=False))
                fns.append(lambda h: h.matmul(bank[SMB][rows, 8:9], WT[rows, bs, r0:r0 + 64], abf[rows, bs, :], start=(c_idx == 0), stop=True))
                PE.group(fns, reads=[qtsB[half], nbfB, WTB[bs], gB[bs]], writes=[denR])
                for dkc in range(2):
                    PE.group([lambda h: h.matmul(bank[DC[dkc]][:, 0:512], ktm[rows, qs, blk, dkc * 128:(dkc + 1) * 128], vp[rows, bs, :], start=True, stop=True)],
                             reads=[ktmB[qs], vpB[bs]], writes=[bankB[DC[dkc]]])
                PE.group([(lambda h, dkc=dkc: h.matmul(bank[SMB][:, 12 + dkc:13 + dkc], ktm[rows, qs, blk, dkc * 128:(dkc + 1) * 128], abf[rows, bs, :], start=True, stop=True))
                          for dkc in range(2)], reads=[ktmB[qs], gB[bs]], writes=[dnR])
                dc = dprev
                for dkc in range(2):
                    DVE.op(lambda h: h.scalar_tensor_tensor(out=Sf[:, dkc, :], in0=Sf[:, dkc, :], scalar=dc, in1=bank[DC[dkc]][:, 0:512], op0=ALU.mult, op1=ALU.add),
                           reads=[bankB[DC[dkc]], gB[bs], gB[1 - bs], SfB], writes=[SfB])
                DVE.op(lambda h: h.scalar_tensor_tensor(out=nf[:], in0=nf[:], scalar=dc, in1=bank[SMB][:, 12:14], op0=ALU.mult, op1=ALU.add),
                       reads=[dnR, gB[bs], gB[1 - bs], nfB], writes=[nfB])
                ACT.op(lambda h: h.activation(out=Sb[:], in_=Sf[:], func=AF.Copy), reads=[SfB], writes=[SbB])
                DVE.op(lambda h: h.tensor_copy(out=nbf[:], in_=nf[:]), reads=[nfB], writes=[nbfB])
            DVE.op(lambda h: h.tensor_copy(out=dvec[:, 0, :], in_=dvec[:, 1 + bs, :]), reads=[gB[bs]], writes=[gB[bs], gB[1 - bs]])
            DVE.op(lambda h: h.tensor_scalar(out=sm[:, 0:1], in0=bank[SMB][:, 8:9], scalar1=el[:, bs, :], scalar2=None, op0=ALU.mult),
                   reads=[denR, gB[bs]], writes=[smB])
            DVE.op(lambda h: h.tensor_scalar(out=sm[:, 6:7], in0=sm[:, 0:1], scalar1=-1.0, scalar2=1.0, op0=ALU.mult, op1=ALU.max), reads=[smB], writes=[smB])
            DVE.op(lambda h: h.tensor_tensor(out=sm[:, 0:1], in0=sm[:, 0:1], in1=sm[:, 6:7], op=ALU.max), reads=[smB], writes=[smB])
            DVE.op(lambda h: h.reciprocal(out=sm[:, 1:2], in_=sm[:, 0:1]), reads=[smB], writes=[smB])
            DVE.op(lambda h: h.tensor_tensor(out=sm[:, 2:3], in0=sm[:, 1:2], in1=el[:, bs, :], op=ALU.mult), reads=[smB, gB[bs]], writes=[smB])
            ACT.op(lambda h: h.activation(out=hb[:], in_=bank[nbk][:, 0:512], func=AF.Copy, scale=sm[:, 2:3]), reads=[bankB[nbk], smB], writes=[hbB])
            ACT.op(lambda h: h.activation(out=hjunk[:], in_=hb[:], func=AF.Square, accum_out=sm[:, 3:4]), reads=[hbB], writes=[hjB, smB])
            DVE.op(lambda h: h.tensor_scalar(out=sm[:, 4:5], in0=sm[:, 3:4], scalar1=1.0 / 512.0, scalar2=1e-6, op0=ALU.mult, op1=ALU.add), reads=[smB], writes=[smB])
            ACT.op(lambda h: h.activation(out=sm[:, 4:5], in_=sm[:, 4:5], func=AF.Ln), reads=[smB], writes=[smB])
            ACT.op(lambda h: h.activation(out=sm[:, 5:6], in_=sm[:, 4:5], func=AF.Exp, scale=-0.5), reads=[smB], writes=[smB])
            DVE.op(lambda h: h.scalar_tensor_tensor(out=htmp[:], in0=hb[:], scalar=sm[:, 5:6], in1=hg[:], op0=ALU.mult, op1=ALU.mult),
                   reads=[hbB, smB, cB], writes=[htmpB])
            DVE.op(lambda h: h.tensor_tensor(out=hstm[:], in0=htmp[:], in1=osg[:, bs, :], op=ALU.mult), reads=[htmpB, osgB[bs]], writes=[hstmB])
            os_ = t % 2
            for j in range(4):
                PE.group([lambda h: h.transpose(tp[:, 512 + j * 128:512 + (j + 1) * 128], hstm[:, j * 128:(j + 1) * 128], ident[:])],
                         reads=[hstmB, cB], writes=[tpB])
            DVE.op(lambda h: h.tensor_copy(out=hsT[:, os_, :, tok], in_=tp[:, 512:1024].rearrange("p (j t) -> p j t", j=4)), reads=[tpB], writes=[hsTB[os_]])

        load_norm(0)
        for t in range(ntile):
            proj_qk(t)
            for blk in range(2):
                block(t, blk, None)
            if t + 1 < ntile:
                load_norm(t + 1)
            SP.dma(ov[:, :, t * TQ:(t + 1) * TQ], hsT[:, t % 2], st_o[t % 2], reads=[hsTB[t % 2]], writes=[outB])
        SP.wait([Tok(s_.sem, s_.count) for s_ in st_o if s_.count > 0])
        toks = [Tok(e.sem, e.count) for e in (PE, ACT, DVE, POOL) if e.count > 0]
        toks += [Tok(s.sem, s.count) for s in [ld_x, ld_c, ld_w] + st_o if s.count > 0]
        for e in (PE, ACT, DVE, POOL, SP, GQ):
            e.wait(toks)


def _c(a):
    return np.ascontiguousarray(a)


def _build_attn():
    nc = bass.Bass("TRN2", target_bir_lowering=False)
    d = {}

    def inp(name, shape, dt=F32):
        d[name] = nc.dram_tensor(name, shape, dt, kind="ExternalInput").ap()
    inp("xfull", [2048, 4096]); inp("wqkv", [2048, 1536]); inp("gn", [128, 16]); inp("qg", [128, 1]); inp("kg", [128, 1])
    inp("mask", [128, ND * 128], BF16); inp("ident", [128, 128], BF16)
    d["oT"] = nc.dram_tensor("oT", [512, 4096], BF16, kind="ExternalOutput").ap()
    with ExitStack() as es:
        K = Kit(nc, es)
        phase_attn(nc, K, d)
    return nc


def _build_lstm():
    nc = bass.Bass("TRN2", target_bir_lowering=False)
    d = {}

    def inp(name, shape, dt=F32):
        d[name] = nc.dram_tensor(name, shape, dt, kind="ExternalInput").ap()
    inp("xfull", [2048, 4096]); inp("win", [2048, 1536]); inp("wg", [2048, 2]); inp("gn", [128, 16])
    inp("cw", [128, 4, 4]); inp("cb", [128, 4]); inp("hg", [128, 512]); inp("gbias", [128, 2])
    inp("tri", [128, 128]); inp("sel", [128, 2]); inp("ident", [128, 128], BF16)
    d["hsT"] = nc.dram_tensor("hsT", [512, 4096], BF16, kind="ExternalOutput").ap()
    with ExitStack() as es:
        K = Kit(nc, es)
        phase_lstm(nc, K, d)
    return nc


def _build_of(last):
    nc = bass.Bass("TRN2", target_bir_lowering=False)
    d = {}

    def inp(name, shape, dt=F32):
        d[name] = nc.dram_tensor(name, shape, dt, kind="ExternalInput").ap()
    inp("xin", [2048, NT]); inp("m", [2048, NT], BF16); inp("wo", [2048, 2048]); inp("wup", [2048, 2 * FFN]); inp("wdn", [FFN, 2048])
    inp("cw", [128, 88, 3]); inp("cb", [128, 88]); inp("gn", [128, 16]); inp("hv", [128, 1])
    d["xout"] = nc.dram_tensor("xout", [2048, NT], F32, kind="ExternalOutput").ap()
    with ExitStack() as es:
        K = Kit(nc, es)
        phase_of(nc, K, d, True, last)
    return nc


def _tok_shard(fullT, c):
    s0 = c * 1024
    out = np.zeros((fullT.shape[0], NT), fullT.dtype)
    lo = max(0, s0 - 4)
    out[:, 4 - (s0 - lo):] = fullT[:, lo:s0 + 1024]
    return out


def _of_inputs(z, L, xT_full, mT_full, wo):
    wup = _c(z["ffn_w_up"][L]); wdn = _c(z["ffn_w_down"][L])
    cw = _c(z["ffn_conv_w"][L].reshape(3, 88, 128).transpose(2, 1, 0))
    cb = _c(z["ffn_conv_b"][L].reshape(88, 128).T)
    gn = _c(z["ffn_norm"][L].reshape(16, 128).T)
    maps = []
    for core in range(8):
        b, c = core // 4, core % 4
        maps.append({"xin": _tok_shard(xT_full[b], c), "m": _tok_shard(mT_full[b], c), "wo": wo, "wup": wup, "wdn": wdn,
                     "cw": cw, "cb": cb, "gn": gn, "hv": np.full((128, 1), 0.0 if c == 0 else 1.0, np.float32)})
    return maps


def _lstm_inputs(z, xT_b, hd):
    w = z["lstm_w_in"][0]
    win = np.concatenate([w[:, hd * 256:(hd + 1) * 256], w[:, 1024 + hd * 256:1024 + (hd + 1) * 256],
                          w[:, 2048 + hd * 512:2048 + (hd + 1) * 512], w[:, 4096 + hd * 512:4096 + (hd + 1) * 512]], axis=1)
    wg = np.stack([w[:, 6144 + hd], w[:, 6148 + hd]], axis=1)
    cwf = z["lstm_conv_w"][0]; cbf = z["lstm_conv_b"][0]
    chans = np.concatenate([np.arange(hd * 256, (hd + 1) * 256), np.arange(1024 + hd * 256, 1024 + (hd + 1) * 256)])
    cw = cwf[:, chans].reshape(4, 4, 128).transpose(2, 1, 0)
    cb = cbf[chans].reshape(4, 128).T
    hg = np.broadcast_to(z["lstm_head_gain"][0][hd * 512:(hd + 1) * 512][None, :], (128, 512))
    gb = z["lstm_gate_bias"][0]
    gbias = np.broadcast_to(np.stack([gb[hd], gb[4 + hd]])[None, :], (128, 2))
    tri, sel = lstm_consts_np()
    return {"xfull": _c(xT_b), "win": _c(win), "wg": _c(wg), "gn": _c(z["lstm_norm"][0].reshape(16, 128).T), "cw": _c(cw), "cb": _c(cb),
            "hg": _c(hg), "gbias": _c(gbias), "tri": tri, "sel": sel, "ident": np.eye(128, dtype=np.float32).astype(ml_dtypes.bfloat16)}


def kernel(**inputs):
    z = {k: np.asarray(v, dtype=np.float32) for k, v in inputs.items()}
    x = z["x"]
    cores = list(range(8))
    xT = [_c(x[b].T) for b in range(2)]
    wqkv = z["attn_w_qkv"][0]
    gn = _c(z["attn_norm"][0].reshape(16, 128).T)
    qg = _c(z["attn_q_gain"][0].reshape(128, 1)); kg = _c(z["attn_k_gain"][0].reshape(128, 1))
    mask = attn_mask_np(); ident = np.eye(128, dtype=np.float32).astype(ml_dtypes.bfloat16)
    maps = []
    for core in cores:
        b, g = core // 4, core % 4
        w = np.concatenate([wqkv[:, i * 2048 + g * 512: i * 2048 + (g + 1) * 512] for i in range(3)], axis=1)
        maps.append({"xfull": xT[b], "wqkv": _c(w), "gn": gn, "qg": qg, "kg": kg, "mask": mask, "ident": ident})
    r1 = run_bass_kernel_spmd(_build_attn(), maps, core_ids=cores)
    oT = [np.concatenate([r1.results[b * 4 + g]["oT"] for g in range(4)], axis=0) for b in range(2)]
    r2 = run_bass_kernel_spmd(_build_of(False), _of_inputs(z, 0, xT, oT, _c(z["attn_w_o"][0])), core_ids=cores)
    x2T = [np.concatenate([r2.results[b * 4 + c]["xout"][:, 4:] for c in range(4)], axis=1) for b in range(2)]
    r3 = run_bass_kernel_spmd(_build_lstm(), [_lstm_inputs(z, x2T[core // 4], core % 4) for core in cores], core_ids=cores)
    hsT = [np.concatenate([r3.results[b * 4 + g]["hsT"] for g in range(4)], axis=0) for b in range(2)]
    r4 = run_bass_kernel_spmd(_build_of(True), _of_inputs(z, 1, x2T, hsT, _c(z["lstm_w_out"][0])), core_ids=cores)
    out = np.empty((2, 4096, 2048), np.float32)
    for core in cores:
        b, c = core // 4, core % 4
        out[b, c * 1024:(c + 1) * 1024, :] = r4.results[core]["xout"][:, 4:].T
    return out
```

```python
from contextlib import ExitStack
import os


import numpy as np
import ml_dtypes
import concourse.bass as bass
import concourse.mybir as mybir
from concourse.bass_utils import run_bass_kernel_spmd

F32 = mybir.dt.float32
BF16 = mybir.dt.bfloat16
AF = mybir.ActivationFunctionType
ALU = mybir.AluOpType
AX = mybir.AxisListType


class Tok:
    __slots__ = ("sem", "val")

    def __init__(self, sem, val):
        self.sem, self.val = sem, val


class Buf:
    def __init__(self, name=""):
        self.name = name
        self.w = {}
        self.r = {}


def _merge(d, tok):
    k = id(tok.sem)
    if k not in d or d[k].val < tok.val:
        d[k] = tok


class Eng:
    def __init__(self, h, sem, skip_self=False):
        self.h, self.sem, self.count, self.waited = h, sem, 0, {}
        self.skip_self = skip_self

    def wait(self, deps):
        for d in deps:
            if d is None:
                continue
            k = id(d.sem)
            if self.waited.get(k, 0) >= d.val:
                continue
            if getattr(self, "skip_self", False) and d.sem is self.sem:
                continue
            self.h.wait_ge(d.sem, d.val)
            self.waited[k] = d.val

    @staticmethod
    def _deps(reads, writes):
        deps = []
        for b in reads:
            deps.extend(b.w.values())
        for b in writes:
            deps.extend(b.w.values())
            deps.extend(b.r.values())
        return deps

    @staticmethod
    def _commit(tok, reads, writes):
        for b in reads:
            _merge(b.r, tok)
        for b in writes:
            _merge(b.w, tok)

    def op(self, fn, reads=(), writes=(), extra=()):
        self.wait(self._deps(reads, writes))
        self.wait(extra)
        inst = fn(self.h)
        self.count += 1
        inst.then_inc(self.sem, 1)
        tok = Tok(self.sem, self.count)
        self._commit(tok, reads, writes)
        return tok

    def group(self, fns, reads=(), writes=(), extra=()):
        self.wait(self._deps(reads, writes))
        self.wait(extra)
        inst = None
        for fn in fns:
            inst = fn(self.h)
        self.count += 1
        inst.then_inc(self.sem, 1)
        tok = Tok(self.sem, self.count)
        self._commit(tok, reads, writes)
        return tok


class DmaSem:
    def __init__(self, sem):
        self.sem, self.count = sem, 0


class DmaQ:
    def __init__(self, h):
        self.h, self.waited = h, {}

    wait = Eng.wait

    def dma(self, out, in_, dsem, reads=(), writes=(), extra=(), **kw):
        self.wait(Eng._deps(reads, writes))
        self.wait(extra)
        self.h.dma_start(out=out, in_=in_, **kw).then_inc(dsem.sem, 16)
        dsem.count += 16
        tok = Tok(dsem.sem, dsem.count)
        Eng._commit(tok, reads, writes)
        return tok


NDP = int(os.environ.get('NDP', 22))
SKIPDOWN = int(os.environ.get('SKIPDOWN', 0))
SKIPNORM = int(os.environ.get('SKIPNORM', 0))

NT = 1028
TILES = [(0, 4), (4, 516), (516, 1028)]
NPAIR = 44
NGRP = 11
FFN = 5632


class Kit:
    def __init__(self, nc, es):
        self.nc = nc
        sem = lambda n: es.enter_context(nc.semaphore(n))
        self.PE = Eng(nc.tensor, sem("s_pe"), skip_self=True)
        self.ACT = Eng(nc.scalar, sem("s_act"))
        self.DVE = Eng(nc.vector, sem("s_dve"))
        self.POOL = Eng(nc.gpsimd, sem("s_pool"))
        self.SP = DmaQ(nc.sync)
        self.GQ = DmaQ(nc.gpsimd)
        self.es = es
        self.banks = [es.enter_context(nc.psum_tensor(f"bank{i}", [128, 512], F32)) for i in range(7)]
        self.bankB = [Buf(f"bank{i}") for i in range(8)]
        self._nsem = 0

    def dsem(self):
        self._nsem += 1
        return DmaSem(self.es.enter_context(self.nc.semaphore(f"dsem{self._nsem}")))

    def barrier(self):
        toks = [Tok(e.sem, e.count) for e in (self.PE, self.ACT, self.DVE, self.POOL) if e.count > 0]
        toks += self._dtoks()
        for e in (self.PE, self.ACT, self.DVE, self.POOL, self.SP, self.GQ):
            e.wait(toks)

    def _dtoks(self):
        return [Tok(d.sem, d.count) for d in getattr(self, "_dsems", []) if d.count > 0]


def phase_of(nc, K, d, do_outproj, last, tag="of"):
    PE, ACT, DVE, POOL, SP, GQ = K.PE, K.ACT, K.DVE, K.POOL, K.SP, K.GQ
    with ExitStack() as es:
        sb = lambda name, shape, dt: es.enter_context(nc.sbuf_tensor(f"sb_{tag}_{name}", shape, dt))
        xT = sb("xT", [128, 16, NT], F32)
        hT = sb("hT", [128, 16, NT], BF16)
        gb = sb("gb", [128, 2, 4, 1026], BF16)
        wb = sb("wb", [128, 4, 16, 256], BF16)
        wd = sb("wd", [128, 2, 4, 2048], BF16)
        ub = [sb(f"ub{i}", [128, NT], F32) for i in range(2)]
        yy = [sb(f"yy{i}", [128, NT], F32) for i in range(2)]
        sg = sb("sg", [128, NT], F32)
        cw = sb("cw", [128, 88, 3], F32)
        cb = sb("cb", [128, 88], F32)
        gn = sb("gn", [128, 16], F32)
        hv = sb("hv", [128, 1], F32)
        ones = sb("ones", [128, 128], BF16)
        rt = sb("rt", [128, 512], F32)
        rstd = sb("rstd", [128, 512], F32)

        xB = [Buf(f"x{j}") for j in range(16)]
        hB, gB, wbB, wdB = Buf("h"), [Buf(), Buf()], [Buf(), Buf(), Buf(), Buf()], [Buf(), Buf()]
        ubB, yB, sgB = [Buf(), Buf()], [Buf(), Buf()], Buf()
        cB, onesB, rtB, rstdB = Buf("c"), Buf("ones"), Buf(), Buf()
        outB = Buf("out")
        ld_x, ld_c, st_o = K.dsem(), K.dsem(), K.dsem()
        ld_wb = [K.dsem() for _ in range(4)]
        ld_wd = [K.dsem() for _ in range(2)]
        bank7 = es.enter_context(nc.psum_tensor(f"{tag}_bank7", [128, 512], F32))
        bank, bankB = K.banks + [bank7], K.bankB
        U = [0, 1, 2, 3]
        Hb, D, Nb = 4, [5, 6], 7

        xin_v = d["xin"].rearrange("(c p) t -> p c t", p=128)
        for q in range(4):
            SP.dma(xT[:, 4 * q:4 * q + 4, :], xin_v[:, 4 * q:4 * q + 4, :], ld_x, writes=xB[4 * q:4 * q + 4])
        if do_outproj:
            m3 = d["m3"] if "m3" in d else d["m"].rearrange("(c p) t -> p c t", p=128)
            SP.dma(hT[:], m3, ld_x, writes=[hB], extra=d.get("m_dep", ()))
        for b_ in xB + [hB]:
            _merge(b_.w, Tok(ld_x.sem, ld_x.count))
        for (t, src) in ((cw, d["cw"]), (cb, d["cb"]), (gn, d["gn"]), (hv, d["hv"])):
            SP.dma(t[:], src, ld_c, writes=[cB])
        POOL.op(lambda h: h.memset(ones[:], 1.0), writes=[onesB])

        wb_n = [0]

        def load_wblk(src_cols):
            s = wb_n[0] % 4
            wb_n[0] += 1
            GQ.dma(wb[:, s], src_cols.rearrange("(c p) n -> p c n", p=128), ld_wb[s], writes=[wbB[s]])
            return s

        ring = {"u": 0, "d": 0}

        def next_bank(kind):
            lst = U if kind == "u" else D
            b = lst[ring[kind] % len(lst)]
            ring[kind] += 1
            return b

        def mm_group(bk, n, lhs_fn, rhs_fn, nch, reads):
            fns = []
            for c in range(nch):
                fns.append(lambda h, c=c: h.matmul(bank[bk][:, 0:n], lhs_fn(c), rhs_fn(c), start=(c == 0), stop=(c == nch - 1)))
            return PE.group(fns, reads=reads, writes=[bankB[bk]])

        if do_outproj:
            for blk in range(8):
                s = load_wblk(d["wo"][:, blk * 256:(blk + 1) * 256])
                for jj in range(2):
                    j = blk * 2 + jj
                    for (lo, hi) in TILES:
                        n = hi - lo
                        bk = Hb if n == 4 else next_bank("u")
                        mm_group(bk, n, lambda c: wb[:, s, c, jj * 128:(jj + 1) * 128], lambda c: hT[:, c, lo:hi], 16, [wbB[s], hB])
                        DVE.op(lambda h: h.tensor_tensor(out=xT[:, j, lo:hi], in0=bank[bk][:, 0:n], in1=xT[:, j, lo:hi], op=ALU.add),
                               reads=[bankB[bk]], writes=[xB[j]])

        sqv = wd[:, 0].rearrange("p a (b t) -> p (a b) t", t=512)
        for (lo, hi) in ([] if SKIPNORM else TILES):
            n = hi - lo
            ACT.op(lambda h: h.activation(out=sqv[:, :, 0:n], in_=xT[:, :, lo:hi], func=AF.Square), reads=xB, writes=[wdB[0]])
            mm_group(Nb, n, lambda c: ones[:], lambda c: sqv[:, c, 0:n], 16, [wdB[0], onesB])
            DVE.op(lambda h: h.tensor_scalar(out=rt[:, 0:n], in0=bank[Nb][:, 0:n], scalar1=1.0 / 2048.0, scalar2=1e-6, op0=ALU.mult, op1=ALU.add),
                   reads=[bankB[Nb]], writes=[rtB])
            ACT.op(lambda h: h.activation(out=rt[:, 0:n], in_=rt[:, 0:n], func=AF.Sqrt), reads=[rtB], writes=[rtB])
            DVE.op(lambda h: h.reciprocal(out=rstd[:, 0:n], in_=rt[:, 0:n]), reads=[rtB], writes=[rstdB])
            for c in range(16):
                DVE.op(lambda h: h.scalar_tensor_tensor(out=hT[:, c, lo:hi], in0=xT[:, c, lo:hi], scalar=gn[:, c:c + 1], in1=rstd[:, 0:n],
                                                        op0=ALU.mult, op1=ALU.mult),
                       reads=[xB[c], rstdB, cB], writes=[hB])

        def load_wd(grp):
            s = grp % 2
            src = d["wdn"][grp * 512:(grp + 1) * 512, :].rearrange("(a p) n -> p a n", p=128)
            GQ.dma(wd[:, s], src, ld_wd[s], writes=[wdB[s]])

        def down(grp):
            s = grp % 2
            if SKIPDOWN: return
            for dj in range(16):
                for (lo, hi) in TILES:
                    n = hi - lo
                    if n == 4:
                        if last:
                            continue
                        lo, n = 2, 2
                    glo = lo - 2
                    bk = Hb if n == 2 else next_bank("d")
                    mm_group(bk, n, lambda a: wd[:, s, a, dj * 128:(dj + 1) * 128], lambda a: gb[:, s, a, glo:glo + n], 4, [wdB[s], gB[s]])
                    DVE.op(lambda h: h.tensor_tensor(out=xT[:, dj, lo:lo + n], in0=bank[bk][:, 0:n], in1=xT[:, dj, lo:lo + n], op=ALU.add),
                           reads=[bankB[bk]], writes=[xB[dj]])

        for dp in range(NDP):
            if dp % 2 == 0:
                load_wd(dp // 2)
            slots = (load_wblk(d["wup"][:, dp * 256:(dp + 1) * 256]),
                     load_wblk(d["wup"][:, FFN + dp * 256:FFN + (dp + 1) * 256]))
            for jj in range(2):
                j = 2 * dp + jj
                grp, gi = j // 4, j % 4
                gs = grp % 2
                for kind in range(2):
                    s = slots[kind]
                    tl = kind * 44 + j
                    for (lo, hi) in TILES:
                        n = hi - lo
                        bk = Hb if n == 4 else next_bank("u")
                        mm_group(bk, n, lambda c: wb[:, s, c, jj * 128:(jj + 1) * 128], lambda c: hT[:, c, lo:hi], 16, [wbB[s], hB])
                        ACT.op(lambda h: h.activation(out=ub[kind][:, lo:hi], in_=bank[bk][:, 0:n], func=AF.Copy),
                               reads=[bankB[bk]], writes=[ubB[kind]])
                        ACT.op(lambda h: h.activation(out=yy[kind][:, lo:hi], in_=bank[bk][:, 0:n], func=AF.Identity,
                                                      bias=cb[:, tl:tl + 1], scale=cw[:, tl, 2:3]),
                               reads=[bankB[bk], cB], writes=[yB[kind]])
                    for tap, sh in ((1, 1), (0, 2)):
                        DVE.op(lambda h: h.scalar_tensor_tensor(out=yy[kind][:, 2:NT], in0=ub[kind][:, 2 - sh:NT - sh], scalar=cw[:, tl, tap:tap + 1],
                                                                in1=yy[kind][:, 2:NT], op0=ALU.mult, op1=ALU.add),
                               reads=[ubB[kind], yB[kind], cB], writes=[yB[kind]])
                ACT.op(lambda h: h.activation(out=sg[:, 2:NT], in_=yy[0][:, 2:NT], func=AF.Silu), reads=[yB[0]], writes=[sgB])
                DVE.op(lambda h: h.tensor_tensor(out=gb[:, gs, gi, :], in0=sg[:, 2:NT], in1=yy[1][:, 2:NT], op=ALU.mult),
                       reads=[sgB, yB[1]], writes=[gB[gs]])
                if gi == 3 and grp >= 1:
                    down(grp - 1)
        if NDP == 22: down(NGRP - 1)

        if not last:
            DVE.op(lambda h: h.tensor_scalar(out=xT[:, :, 0:4], in0=xT[:, :, 0:4], scalar1=hv[:, 0:1], scalar2=None, op0=ALU.mult),
                   reads=xB + [cB], writes=xB)
        if "xout" in d:
            xo_v = d["xout"].rearrange("(c p) t -> p c t", p=128)
            for q in range(4):
                SP.dma(xo_v[:, 4 * q:4 * q + 4, :], xT[:, 4 * q:4 * q + 4, :], st_o, reads=xB[4 * q:4 * q + 4], writes=[outB])
        if "xout_own" in d:
            xo_v = d["xout_own"].rearrange("(c p) t -> p c t", p=128)
            for q in range(4):
                SP.dma(xo_v[:, 4 * q:4 * q + 4, :], xT[:, 4 * q:4 * q + 4, 4:NT], st_o, reads=xB[4 * q:4 * q + 4], writes=[outB])
        SP.wait([Tok(st_o.sem, st_o.count)])
        toks = [Tok(e.sem, e.count) for e in (PE, ACT, DVE, POOL) if e.count > 0]
        toks += [Tok(s.sem, s.count) for s in [ld_x, ld_c, st_o] + ld_wb + ld_wd if s.count > 0]
        for e in (PE, ACT, DVE, POOL, SP, GQ):
            e.wait(toks)


PN = int(os.environ.get('POOLNORM', 1))
PS = int(os.environ.get('POOLSQ', 1))

TQ = 256
NTILE = 4096 // TQ
ND = 19


def attn_mask_np():
    j = np.arange(128)[:, None]
    i = np.arange(128)[None, :]
    M = np.zeros((128, ND * 128), np.float32)
    for d in range(-1, 18):
        dist = 128 * d + i - j
        m = np.zeros((128, 128), np.float32)
        for win, dil in ((128, 1), (512, 4), (2048, 16)):
            m += ((dist >= 0) & (dist <= win) & (dist % dil == 0)).astype(np.float32)
        M[:, (d + 1) * 128:(d + 2) * 128] = m
    return M.astype(ml_dtypes.bfloat16)


def phase_attn(nc, K, d, tag="at", ntile=NTILE):
    PE, ACT, DVE, POOL, SP, GQ = K.PE, K.ACT, K.DVE, K.POOL, K.SP, K.GQ
    bankB = K.bankB
    with ExitStack() as es:
        sb = lambda name, shape, dt: es.enter_context(nc.sbuf_tensor(f"sb_{tag}_{name}", shape, dt))
        xt = sb("xt", [128, 16, TQ], F32)
        ht = sb("ht", [128, 2, 16, TQ], BF16)
        wq = sb("wq", [128, 3, 16, 512], BF16)
        kT = sb("kT", [128, 4, 4096], BF16)
        va = sb("va", [128, 32, 4, 130], BF16)
        qT = sb("qT", [128, 4, TQ], BF16)
        mk = sb("mk", [128, ND * 128], BF16)
        pT = sb("pT", [128, 3, TQ], BF16)
        sqs = sb("sqs", [128, TQ], BF16)
        rt = sb("rt", [128, TQ], F32)
        rstd = sb("rstd", [128, TQ], F32)
        rq = sb("rq", [128, TQ], F32)
        on = sb("on", [128, 2, 128], BF16)
        rden = sb("rden", [128, 2], F32)
        oT = sb("oT", [128, 2, 4, TQ], BF16)
        gn = sb("gn", [128, 16], F32)
        qkg = sb("qkg", [128, 2], F32)
        ones = sb("ones", [128, 128], BF16)
        ident = sb("ident", [128, 128], BF16)
        tp = es.enter_context(nc.psum_tensor(f"{tag}_tp", [128, 1024], BF16))
        bank = K.banks

        xtB, htB, wB, kB, vB, qB, cB = Buf(), [Buf(), Buf()], Buf(), Buf(), Buf(), Buf(), Buf()
        pB, sqsB, rtB, rstdB, rqB, onB, rdB, oTB = [Buf(), Buf(), Buf()], Buf(), Buf(), Buf(), Buf(), [Buf(), Buf()], [Buf(), Buf()], [Buf(), Buf()]
        onesB, outB, tpB = Buf(), Buf(), K.bankB[7]
        ld_x, ld_c, ld_w = K.dsem(), K.dsem(), K.dsem()
        st_o = [K.dsem(), K.dsem()]
        PR, NB, SR, OB = [0, 1, 3, 4], 2, [3, 4], [5, 6]
        ring = {"p": 0, "s": 0, "pt": 0}

        toks = []
        for (t, src) in ((gn[:], d["gn"]), (qkg[:, 0:1], d["qg"]), (qkg[:, 1:2], d["kg"]), (mk[:], d["mask"]), (ident[:], d["ident"])):
            SP.dma(t, src, ld_c, writes=[cB])
        for i in range(3):
            GQ.dma(wq[:, i], d["wqkv"][:, i * 512:(i + 1) * 512].rearrange("(c p) n -> p c n", p=128), ld_w, writes=[wB])
        POOL.op(lambda h: h.memset(ones[:], 1.0), writes=[onesB])
        POOL.op(lambda h: h.memset(va[:, :, :, 128:130], 1.0), writes=[vB])
        if "opad" in d:
            POOL.op(lambda h: h.memset(on[:, 0, 0:16], 0.0), writes=[onB[0]])
            SP.dma(d["opad"].rearrange("(h p) t -> p h t", p=128), on[:, 0, 0:16].rearrange("p (h t) -> p h t", h=4), st_o[0], reads=[onB[0]], writes=[outB])
        DVE.op(lambda h: h.tensor_scalar(out=qkg[:, 0:1], in0=qkg[:, 0:1], scalar1=128.0 ** -0.5, scalar2=None, op0=ALU.mult), reads=[cB], writes=[cB])

        xv = d["xfull"].rearrange("(c p) t -> p c t", p=128)
        ov = d["oT"].rearrange("(h p) t -> p h t", p=128)

        def mm_group(bk, n, lhs_fn, rhs_fn, nch, reads):
            fns = [(lambda h, c=c: h.matmul(bank[bk][:, 0:n], lhs_fn(c), rhs_fn(c), start=(c == 0), stop=(c == nch - 1))) for c in range(nch)]
            return PE.group(fns, reads=reads, writes=[bankB[bk]])

        def nb(kind, lst):
            b = lst[ring[kind] % len(lst)]
            ring[kind] += 1
            return b

        def rsqrt_from_bank(bk, n, scale, dst, dstB):
            DVE.op(lambda h: h.tensor_scalar(out=rt[:, 0:n], in0=bank[bk][:, 0:n], scalar1=scale, scalar2=1e-6, op0=ALU.mult, op1=ALU.add),
                   reads=[bankB[bk]], writes=[rtB])
            ACT.op(lambda h: h.activation(out=rt[:, 0:n], in_=rt[:, 0:n], func=AF.Ln), reads=[rtB], writes=[rtB])
            ACT.op(lambda h: h.activation(out=dst[:, 0:n], in_=rt[:, 0:n], func=AF.Exp, scale=-0.5), reads=[rtB], writes=[dstB])

        raw = sb("raw", [128, 4, TQ], F32)
        sq2 = sb("sq2", [128, 2, TQ], BF16)
        oraw = sb("oraw", [128, 2, 2, 130], F32)
        rawB, sq2B, orawB = [Buf() for _ in range(4)], [Buf(), Buf()], [[Buf(), Buf()], [Buf(), Buf()]]
        sslots = [(0, 0), (1, 0), (3, 0), (4, 0)]
        sslotB = [bankB[0], bankB[1], bankB[3], bankB[4]]
        pT4 = sb("pT4", [128, 4, TQ], BF16)
        pB4 = [Buf() for _ in range(4)]

        def load_norm(t):
            T0 = t * TQ
            hs = t % 2
            SP.dma(xt[:], xv[:, :, T0:T0 + TQ], ld_x, writes=[xtB])
            if PN:
                POOL.op(lambda h: h.tensor_tensor(out=ht[:, hs], in0=xt[:], in1=xt[:], op=ALU.mult), reads=[xtB], writes=[htB[hs]])
            else:
                ACT.op(lambda h: h.activation(out=ht[:, hs], in_=xt[:], func=AF.Square), reads=[xtB], writes=[htB[hs]])
            mm_group(NB, TQ, lambda c: ones[:], lambda c: ht[:, hs, c, :], 16, [htB[hs], onesB])
            rsqrt_from_bank(NB, TQ, 1.0 / 2048.0, rstd, rstdB)
            for c in range(16):
                if PN:
                    POOL.op(lambda h: h.tensor_scalar(out=xt[:, c, :], in0=xt[:, c, :], scalar1=gn[:, c:c + 1], scalar2=1.0, op0=ALU.mult, op1=ALU.mult),
                            reads=[xtB, cB], writes=[xtB])
                    POOL.op(lambda h: h.tensor_tensor(out=ht[:, hs, c, :], in0=xt[:, c, :], in1=rstd[:], op=ALU.mult),
                            reads=[xtB, rstdB], writes=[htB[hs]])
                else:
                    DVE.op(lambda h: h.scalar_tensor_tensor(out=ht[:, hs, c, :], in0=xt[:, c, :], scalar=gn[:, c:c + 1], in1=rstd[:], op0=ALU.mult, op1=ALU.mult),
                           reads=[xtB, rstdB, cB], writes=[htB[hs]])

        def qk_norm_rest(f, t):
            T0 = t * TQ
            isk, hh = f // 4, f % 4
            rs, ss = f % 4, f % 2
            mm_group(NB, TQ, lambda c: ones[:], lambda c: sq2[:, ss, :], 1, [sq2B[ss], onesB])
            rsqrt_from_bank(NB, TQ, 1.0 / 128.0, rq, rqB)
            dst = kT[:, hh, T0:T0 + TQ] if isk else qT[:, hh, :]
            DVE.op(lambda h: h.scalar_tensor_tensor(out=dst, in0=raw[:, rs, :], scalar=qkg[:, isk:isk + 1], in1=rq[:], op0=ALU.mult, op1=ALU.mult),
                   reads=[rawB[rs], rqB, cB], writes=[kB if isk else qB])

        def proj(t):
            hs = t % 2
            for f in range(8):
                isk, hh = f // 4, f % 4
                rs, ss = f % 4, f % 2
                bk = nb("p", PR)
                mm_group(bk, TQ, lambda c: wq[:, isk, c, hh * 128:(hh + 1) * 128], lambda c: ht[:, hs, c, :], 16, [wB, htB[hs]])
                ACT.op(lambda h: h.activation(out=raw[:, rs, :], in_=bank[bk][:, 0:TQ], func=AF.Copy), reads=[bankB[bk]], writes=[rawB[rs]])
                if PS:
                    POOL.op(lambda h: h.tensor_tensor(out=sq2[:, ss, :], in0=raw[:, rs, :], in1=raw[:, rs, :], op=ALU.mult), reads=[rawB[rs]], writes=[sq2B[ss]])
                else:
                    ACT.op(lambda h: h.activation(out=sq2[:, ss, :], in_=raw[:, rs, :], func=AF.Square), reads=[rawB[rs]], writes=[sq2B[ss]])
                if f >= 1:
                    qk_norm_rest(f - 1, t)
            for a in range(TQ // 128):
                blk = t * (TQ // 128) + a
                bk = nb("p", PR)
                mm_group(bk, 512, lambda c: ht[:, hs, c, a * 128:(a + 1) * 128], lambda c: wq[:, 2, c, :], 16, [wB, htB[hs]])
                if a == 0:
                    qk_norm_rest(7, t)
                ACT.op(lambda h: h.activation(out=va[:, blk, :, 0:128], in_=bank[bk][:, 0:512].rearrange("p (h e) -> p h e", h=4), func=AF.Copy),
                       reads=[bankB[bk]], writes=[vB])

        def attention(t):
            T0 = t * TQ
            os_ = t % 2
            kbs = list(range(max(0, 2 * t - 16), 2 * t + 2))
            steps = [(hh, kb) for hh in range(4) for kb in kbs]
            info = {}
            pending = []

            def front(i):
                hh, kb = steps[i]
                D = 2 * t - kb
                si = ring["s"] % 4
                ring["s"] += 1
                sbk, sc0 = sslots[si]
                ps = ring["pt"] % 4
                ring["pt"] += 1
                info[i] = ps
                PE.group([lambda h: h.matmul(bank[sbk][:, sc0:sc0 + TQ], kT[:, hh, kb * 128:(kb + 1) * 128], qT[:, hh, :], start=True, stop=True)],
                         reads=[kB, qB], writes=[sslotB[si]])
                ACT.op(lambda h: h.activation(out=pT4[:, ps, :], in_=bank[sbk][:, sc0:sc0 + TQ], func=AF.Exp), reads=[sslotB[si]], writes=[pB4[ps]])
                DVE.op(lambda h: h.tensor_tensor(out=pT4[:, ps, :], in0=pT4[:, ps, :], in1=mk[:, (D + 1) * 128:(D + 3) * 128], op=ALU.mult),
                       reads=[pB4[ps], cB], writes=[pB4[ps]])

            def epi2(hh, par):
                for a in range(2):
                    DVE.op(lambda h: h.reciprocal(out=rden[:, a:a + 1], in_=oraw[:, par, a, 128:129]), reads=[orawB[par][a]], writes=[rdB[a]])
                    ACT.op(lambda h: h.activation(out=on[:, a, :], in_=oraw[:, par, a, 0:128], func=AF.Copy, scale=rden[:, a:a + 1]),
                           reads=[orawB[par][a], rdB[a]], writes=[onB[a]])
                    PE.group([lambda h: h.transpose(tp[:, a * 128:(a + 1) * 128], on[:, a, :], ident[:])], reads=[onB[a], cB], writes=[tpB])
                DVE.op(lambda h: h.tensor_copy(out=oT[:, os_, hh, :], in_=tp[:, 0:256]), reads=[tpB], writes=[oTB[os_]])

            def back(i):
                hh, kb = steps[i]
                D = 2 * t - kb
                ps = info.pop(i)
                for a in range(2):
                    dd = D + a
                    if dd < 0 or dd > 16:
                        continue
                    first = (kb == kbs[0]) or (dd == 16)
                    lastk = (dd == 0)
                    PE.group([lambda h: h.matmul(bank[OB[a]][:, 0:129], pT4[:, ps, a * 128:(a + 1) * 128], va[:, kb, hh, 0:129], start=first, stop=lastk)],
                             reads=[pB4[ps], vB], writes=[bankB[OB[a]]])
                if kb == kbs[-1]:
                    par = hh % 2
                    for a in range(2):
                        DVE.op(lambda h: h.tensor_copy(out=oraw[:, par, a, 0:129], in_=bank[OB[a]][:, 0:129]), reads=[bankB[OB[a]]], writes=[orawB[par][a]])
                    pending.append((i + min(4, 2 * len(kbs) - 1), hh, par))

            LA = 3
            n = len(steps)
            for i in range(min(LA, n)):
                front(i)
            for i in range(n):
                if i + LA < n:
                    front(i + LA)
                back(i)
                while pending and pending[0][0] <= i:
                    _, hh_, par_ = pending.pop(0)
                    epi2(hh_, par_)
            while pending:
                _, hh_, par_ = pending.pop(0)
                epi2(hh_, par_)
            SP.dma(ov[:, :, T0:T0 + TQ], oT[:, os_], st_o[os_], reads=[oTB[os_]], writes=[outB])

        load_norm(0)
        for t in range(ntile):
            proj(t)
            if t + 1 < ntile:
                load_norm(t + 1)
            attention(t)
        SP.wait([Tok(s_.sem, s_.count) for s_ in st_o if s_.count > 0])
        toks = [Tok(e.sem, e.count) for e in (PE, ACT, DVE, POOL) if e.count > 0]
        toks += [Tok(s.sem, s.count) for s in [ld_x, ld_c, ld_w] + st_o if s.count > 0]
        for e in (PE, ACT, DVE, POOL, SP, GQ):
            e.wait(toks)


TQ = 256
NTILE = 4096 // TQ


def lstm_consts_np():
    s = np.arange(128)[:, None]
    l = np.arange(128)[None, :]
    tri = ((s // 64 == l // 64) & (s <= l)).astype(np.float32)
    sel = np.zeros((128, 2), np.float32)
    sel[:64, 0] = 1.0
    sel[64:, 1] = 1.0
    return tri, sel


def phase_lstm(nc, K, d, tag="ls", ntile=NTILE):
    PE, ACT, DVE, POOL, SP, GQ = K.PE, K.ACT, K.DVE, K.POOL, K.SP, K.GQ
    bankB = K.bankB
    with ExitStack() as es:
        sb = lambda name, shape, dt: es.enter_context(nc.sbuf_tensor(f"sb_{tag}_{name}", shape, dt))
        xt = sb("xt", [128, 16, TQ], F32)
        ht = sb("ht", [128, 2, 16, TQ], BF16)
        win = sb("win", [128, 3, 16, 512], BF16)
        wg = sb("wg", [128, 16, 2], BF16)
        zq = sb("zq", [128, 4, TQ + 3], F32)
        acc = sb("acc", [128, 4, TQ], F32)
        qk = sb("qk", [128, 2, 4, TQ], BF16)
        ktm = sb("ktm", [128, 2, 2, 256], BF16)
        vp = sb("vp", [128, 4, 512], BF16)
        osg = sb("osg", [128, 4, 512], BF16)
        hb = sb("hb", [128, 512], F32)
        hjunk = sb("hjunk", [128, 512], BF16)
        htmp = sb("htmp", [128, 512], F32)
        hstm = sb("hstm", [128, 512], BF16)
        hsT = sb("hsT", [128, 2, 4, TQ], BF16)
        Sf = sb("Sf", [128, 2, 512], F32)
        Sb = sb("Sb", [128, 2, 512], BF16)
        nf = sb("nf", [128, 2], F32)
        nbf = sb("nbf", [128, 2], BF16)
        qts = sb("qts", [128, 2, 2, 64], BF16)
        WT = sb("WT", [128, 4, 128], BF16)
        gsb = sb("gsb", [128, 4, 2], F32)
        t1 = sb("t1", [128, 4, 1], F32)
        lf = sb("lf", [128, 4, 1], F32)
        lfm = sb("lfm", [128, 4, 2], F32)
        el = sb("el", [128, 4, 1], F32)
        amb = sb("amb", [128, 4, 1], F32)
        af = sb("af", [128, 4, 1], F32)
        abf = sb("abf", [128, 4, 1], BF16)
        dvec = sb("dvec", [128, 4, 2], F32)
        sm = sb("sm", [128, 8], F32)
        rt = sb("rt", [128, TQ], F32)
        rstd = sb("rstd", [128, TQ], F32)
        gn = sb("gn", [128, 16], F32)
        cw = sb("cw", [128, 4, 4], F32)
        cb = sb("cb", [128, 4], F32)
        hg = sb("hg", [128, 512], F32)
        gbias = sb("gbias", [128, 2], F32)
        tri = sb("tri", [128, 128], F32)
        onesf = sb("onesf", [128, 128], F32)
        sel = sb("sel", [128, 2], F32)
        ones = sb("ones", [128, 128], BF16)
        ident = sb("ident", [128, 128], BF16)
        tp = es.enter_context(nc.psum_tensor(f"{tag}_tp", [128, 1024], BF16))
        bank = K.banks

        B = lambda: Buf()
        xtB, htB, wB, cB, onesB = B(), [B(), B()], B(), B(), B()
        zqB, accB, qkB, ktmB = [B() for _ in range(4)], [B() for _ in range(4)], [B(), B()], [B(), B()]
        vpB, osgB, hbB, hjB, htmpB, hstmB, hsTB = [B() for _ in range(4)], [B() for _ in range(4)], B(), B(), B(), B(), [B(), B()]
        SfB, SbB, nfB, nbfB, qtsB, WTB = B(), B(), B(), B(), [B(), B()], [B() for _ in range(4)]
        gB, smB, rtB, rstdB, outB = [B() for _ in range(4)], B(), B(), B(), B()
        tpB = K.bankB[7]
        SMB = 2
        gateR = cumR = denR = dnR = sTR = K.bankB[2]
        PR, NUMB, DC = [0, 1], [3, 4], [5, 6]
        ld_x, ld_c, ld_w = K.dsem(), K.dsem(), K.dsem()
        st_o = [K.dsem(), K.dsem()]
        ring = {"p": 0}

        for (t, src) in ((gn[:], d["gn"]), (cw[:], d["cw"]), (cb[:], d["cb"]), (hg[:], d["hg"]), (gbias[:], d["gbias"]),
                         (tri[:], d["tri"]), (sel[:], d["sel"]), (ident[:], d["ident"])):
            SP.dma(t, src, ld_c, writes=[cB])
        for i in range(3):
            GQ.dma(win[:, i], d["win"][:, i * 512:(i + 1) * 512].rearrange("(c p) n -> p c n", p=128), ld_w, writes=[wB])
        GQ.dma(wg[:], d["wg"].rearrange("(c p) n -> p c n", p=128), ld_w, writes=[wB])
        POOL.op(lambda h: h.memset(ones[:], 1.0), writes=[onesB])
        POOL.op(lambda h: h.memset(onesf[:], 1.0), writes=[onesB])
        POOL.op(lambda h: h.memset(zq[:], 0.0), writes=zqB)
        POOL.op(lambda h: h.memset(Sf[:], 0.0), writes=[SfB])
        POOL.op(lambda h: h.memset(nf[:], 0.0), writes=[nfB])

        if "xtile" in d:
            xtile = d["xtile"]
        else:
            xv = d["xfull"].rearrange("(c p) t -> p c t", p=128)
            xtile = lambda t: xv[:, :, t * TQ:(t + 1) * TQ]
        if "opad" in d:
            POOL.op(lambda h: h.memset(hstm[:, 0:16], 0.0), writes=[hstmB])
            SP.dma(d["opad"].rearrange("(h p) t -> p h t", p=128), hstm[:, 0:16].rearrange("p (h t) -> p h t", h=4), st_o[0], reads=[hstmB], writes=[outB])
        ov = d["hsT"].rearrange("(h p) t -> p h t", p=128)

        def mm_group(out_ap, lhs_fn, rhs_fn, nch, reads, writes):
            fns = [(lambda h, c=c: h.matmul(out_ap, lhs_fn(c), rhs_fn(c), start=(c == 0), stop=(c == nch - 1))) for c in range(nch)]
            return PE.group(fns, reads=reads, writes=writes)

        def nb(kind, lst):
            b = lst[ring[kind] % len(lst)]
            ring[kind] += 1
            return b

        def load_norm(t):
            T0 = t * TQ
            hs = t % 2
            SP.dma(xt[:], xtile(t), ld_x, writes=[xtB], extra=d.get("x_dep", ()))
            POOL.op(lambda h: h.tensor_tensor(out=ht[:, hs], in0=xt[:], in1=xt[:], op=ALU.mult), reads=[xtB], writes=[htB[hs]])
            mm_group(bank[PR[0]][:, 0:TQ], lambda c: ones[:], lambda c: ht[:, hs, c, :], 16, [htB[hs], onesB], [bankB[PR[0]]])
            DVE.op(lambda h: h.tensor_scalar(out=rt[:], in0=bank[PR[0]][:, 0:TQ], scalar1=1.0 / 2048.0, scalar2=1e-6, op0=ALU.mult, op1=ALU.add),
                   reads=[bankB[PR[0]]], writes=[rtB])
            ACT.op(lambda h: h.activation(out=rt[:], in_=rt[:], func=AF.Ln), reads=[rtB], writes=[rtB])
            ACT.op(lambda h: h.activation(out=rstd[:], in_=rt[:], func=AF.Exp, scale=-0.5), reads=[rtB], writes=[rstdB])
            for c in range(16):
                POOL.op(lambda h: h.tensor_scalar(out=xt[:, c, :], in0=xt[:, c, :], scalar1=gn[:, c:c + 1], scalar2=1.0, op0=ALU.mult, op1=ALU.mult),
                        reads=[xtB, cB], writes=[xtB])
                POOL.op(lambda h: h.tensor_tensor(out=ht[:, hs, c, :], in0=xt[:, c, :], in1=rstd[:], op=ALU.mult),
                        reads=[xtB, rstdB], writes=[htB[hs]])

        def proj_qk_f(t, f):
            hs = t % 2
            qs = t % 2
            if True:
                bk = nb("p", PR)
                mm_group(bank[bk][:, 0:TQ], lambda c: win[:, 0, c, f * 128:(f + 1) * 128], lambda c: ht[:, hs, c, :], 16, [wB, htB[hs]], [bankB[bk]])
                ACT.op(lambda h: h.activation(out=zq[:, f, 3:3 + TQ], in_=bank[bk][:, 0:TQ], func=AF.Copy), reads=[bankB[bk]], writes=[zqB[f]])
                DVE.op(lambda h: h.tensor_scalar(out=acc[:, f, :], in0=zq[:, f, 3:3 + TQ], scalar1=cw[:, f, 3:4], scalar2=cb[:, f:f + 1], op0=ALU.mult, op1=ALU.add),
                       reads=[zqB[f], cB], writes=[accB[f]])
                for j in (1, 2, 3):
                    DVE.op(lambda h: h.scalar_tensor_tensor(out=acc[:, f, :], in0=zq[:, f, 3 - j:3 - j + TQ], scalar=cw[:, f, 3 - j:4 - j], in1=acc[:, f, :],
                                                            op0=ALU.mult, op1=ALU.add), reads=[zqB[f], accB[f], cB], writes=[accB[f]])
                DVE.op(lambda h: h.tensor_copy(out=zq[:, f, 0:3], in_=zq[:, f, TQ:TQ + 3]), reads=[zqB[f]], writes=[zqB[f]])
                ACT.op(lambda h: h.activation(out=acc[:, f, :], in_=acc[:, f, :], func=AF.Silu), reads=[accB[f]], writes=[accB[f]])
                DVE.op(lambda h: h.tensor_scalar(out=qk[:, qs, f, :], in0=acc[:, f, :], scalar1=(1.0 if f < 2 else 0.0625), scalar2=None, op0=ALU.mult),
                       reads=[accB[f]], writes=[qkB[qs]])
        def ktrans(t):
            qs = t % 2
            for blk in range(2):
                for dkc in range(2):
                    col = (blk * 2 + dkc) * 128
                    PE.group([lambda h: h.transpose(tp[:, col:col + 128], qk[:, qs, 2 + dkc, blk * 128:(blk + 1) * 128], ident[:])],
                             reads=[qkB[qs], cB], writes=[tpB])
            DVE.op(lambda h: h.tensor_copy(out=ktm[:, qs].rearrange("p b k -> p (b k)"), in_=tp[:, 0:512]), reads=[tpB], writes=[ktmB[qs]])

        def oproj(t, blk):
            hs = t % 2
            bs = (t * 2 + blk) % 4
            tok = slice(blk * 128, (blk + 1) * 128)
            bk = nb("p", PR)
            mm_group(bank[bk][:, 0:512], lambda c: ht[:, hs, c, tok], lambda c: win[:, 2, c, :], 16, [wB, htB[hs]], [bankB[bk]])
            ACT.op(lambda h: h.activation(out=osg[:, bs, :], in_=bank[bk][:, 0:512], func=AF.Sigmoid), reads=[bankB[bk]], writes=[osgB[bs]])

        def gates(t, blk):
            hs = t % 2
            qs = t % 2
            bs = (t * 2 + blk) % 4
            tok = slice(blk * 128, (blk + 1) * 128)
            mm_group(bank[SMB][:, 0:2], lambda c: ht[:, hs, c, tok], lambda c: wg[:, c, :], 16, [wB, htB[hs]], [gateR])
            DVE.op(lambda h: h.tensor_tensor(out=gsb[:, bs, :], in0=bank[SMB][:, 0:2], in1=gbias[:], op=ALU.add), reads=[gateR, cB], writes=[gB[bs]])
            ACT.op(lambda h: h.activation(out=t1[:, bs, :], in_=gsb[:, bs, 1:2], func=AF.Exp, scale=-1.0), reads=[gB[bs]], writes=[gB[bs]])
            DVE.op(lambda h: h.tensor_scalar(out=t1[:, bs, :], in0=t1[:, bs, :], scalar1=1.0, scalar2=None, op0=ALU.add), reads=[gB[bs]], writes=[gB[bs]])
            ACT.op(lambda h: h.activation(out=t1[:, bs, :], in_=t1[:, bs, :], func=AF.Ln), reads=[gB[bs]], writes=[gB[bs]])
            DVE.op(lambda h: h.tensor_scalar(out=lf[:, bs, :], in0=t1[:, bs, :], scalar1=-1.0, scalar2=None, op0=ALU.mult), reads=[gB[bs]], writes=[gB[bs]])
            DVE.op(lambda h: h.tensor_scalar(out=lfm[:, bs, :], in0=sel[:], scalar1=lf[:, bs, 0:1], scalar2=None, op0=ALU.mult), reads=[gB[bs], cB], writes=[gB[bs]])
            PE.group([lambda h: h.matmul(bank[SMB][:, 4:5], tri[:], lf[:, bs, :], start=True, stop=True),
                      lambda h: h.matmul(bank[SMB][:, 6:8], onesf[:], lfm[:, bs, :], start=True, stop=True)],
                     reads=[gB[bs], cB, onesB], writes=[cumR])
            ACT.op(lambda h: h.activation(out=el[:, bs, :], in_=bank[SMB][:, 4:5], func=AF.Exp), reads=[cumR], writes=[gB[bs]])
            DVE.op(lambda h: h.tensor_tensor(out=amb[:, bs, :], in0=gsb[:, bs, 0:1], in1=bank[SMB][:, 4:5], op=ALU.subtract), reads=[cumR, gB[bs]], writes=[gB[bs]])
            ACT.op(lambda h: h.activation(out=af[:, bs, :], in_=amb[:, bs, :], func=AF.Exp), reads=[gB[bs]], writes=[gB[bs]])
            DVE.op(lambda h: h.tensor_copy(out=abf[:, bs, :], in_=af[:, bs, :]), reads=[gB[bs]], writes=[gB[bs]])
            ACT.op(lambda h: h.activation(out=dvec[:, bs, :], in_=bank[SMB][:, 6:8], func=AF.Exp), reads=[cumR], writes=[gB[bs]])

        def vproj_st(t, blk):
            hs = t % 2
            qs = t % 2
            bs = (t * 2 + blk) % 4
            tok = slice(blk * 128, (blk + 1) * 128)
            bk = nb("p", PR)
            mm_group(bank[bk][:, 0:512], lambda c: ht[:, hs, c, tok], lambda c: win[:, 1, c, :], 16, [wB, htB[hs]], [bankB[bk]])
            ACT.op(lambda h: h.activation(out=vp[:, bs, :], in_=bank[bk][:, 0:512], func=AF.Copy, scale=af[:, bs, :]), reads=[bankB[bk], gB[bs]], writes=[vpB[bs]])
            mm_group(bank[SMB][:, 128:256], lambda c: qk[:, qs, 2 + c, tok], lambda c: qk[:, qs, c, tok], 2, [qkB[qs]], [sTR])
            DVE.op(lambda h: h.tensor_tensor(out=WT[:, bs, :], in0=bank[SMB][:, 128:256], in1=tri[:], op=ALU.mult), reads=[sTR, cB], writes=[WTB[bs]])

        def chunk(t, blk, half):
            qs = t % 2
            bs = (t * 2 + blk) % 4
            pbs = (bs - 1) % 4
            nbk = NUMB[(t * 2 + blk) % 2]
            if True:
                c_idx = (t * 2 + blk) * 2 + half
                r0 = half * 64
                rows = slice(r0, r0 + 64)
                ctok = slice(blk * 128 + r0, blk * 128 + r0 + 64)
                dprev = dvec[:, bs, 0:1] if (half == 1 or c_idx == 0) else dvec[:, pbs, 1:2]
                if c_idx > 0:
                    ACT.op(lambda h: h.activation(out=qts[:, half], in_=qk[:, qs, 0:2, ctok], func=AF.Copy, scale=dprev),
                           reads=[qkB[qs], gB[bs], gB[pbs]], writes=[qtsB[half]])
                fns = []
                if c_idx > 0:
                    for dkc in range(2):
                        fns.append(lambda h, dkc=dkc: h.matmul(bank[nbk][rows, 0:512], qts[:, half, dkc, :], Sb[:, dkc, :], start=(dkc == 0), stop=False))
                fns.append(lambda h: h.matmul(bank[nbk][rows, 0:512], WT[rows, bs, r0:r0 + 64], vp[rows, bs, :], start=(c_idx == 0), stop=True))
                PE.group(fns, reads=[qtsB[half], SbB, WTB[bs], vpB[bs]], writes=[bankB[nbk]])
                fns = []
                if c_idx > 0:
                    for dkc in range(2):
                        fns.append(lambda h, dkc=dkc: h.matmul(bank[SMB][rows, 8:9], qts[:, half, dkc, :], nbf[:, dkc:dkc + 1], start=(dkc == 0), stop=False))
                fns.append(lambda h: h.matmul(bank[SMB][rows, 8:9], WT[rows, bs, r0:r0 + 64], abf[rows, bs, :], start=(c_idx == 0), stop=True))
                PE.group(fns, reads=[qtsB[half], nbfB, WTB[bs], gB[bs]], writes=[denR])
                for dkc in range(2):
                    PE.group([lambda h: h.matmul(bank[DC[dkc]][:, 0:512], ktm[rows, qs, blk, dkc * 128:(dkc + 1) * 128], vp[rows, bs, :], start=True, stop=True)],
                             reads=[ktmB[qs], vpB[bs]], writes=[bankB[DC[dkc]]])
                PE.group([(lambda h, dkc=dkc: h.matmul(bank[SMB][:, 12 + dkc:13 + dkc], ktm[rows, qs, blk, dkc * 128:(dkc + 1) * 128], abf[rows, bs, :], start=True, stop=True))
                          for dkc in range(2)], reads=[ktmB[qs], gB[bs]], writes=[dnR])
                dc = dprev
                for dkc in range(2):
                    DVE.op(lambda h: h.scalar_tensor_tensor(out=Sf[:, dkc, :], in0=Sf[:, dkc, :], scalar=dc, in1=bank[DC[dkc]][:, 0:512], op0=ALU.mult, op1=ALU.add),
                           reads=[bankB[DC[dkc]], gB[bs], gB[pbs], SfB], writes=[SfB])
                DVE.op(lambda h: h.scalar_tensor_tensor(out=nf[:], in0=nf[:], scalar=dc, in1=bank[SMB][:, 12:14], op0=ALU.mult, op1=ALU.add),
                       reads=[dnR, gB[bs], gB[pbs], nfB], writes=[nfB])
                ACT.op(lambda h: h.activation(out=Sb[:], in_=Sf[:], func=AF.Copy), reads=[SfB], writes=[SbB])
                DVE.op(lambda h: h.tensor_copy(out=nbf[:], in_=nf[:]), reads=[nfB], writes=[nbfB])

        def outstage(t, blk):
            bs = (t * 2 + blk) % 4
            nbk = NUMB[(t * 2 + blk) % 2]
            tok = slice(blk * 128, (blk + 1) * 128)
            DVE.op(lambda h: h.tensor_scalar(out=sm[:, 0:1], in0=bank[SMB][:, 8:9], scalar1=el[:, bs, :], scalar2=None, op0=ALU.mult),
                   reads=[denR, gB[bs]], writes=[smB])
            DVE.op(lambda h: h.tensor_scalar(out=sm[:, 6:7], in0=sm[:, 0:1], scalar1=-1.0, scalar2=1.0, op0=ALU.mult, op1=ALU.max), reads=[smB], writes=[smB])
            DVE.op(lambda h: h.tensor_tensor(out=sm[:, 0:1], in0=sm[:, 0:1], in1=sm[:, 6:7], op=ALU.max), reads=[smB], writes=[smB])
            DVE.op(lambda h: h.reciprocal(out=sm[:, 1:2], in_=sm[:, 0:1]), reads=[smB], writes=[smB])
            DVE.op(lambda h: h.tensor_tensor(out=sm[:, 2:3], in0=sm[:, 1:2], in1=el[:, bs, :], op=ALU.mult), reads=[smB, gB[bs]], writes=[smB])
            ACT.op(lambda h: h.activation(out=hb[:], in_=bank[nbk][:, 0:512], func=AF.Copy, scale=sm[:, 2:3]), reads=[bankB[nbk], smB], writes=[hbB])
            ACT.op(lambda h: h.activation(out=hjunk[:], in_=hb[:], func=AF.Square, accum_out=sm[:, 3:4]), reads=[hbB], writes=[hjB, smB])
            DVE.op(lambda h: h.tensor_scalar(out=sm[:, 4:5], in0=sm[:, 3:4], scalar1=1.0 / 512.0, scalar2=1e-6, op0=ALU.mult, op1=ALU.add), reads=[smB], writes=[smB])
            ACT.op(lambda h: h.activation(out=sm[:, 4:5], in_=sm[:, 4:5], func=AF.Ln), reads=[smB], writes=[smB])
            ACT.op(lambda h: h.activation(out=sm[:, 5:6], in_=sm[:, 4:5], func=AF.Exp, scale=-0.5), reads=[smB], writes=[smB])
            DVE.op(lambda h: h.scalar_tensor_tensor(out=htmp[:], in0=hb[:], scalar=sm[:, 5:6], in1=hg[:], op0=ALU.mult, op1=ALU.mult),
                   reads=[hbB, smB, cB], writes=[htmpB])
            DVE.op(lambda h: h.tensor_tensor(out=hstm[:], in0=htmp[:], in1=osg[:, bs, :], op=ALU.mult), reads=[htmpB, osgB[bs]], writes=[hstmB])
            os_ = t % 2
            for j in range(4):
                PE.group([lambda h: h.transpose(tp[:, 512 + j * 128:512 + (j + 1) * 128], hstm[:, j * 128:(j + 1) * 128], ident[:])],
                         reads=[hstmB, cB], writes=[tpB])
            DVE.op(lambda h: h.tensor_copy(out=hsT[:, os_, :, tok], in_=tp[:, 512:1024].rearrange("p (j t) -> p j t", j=4)), reads=[tpB], writes=[hsTB[os_]])

        def bulk_steps(t):
            st = []
            if t + 1 < ntile:
                st.append(lambda: load_norm(t + 1))
            for f in range(4):
                st.append(lambda f=f: proj_qk_f(t, f))
            st.append(lambda: ktrans(t))
            st.append(lambda: oproj(t, 0))
            st.append(lambda: oproj(t, 1))
            for blk in range(2):
                st.append(lambda blk=blk: gates(t, blk))
                st.append(lambda blk=blk: vproj_st(t, blk))
            return st

        def chain_steps(t):
            st = []
            for blk in range(2):
                st.append(lambda blk=blk: chunk(t, blk, 0))
                st.append(lambda blk=blk: chunk(t, blk, 1))
                st.append(lambda blk=blk: outstage(t, blk))
            return st

        load_norm(0)
        for f_ in bulk_steps(0):
            f_()
        for t in range(ntile):
            bs_ = bulk_steps(t + 1) if t + 1 < ntile else []
            for cs_ in chain_steps(t):
                cs_()
                for _ in range(2):
                    if bs_:
                        bs_.pop(0)()
            while bs_:
                bs_.pop(0)()
            SP.dma(ov[:, :, t * TQ:(t + 1) * TQ], hsT[:, t % 2], st_o[t % 2], reads=[hsTB[t % 2]], writes=[outB])
        SP.wait([Tok(s_.sem, s_.count) for s_ in st_o if s_.count > 0])
        toks = [Tok(e.sem, e.count) for e in (PE, ACT, DVE, POOL) if e.count > 0]
        toks += [Tok(s.sem, s.count) for s in [ld_x, ld_c, ld_w] + st_o if s.count > 0]
        for e in (PE, ACT, DVE, POOL, SP, GQ):
            e.wait(toks)


def _c(a):
    return np.ascontiguousarray(a)


def _build_attn():
    nc = bass.Bass("TRN2", target_bir_lowering=False)
    d = {}

    def inp(name, shape, dt=F32):
        d[name] = nc.dram_tensor(name, shape, dt, kind="ExternalInput").ap()
    inp("xfull", [2048, 4096]); inp("wqkv", [2048, 1536]); inp("gn", [128, 16]); inp("qg", [128, 1]); inp("kg", [128, 1])
    inp("mask", [128, ND * 128], BF16); inp("ident", [128, 128], BF16)
    d["oT"] = nc.dram_tensor("oT", [512, 4096], BF16, kind="ExternalOutput").ap()
    with ExitStack() as es:
        K = Kit(nc, es)
        phase_attn(nc, K, d)
    return nc


def _build_lstm():
    nc = bass.Bass("TRN2", target_bir_lowering=False)
    d = {}

    def inp(name, shape, dt=F32):
        d[name] = nc.dram_tensor(name, shape, dt, kind="ExternalInput").ap()
    inp("xfull", [2048, 4096]); inp("win", [2048, 1536]); inp("wg", [2048, 2]); inp("gn", [128, 16])
    inp("cw", [128, 4, 4]); inp("cb", [128, 4]); inp("hg", [128, 512]); inp("gbias", [128, 2])
    inp("tri", [128, 128]); inp("sel", [128, 2]); inp("ident", [128, 128], BF16)
    d["hsT"] = nc.dram_tensor("hsT", [512, 4096], BF16, kind="ExternalOutput").ap()
    with ExitStack() as es:
        K = Kit(nc, es)
        phase_lstm(nc, K, d)
    return nc


def _build_of(last):
    nc = bass.Bass("TRN2", target_bir_lowering=False)
    d = {}

    def inp(name, shape, dt=F32):
        d[name] = nc.dram_tensor(name, shape, dt, kind="ExternalInput").ap()
    inp("xin", [2048, NT]); inp("m", [2048, NT], BF16); inp("wo", [2048, 2048]); inp("wup", [2048, 2 * FFN]); inp("wdn", [FFN, 2048])
    inp("cw", [128, 88, 3]); inp("cb", [128, 88]); inp("gn", [128, 16]); inp("hv", [128, 1])
    d["xout"] = nc.dram_tensor("xout", [2048, NT], F32, kind="ExternalOutput").ap()
    with ExitStack() as es:
        K = Kit(nc, es)
        phase_of(nc, K, d, True, last)
    return nc


def _tok_shard(fullT, c):
    s0 = c * 1024
    out = np.zeros((fullT.shape[0], NT), fullT.dtype)
    lo = max(0, s0 - 4)
    out[:, 4 - (s0 - lo):] = fullT[:, lo:s0 + 1024]
    return out


def _of_inputs(z, L, xT_full, mT_full, wo):
    wup = _c(z["ffn_w_up"][L]); wdn = _c(z["ffn_w_down"][L])
    cw = _c(z["ffn_conv_w"][L].reshape(3, 88, 128).transpose(2, 1, 0))
    cb = _c(z["ffn_conv_b"][L].reshape(88, 128).T)
    gn = _c(z["ffn_norm"][L].reshape(16, 128).T)
    maps = []
    for core in range(8):
        b, c = core // 4, core % 4
        maps.append({"xin": _tok_shard(xT_full[b], c), "m": _tok_shard(mT_full[b], c), "wo": wo, "wup": wup, "wdn": wdn,
                     "cw": cw, "cb": cb, "gn": gn, "hv": np.full((128, 1), 0.0 if c == 0 else 1.0, np.float32)})
    return maps


def _lstm_inputs(z, xT_b, hd):
    w = z["lstm_w_in"][0]
    win = np.concatenate([w[:, hd * 256:(hd + 1) * 256], w[:, 1024 + hd * 256:1024 + (hd + 1) * 256],
                          w[:, 2048 + hd * 512:2048 + (hd + 1) * 512], w[:, 4096 + hd * 512:4096 + (hd + 1) * 512]], axis=1)
    wg = np.stack([w[:, 6144 + hd], w[:, 6148 + hd]], axis=1)
    cwf = z["lstm_conv_w"][0]; cbf = z["lstm_conv_b"][0]
    chans = np.concatenate([np.arange(hd * 256, (hd + 1) * 256), np.arange(1024 + hd * 256, 1024 + (hd + 1) * 256)])
    cw = cwf[:, chans].reshape(4, 4, 128).transpose(2, 1, 0)
    cb = cbf[chans].reshape(4, 128).T
    hg = np.broadcast_to(z["lstm_head_gain"][0][hd * 512:(hd + 1) * 512][None, :], (128, 512))
    gb = z["lstm_gate_bias"][0]
    gbias = np.broadcast_to(np.stack([gb[hd], gb[4 + hd]])[None, :], (128, 2))
    tri, sel = lstm_consts_np()
    return {"xfull": _c(xT_b), "win": _c(win), "wg": _c(wg), "gn": _c(z["lstm_norm"][0].reshape(16, 128).T), "cw": _c(cw), "cb": _c(cb),
            "hg": _c(hg), "gbias": _c(gbias), "tri": tri, "sel": sel, "ident": np.eye(128, dtype=np.float32).astype(ml_dtypes.bfloat16)}


def kernel(**inputs):
    z = {k: np.asarray(v, dtype=np.float32) for k, v in inputs.items()}
    x = z["x"]
    cores = list(range(8))
    xT = [_c(x[b].T) for b in range(2)]
    wqkv = z["attn_w_qkv"][0]
    gn = _c(z["attn_norm"][0].reshape(16, 128).T)
    qg = _c(z["attn_q_gain"][0].reshape(128, 1)); kg = _c(z["attn_k_gain"][0].reshape(128, 1))
    mask = attn_mask_np(); ident = np.eye(128, dtype=np.float32).astype(ml_dtypes.bfloat16)
    maps = []
    for core in cores:
        b, g = core // 4, core % 4
        w = np.concatenate([wqkv[:, i * 2048 + g * 512: i * 2048 + (g + 1) * 512] for i in range(3)], axis=1)
        maps.append({"xfull": xT[b], "wqkv": _c(w), "gn": gn, "qg": qg, "kg": kg, "mask": mask, "ident": ident})
    r1 = run_bass_kernel_spmd(_build_attn(), maps, core_ids=cores)
    oT = [np.concatenate([r1.results[b * 4 + g]["oT"] for g in range(4)], axis=0) for b in range(2)]
    r2 = run_bass_kernel_spmd(_build_of(False), _of_inputs(z, 0, xT, oT, _c(z["attn_w_o"][0])), core_ids=cores)
    x2T = [np.concatenate([r2.results[b * 4 + c]["xout"][:, 4:] for c in range(4)], axis=1) for b in range(2)]
    r3 = run_bass_kernel_spmd(_build_lstm(), [_lstm_inputs(z, x2T[core // 4], core % 4) for core in cores], core_ids=cores)
    hsT = [np.concatenate([r3.results[b * 4 + g]["hsT"] for g in range(4)], axis=0) for b in range(2)]
    r4 = run_bass_kernel_spmd(_build_of(True), _of_inputs(z, 1, x2T, hsT, _c(z["lstm_w_out"][0])), core_ids=cores)
    out = np.empty((2, 4096, 2048), np.float32)
    for core in cores:
        b, c = core // 4, core % 4
        out[b, c * 1024:(c + 1) * 1024, :] = r4.results[core]["xout"][:, 4:].T
    return out
```

```python
from contextlib import ExitStack
import os


import numpy as np
import ml_dtypes
import concourse.bass as bass
import concourse.mybir as mybir
from concourse.bass_utils import run_bass_kernel_spmd

F32 = mybir.dt.float32
BF16 = mybir.dt.bfloat16
AF = mybir.ActivationFunctionType
ALU = mybir.AluOpType
AX = mybir.AxisListType


class Tok:
    __slots__ = ("sem", "val")

    def __init__(self, sem, val):
        self.sem, self.val = sem, val


class Buf:
    def __init__(self, name=""):
        self.name = name
        self.w = {}
        self.r = {}


def _merge(d, tok):
    k = id(tok.sem)
    if k not in d or d[k].val < tok.val:
        d[k] = tok


class Eng:
    def __init__(self, h, sem, skip_self=False):
        self.h, self.sem, self.count, self.waited = h, sem, 0, {}
        self.skip_self = skip_self

    def wait(self, deps):
        for d in deps:
            if d is None:
                continue
            k = id(d.sem)
            if self.waited.get(k, 0) >= d.val:
                continue
            if getattr(self, "skip_self", False) and d.sem is self.sem:
                continue
            self.h.wait_ge(d.sem, d.val)
            self.waited[k] = d.val

    @staticmethod
    def _deps(reads, writes):
        deps = []
        for b in reads:
            deps.extend(b.w.values())
        for b in writes:
            deps.extend(b.w.values())
            deps.extend(b.r.values())
        return deps

    @staticmethod
    def _commit(tok, reads, writes):
        for b in reads:
            _merge(b.r, tok)
        for b in writes:
            _merge(b.w, tok)

    def op(self, fn, reads=(), writes=(), extra=()):
        self.wait(self._deps(reads, writes))
        self.wait(extra)
        inst = fn(self.h)
        self.count += 1
        inst.then_inc(self.sem, 1)
        tok = Tok(self.sem, self.count)
        self._commit(tok, reads, writes)
        return tok

    def group(self, fns, reads=(), writes=(), extra=()):
        self.wait(self._deps(reads, writes))
        self.wait(extra)
        inst = None
        for fn in fns:
            inst = fn(self.h)
        self.count += 1
        inst.then_inc(self.sem, 1)
        tok = Tok(self.sem, self.count)
        self._commit(tok, reads, writes)
        return tok


class DmaSem:
    def __init__(self, sem):
        self.sem, self.count = sem, 0


class DmaQ:
    def __init__(self, h):
        self.h, self.waited = h, {}

    wait = Eng.wait

    def dma(self, out, in_, dsem, reads=(), writes=(), extra=(), **kw):
        self.wait(Eng._deps(reads, writes))
        self.wait(extra)
        self.h.dma_start(out=out, in_=in_, **kw).then_inc(dsem.sem, 16)
        dsem.count += 16
        tok = Tok(dsem.sem, dsem.count)
        Eng._commit(tok, reads, writes)
        return tok


NDP = int(os.environ.get('NDP', 22))
SKIPDOWN = int(os.environ.get('SKIPDOWN', 0))
SKIPNORM = int(os.environ.get('SKIPNORM', 0))

NT = 1028
TILES = [(0, 4), (4, 516), (516, 1028)]
NPAIR = 44
NGRP = 11
FFN = 5632


class Kit:
    def __init__(self, nc, es):
        self.nc = nc
        sem = lambda n: es.enter_context(nc.semaphore(n))
        self.PE = Eng(nc.tensor, sem("s_pe"), skip_self=True)
        self.ACT = Eng(nc.scalar, sem("s_act"))
        self.DVE = Eng(nc.vector, sem("s_dve"))
        self.POOL = Eng(nc.gpsimd, sem("s_pool"))
        self.SP = DmaQ(nc.sync)
        self.GQ = DmaQ(nc.gpsimd)
        self.es = es
        self.banks = [es.enter_context(nc.psum_tensor(f"bank{i}", [128, 512], F32)) for i in range(7)]
        self.bankB = [Buf(f"bank{i}") for i in range(8)]
        self._nsem = 0

    def dsem(self):
        self._nsem += 1
        return DmaSem(self.es.enter_context(self.nc.semaphore(f"dsem{self._nsem}")))

    def barrier(self):
        toks = [Tok(e.sem, e.count) for e in (self.PE, self.ACT, self.DVE, self.POOL) if e.count > 0]
        toks += self._dtoks()
        for e in (self.PE, self.ACT, self.DVE, self.POOL, self.SP, self.GQ):
            e.wait(toks)

    def _dtoks(self):
        return [Tok(d.sem, d.count) for d in getattr(self, "_dsems", []) if d.count > 0]


def phase_of(nc, K, d, do_outproj, last, tag="of"):
    PE, ACT, DVE, POOL, SP, GQ = K.PE, K.ACT, K.DVE, K.POOL, K.SP, K.GQ
    with ExitStack() as es:
        sb = lambda name, shape, dt: es.enter_context(nc.sbuf_tensor(f"sb_{tag}_{name}", shape, dt))
        xT = sb("xT", [128, 16, NT], F32)
        hT = sb("hT", [128, 16, NT], BF16)
        gb = sb("gb", [128, 2, 4, 1026], BF16)
        wb = sb("wb", [128, 4, 16, 256], BF16)
        wd = sb("wd", [128, 2, 4, 2048], BF16)
        ub = [sb(f"ub{i}", [128, NT], F32) for i in range(2)]
        yy = [sb(f"yy{i}", [128, NT], F32) for i in range(2)]
        sg = sb("sg", [128, NT], F32)
        cw = sb("cw", [128, 88, 3], F32)
        cb = sb("cb", [128, 88], F32)
        gn = sb("gn", [128, 16], F32)
        hv = sb("hv", [128, 1], F32)
        ones = sb("ones", [128, 128], BF16)
        rt = sb("rt", [128, 512], F32)
        rstd = sb("rstd", [128, 512], F32)

        xB = [Buf(f"x{j}") for j in range(16)]
        hB, gB, wbB, wdB = Buf("h"), [Buf(), Buf()], [Buf(), Buf(), Buf(), Buf()], [Buf(), Buf()]
        ubB, yB, sgB = [Buf(), Buf()], [Buf(), Buf()], Buf()
        cB, onesB, rtB, rstdB = Buf("c"), Buf("ones"), Buf(), Buf()
        outB = Buf("out")
        ld_x, ld_c, st_o = K.dsem(), K.dsem(), K.dsem()
        ld_wb = [K.dsem() for _ in range(4)]
        ld_wd = [K.dsem() for _ in range(2)]
        bank7 = es.enter_context(nc.psum_tensor(f"{tag}_bank7", [128, 512], F32))
        bank, bankB = K.banks + [bank7], K.bankB
        U = [0, 1, 2, 3]
        Hb, D, Nb = 4, [5, 6], 7

        xin_v = d["xin"].rearrange("(c p) t -> p c t", p=128)
        for q in range(4):
            SP.dma(xT[:, 4 * q:4 * q + 4, :], xin_v[:, 4 * q:4 * q + 4, :], ld_x, writes=xB[4 * q:4 * q + 4])
        if do_outproj:
            m3 = d["m3"] if "m3" in d else d["m"].rearrange("(c p) t -> p c t", p=128)
            SP.dma(hT[:], m3, ld_x, writes=[hB], extra=d.get("m_dep", ()))
        for b_ in xB + [hB]:
            _merge(b_.w, Tok(ld_x.sem, ld_x.count))
        for (t, src) in ((cw, d["cw"]), (cb, d["cb"]), (gn, d["gn"]), (hv, d["hv"])):
            SP.dma(t[:], src, ld_c, writes=[cB])
        POOL.op(lambda h: h.memset(ones[:], 1.0), writes=[onesB])

        wb_n = [0]

        def load_wblk(src_cols):
            s = wb_n[0] % 4
            wb_n[0] += 1
            GQ.dma(wb[:, s], src_cols.rearrange("(c p) n -> p c n", p=128), ld_wb[s], writes=[wbB[s]])
            return s

        ring = {"u": 0, "d": 0}

        def next_bank(kind):
            lst = U if kind == "u" else D
            b = lst[ring[kind] % len(lst)]
            ring[kind] += 1
            return b

        def mm_group(bk, n, lhs_fn, rhs_fn, nch, reads):
            fns = []
            for c in range(nch):
                fns.append(lambda h, c=c: h.matmul(bank[bk][:, 0:n], lhs_fn(c), rhs_fn(c), start=(c == 0), stop=(c == nch - 1)))
            return PE.group(fns, reads=reads, writes=[bankB[bk]])

        if do_outproj:
            for blk in range(8):
                s = load_wblk(d["wo"][:, blk * 256:(blk + 1) * 256])
                for jj in range(2):
                    j = blk * 2 + jj
                    for (lo, hi) in TILES:
                        n = hi - lo
                        bk = Hb if n == 4 else next_bank("u")
                        mm_group(bk, n, lambda c: wb[:, s, c, jj * 128:(jj + 1) * 128], lambda c: hT[:, c, lo:hi], 16, [wbB[s], hB])
                        DVE.op(lambda h: h.tensor_tensor(out=xT[:, j, lo:hi], in0=bank[bk][:, 0:n], in1=xT[:, j, lo:hi], op=ALU.add),
                               reads=[bankB[bk]], writes=[xB[j]])

        sqv = wd[:, 0].rearrange("p a (b t) -> p (a b) t", t=512)
        for (lo, hi) in ([] if SKIPNORM else TILES):
            n = hi - lo
            ACT.op(lambda h: h.activation(out=sqv[:, :, 0:n], in_=xT[:, :, lo:hi], func=AF.Square), reads=xB, writes=[wdB[0]])
            mm_group(Nb, n, lambda c: ones[:], lambda c: sqv[:, c, 0:n], 16, [wdB[0], onesB])
            DVE.op(lambda h: h.tensor_scalar(out=rt[:, 0:n], in0=bank[Nb][:, 0:n], scalar1=1.0 / 2048.0, scalar2=1e-6, op0=ALU.mult, op1=ALU.add),
                   reads=[bankB[Nb]], writes=[rtB])
            ACT.op(lambda h: h.activation(out=rt[:, 0:n], in_=rt[:, 0:n], func=AF.Sqrt), reads=[rtB], writes=[rtB])
            DVE.op(lambda h: h.reciprocal(out=rstd[:, 0:n], in_=rt[:, 0:n]), reads=[rtB], writes=[rstdB])
            for c in range(16):
                DVE.op(lambda h: h.scalar_tensor_tensor(out=hT[:, c, lo:hi], in0=xT[:, c, lo:hi], scalar=gn[:, c:c + 1], in1=rstd[:, 0:n],
                                                        op0=ALU.mult, op1=ALU.mult),
                       reads=[xB[c], rstdB, cB], writes=[hB])

        def load_wd(grp):
            s = grp % 2
            src = d["wdn"][grp * 512:(grp + 1) * 512, :].rearrange("(a p) n -> p a n", p=128)
            GQ.dma(wd[:, s], src, ld_wd[s], writes=[wdB[s]])

        def down(grp):
            s = grp % 2
            if SKIPDOWN: return
            for dj in range(16):
                for (lo, hi) in TILES:
                    n = hi - lo
                    if n == 4:
                        if last:
                            continue
                        lo, n = 2, 2
                    glo = lo - 2
                    bk = Hb if n == 2 else next_bank("d")
                    mm_group(bk, n, lambda a: wd[:, s, a, dj * 128:(dj + 1) * 128], lambda a: gb[:, s, a, glo:glo + n], 4, [wdB[s], gB[s]])
                    DVE.op(lambda h: h.tensor_tensor(out=xT[:, dj, lo:lo + n], in0=bank[bk][:, 0:n], in1=xT[:, dj, lo:lo + n], op=ALU.add),
                           reads=[bankB[bk]], writes=[xB[dj]])

        for dp in range(NDP):
            slots = (load_wblk(d["wup"][:, dp * 256:(dp + 1) * 256]),
                     load_wblk(d["wup"][:, FFN + dp * 256:FFN + (dp + 1) * 256]))
            if dp % 2 == 0:
                load_wd(dp // 2)
            for jj in range(2):
                j = 2 * dp + jj
                grp, gi = j // 4, j % 4
                gs = grp % 2
                for kind in range(2):
                    s = slots[kind]
                    tl = kind * 44 + j
                    for (lo, hi) in TILES:
                        n = hi - lo
                        bk = Hb if n == 4 else next_bank("u")
                        mm_group(bk, n, lambda c: wb[:, s, c, jj * 128:(jj + 1) * 128], lambda c: hT[:, c, lo:hi], 16, [wbB[s], hB])
                        ACT.op(lambda h: h.activation(out=ub[kind][:, lo:hi], in_=bank[bk][:, 0:n], func=AF.Copy),
                               reads=[bankB[bk]], writes=[ubB[kind]])
                        ACT.op(lambda h: h.activation(out=yy[kind][:, lo:hi], in_=bank[bk][:, 0:n], func=AF.Identity,
                                                      bias=cb[:, tl:tl + 1], scale=cw[:, tl, 2:3]),
                               reads=[bankB[bk], cB], writes=[yB[kind]])
                    for tap, sh in ((1, 1), (0, 2)):
                        DVE.op(lambda h: h.scalar_tensor_tensor(out=yy[kind][:, 2:NT], in0=ub[kind][:, 2 - sh:NT - sh], scalar=cw[:, tl, tap:tap + 1],
                                                                in1=yy[kind][:, 2:NT], op0=ALU.mult, op1=ALU.add),
                               reads=[ubB[kind], yB[kind], cB], writes=[yB[kind]])
                ACT.op(lambda h: h.activation(out=sg[:, 2:NT], in_=yy[0][:, 2:NT], func=AF.Silu), reads=[yB[0]], writes=[sgB])
                DVE.op(lambda h: h.tensor_tensor(out=gb[:, gs, gi, :], in0=sg[:, 2:NT], in1=yy[1][:, 2:NT], op=ALU.mult),
                       reads=[sgB, yB[1]], writes=[gB[gs]])
                if gi == 3 and grp >= 1:
                    down(grp - 1)
        if NDP == 22: down(NGRP - 1)

        if not last:
            DVE.op(lambda h: h.tensor_scalar(out=xT[:, :, 0:4], in0=xT[:, :, 0:4], scalar1=hv[:, 0:1], scalar2=None, op0=ALU.mult),
                   reads=xB + [cB], writes=xB)
        if "xout" in d:
            xo_v = d["xout"].rearrange("(c p) t -> p c t", p=128)
            for q in range(4):
                SP.dma(xo_v[:, 4 * q:4 * q + 4, :], xT[:, 4 * q:4 * q + 4, :], st_o, reads=xB[4 * q:4 * q + 4], writes=[outB])
        if "xout_own" in d:
            xo_v = d["xout_own"].rearrange("(c p) t -> p c t", p=128)
            for q in range(4):
                SP.dma(xo_v[:, 4 * q:4 * q + 4, :], xT[:, 4 * q:4 * q + 4, 4:NT], st_o, reads=xB[4 * q:4 * q + 4], writes=[outB])
        SP.wait([Tok(st_o.sem, st_o.count)])
        toks = [Tok(e.sem, e.count) for e in (PE, ACT, DVE, POOL) if e.count > 0]
        toks += [Tok(s.sem, s.count) for s in [ld_x, ld_c, st_o] + ld_wb + ld_wd if s.count > 0]
        for e in (PE, ACT, DVE, POOL, SP, GQ):
            e.wait(toks)


PN = int(os.environ.get('POOLNORM', 1))
PS = int(os.environ.get('POOLSQ', 1))

TQ = 256
NTILE = 4096 // TQ
ND = 19


def attn_mask_np():
    j = np.arange(128)[:, None]
    i = np.arange(128)[None, :]
    M = np.zeros((128, ND * 128), np.float32)
    for d in range(-1, 18):
        dist = 128 * d + i - j
        m = np.zeros((128, 128), np.float32)
        for win, dil in ((128, 1), (512, 4), (2048, 16)):
            m += ((dist >= 0) & (dist <= win) & (dist % dil == 0)).astype(np.float32)
        M[:, (d + 1) * 128:(d + 2) * 128] = m
    return M.astype(ml_dtypes.bfloat16)


def phase_attn(nc, K, d, tag="at", ntile=NTILE):
    PE, ACT, DVE, POOL, SP, GQ = K.PE, K.ACT, K.DVE, K.POOL, K.SP, K.GQ
    bankB = K.bankB
    with ExitStack() as es:
        sb = lambda name, shape, dt: es.enter_context(nc.sbuf_tensor(f"sb_{tag}_{name}", shape, dt))
        xt = sb("xt", [128, 16, TQ], F32)
        ht = sb("ht", [128, 2, 16, TQ], BF16)
        wq = sb("wq", [128, 3, 16, 512], BF16)
        kT = sb("kT", [128, 4, 4096], BF16)
        va = sb("va", [128, 32, 4, 130], BF16)
        qT = sb("qT", [128, 4, TQ], BF16)
        mk = sb("mk", [128, ND * 128], BF16)
        pT = sb("pT", [128, 3, TQ], BF16)
        sqs = sb("sqs", [128, TQ], BF16)
        rt = sb("rt", [128, TQ], F32)
        rstd = sb("rstd", [128, TQ], F32)
        rq = sb("rq", [128, TQ], F32)
        on = sb("on", [128, 2, 128], BF16)
        rden = sb("rden", [128, 2], F32)
        oT = sb("oT", [128, 2, 4, TQ], BF16)
        gn = sb("gn", [128, 16], F32)
        qkg = sb("qkg", [128, 2], F32)
        ones = sb("ones", [128, 128], BF16)
        ident = sb("ident", [128, 128], BF16)
        tp = es.enter_context(nc.psum_tensor(f"{tag}_tp", [128, 1024], BF16))
        bank = K.banks

        xtB, htB, wB, kB, vB, qB, cB = Buf(), [Buf(), Buf()], Buf(), Buf(), Buf(), Buf(), Buf()
        pB, sqsB, rtB, rstdB, rqB, onB, rdB, oTB = [Buf(), Buf(), Buf()], Buf(), Buf(), Buf(), Buf(), [Buf(), Buf()], [Buf(), Buf()], [Buf(), Buf()]
        onesB, outB, tpB = Buf(), Buf(), K.bankB[7]
        ld_x, ld_c, ld_w = K.dsem(), K.dsem(), K.dsem()
        st_o = [K.dsem(), K.dsem()]
        PR, NB, SR, OB = [0, 1, 3, 4], 2, [3, 4], [5, 6]
        ring = {"p": 0, "s": 0, "pt": 0}

        toks = []
        for (t, src) in ((gn[:], d["gn"]), (qkg[:, 0:1], d["qg"]), (qkg[:, 1:2], d["kg"]), (mk[:], d["mask"]), (ident[:], d["ident"])):
            SP.dma(t, src, ld_c, writes=[cB])
        for i in range(3):
            GQ.dma(wq[:, i], d["wqkv"][:, i * 512:(i + 1) * 512].rearrange("(c p) n -> p c n", p=128), ld_w, writes=[wB])
        POOL.op(lambda h: h.memset(ones[:], 1.0), writes=[onesB])
        POOL.op(lambda h: h.memset(va[:, :, :, 128:130], 1.0), writes=[vB])
        if "opad" in d:
            POOL.op(lambda h: h.memset(on[:, 0, 0:16], 0.0), writes=[onB[0]])
            SP.dma(d["opad"].rearrange("(h p) t -> p h t", p=128), on[:, 0, 0:16].rearrange("p (h t) -> p h t", h=4), st_o[0], reads=[onB[0]], writes=[outB])
        DVE.op(lambda h: h.tensor_scalar(out=qkg[:, 0:1], in0=qkg[:, 0:1], scalar1=128.0 ** -0.5, scalar2=None, op0=ALU.mult), reads=[cB], writes=[cB])

        xv = d["xfull"].rearrange("(c p) t -> p c t", p=128)
        ov = d["oT"].rearrange("(h p) t -> p h t", p=128)

        def mm_group(bk, n, lhs_fn, rhs_fn, nch, reads):
            fns = [(lambda h, c=c: h.matmul(bank[bk][:, 0:n], lhs_fn(c), rhs_fn(c), start=(c == 0), stop=(c == nch - 1))) for c in range(nch)]
            return PE.group(fns, reads=reads, writes=[bankB[bk]])

        def nb(kind, lst):
            b = lst[ring[kind] % len(lst)]
            ring[kind] += 1
            return b

        def rsqrt_from_bank(bk, n, scale, dst, dstB):
            DVE.op(lambda h: h.tensor_scalar(out=rt[:, 0:n], in0=bank[bk][:, 0:n], scalar1=scale, scalar2=1e-6, op0=ALU.mult, op1=ALU.add),
                   reads=[bankB[bk]], writes=[rtB])
            ACT.op(lambda h: h.activation(out=rt[:, 0:n], in_=rt[:, 0:n], func=AF.Ln), reads=[rtB], writes=[rtB])
            ACT.op(lambda h: h.activation(out=dst[:, 0:n], in_=rt[:, 0:n], func=AF.Exp, scale=-0.5), reads=[rtB], writes=[dstB])

        raw = sb("raw", [128, 4, TQ], F32)
        sq2 = sb("sq2", [128, 2, TQ], BF16)
        oraw = sb("oraw", [128, 2, 2, 130], F32)
        rawB, sq2B, orawB = [Buf() for _ in range(4)], [Buf(), Buf()], [[Buf(), Buf()], [Buf(), Buf()]]
        sslots = [(0, 0), (1, 0), (3, 0), (4, 0)]
        sslotB = [bankB[0], bankB[1], bankB[3], bankB[4]]
        pT4 = sb("pT4", [128, 4, TQ], BF16)
        pB4 = [Buf() for _ in range(4)]

        def load_norm(t):
            T0 = t * TQ
            hs = t % 2
            SP.dma(xt[:], xv[:, :, T0:T0 + TQ], ld_x, writes=[xtB])
            if PN:
                POOL.op(lambda h: h.tensor_tensor(out=ht[:, hs], in0=xt[:], in1=xt[:], op=ALU.mult), reads=[xtB], writes=[htB[hs]])
            else:
                ACT.op(lambda h: h.activation(out=ht[:, hs], in_=xt[:], func=AF.Square), reads=[xtB], writes=[htB[hs]])
            mm_group(NB, TQ, lambda c: ones[:], lambda c: ht[:, hs, c, :], 16, [htB[hs], onesB])
            rsqrt_from_bank(NB, TQ, 1.0 / 2048.0, rstd, rstdB)
            for c in range(16):
                if PN:
                    POOL.op(lambda h: h.tensor_scalar(out=xt[:, c, :], in0=xt[:, c, :], scalar1=gn[:, c:c + 1], scalar2=1.0, op0=ALU.mult, op1=ALU.mult),
                            reads=[xtB, cB], writes=[xtB])
                    POOL.op(lambda h: h.tensor_tensor(out=ht[:, hs, c, :], in0=xt[:, c, :], in1=rstd[:], op=ALU.mult),
                            reads=[xtB, rstdB], writes=[htB[hs]])
                else:
                    DVE.op(lambda h: h.scalar_tensor_tensor(out=ht[:, hs, c, :], in0=xt[:, c, :], scalar=gn[:, c:c + 1], in1=rstd[:], op0=ALU.mult, op1=ALU.mult),
                           reads=[xtB, rstdB, cB], writes=[htB[hs]])

        def qk_norm_rest(f, t):
            T0 = t * TQ
            isk, hh = f // 4, f % 4
            rs, ss = f % 4, f % 2
            mm_group(NB, TQ, lambda c: ones[:], lambda c: sq2[:, ss, :], 1, [sq2B[ss], onesB])
            rsqrt_from_bank(NB, TQ, 1.0 / 128.0, rq, rqB)
            dst = kT[:, hh, T0:T0 + TQ] if isk else qT[:, hh, :]
            DVE.op(lambda h: h.scalar_tensor_tensor(out=dst, in0=raw[:, rs, :], scalar=qkg[:, isk:isk + 1], in1=rq[:], op0=ALU.mult, op1=ALU.mult),
                   reads=[rawB[rs], rqB, cB], writes=[kB if isk else qB])

        def proj(t):
            hs = t % 2
            for f in range(8):
                isk, hh = f // 4, f % 4
                rs, ss = f % 4, f % 2
                bk = nb("p", PR)
                mm_group(bk, TQ, lambda c: wq[:, isk, c, hh * 128:(hh + 1) * 128], lambda c: ht[:, hs, c, :], 16, [wB, htB[hs]])
                ACT.op(lambda h: h.activation(out=raw[:, rs, :], in_=bank[bk][:, 0:TQ], func=AF.Copy), reads=[bankB[bk]], writes=[rawB[rs]])
                if PS:
                    POOL.op(lambda h: h.tensor_tensor(out=sq2[:, ss, :], in0=raw[:, rs, :], in1=raw[:, rs, :], op=ALU.mult), reads=[rawB[rs]], writes=[sq2B[ss]])
                else:
                    ACT.op(lambda h: h.activation(out=sq2[:, ss, :], in_=raw[:, rs, :], func=AF.Square), reads=[rawB[rs]], writes=[sq2B[ss]])
                if f >= 1:
                    qk_norm_rest(f - 1, t)
            for a in range(TQ // 128):
                blk = t * (TQ // 128) + a
                bk = nb("p", PR)
                mm_group(bk, 512, lambda c: ht[:, hs, c, a * 128:(a + 1) * 128], lambda c: wq[:, 2, c, :], 16, [wB, htB[hs]])
                if a == 0:
                    qk_norm_rest(7, t)
                ACT.op(lambda h: h.activation(out=va[:, blk, :, 0:128], in_=bank[bk][:, 0:512].rearrange("p (h e) -> p h e", h=4), func=AF.Copy),
                       reads=[bankB[bk]], writes=[vB])

        def attention(t):
            T0 = t * TQ
            os_ = t % 2
            kbs = list(range(max(0, 2 * t - 16), 2 * t + 2))
            steps = [(hh, kb) for hh in range(4) for kb in kbs]
            info = {}
            pending = []

            def front(i):
                hh, kb = steps[i]
                D = 2 * t - kb
                si = ring["s"] % 4
                ring["s"] += 1
                sbk, sc0 = sslots[si]
                ps = ring["pt"] % 4
                ring["pt"] += 1
                info[i] = ps
                PE.group([lambda h: h.matmul(bank[sbk][:, sc0:sc0 + TQ], kT[:, hh, kb * 128:(kb + 1) * 128], qT[:, hh, :], start=True, stop=True)],
                         reads=[kB, qB], writes=[sslotB[si]])
                ACT.op(lambda h: h.activation(out=pT4[:, ps, :], in_=bank[sbk][:, sc0:sc0 + TQ], func=AF.Exp), reads=[sslotB[si]], writes=[pB4[ps]])
                DVE.op(lambda h: h.tensor_tensor(out=pT4[:, ps, :], in0=pT4[:, ps, :], in1=mk[:, (D + 1) * 128:(D + 3) * 128], op=ALU.mult),
                       reads=[pB4[ps], cB], writes=[pB4[ps]])

            def epi2(hh, par):
                for a in range(2):
                    DVE.op(lambda h: h.reciprocal(out=rden[:, a:a + 1], in_=oraw[:, par, a, 128:129]), reads=[orawB[par][a]], writes=[rdB[a]])
                    ACT.op(lambda h: h.activation(out=on[:, a, :], in_=oraw[:, par, a, 0:128], func=AF.Copy, scale=rden[:, a:a + 1]),
                           reads=[orawB[par][a], rdB[a]], writes=[onB[a]])
                    PE.group([lambda h: h.transpose(tp[:, a * 128:(a + 1) * 128], on[:, a, :], ident[:])], reads=[onB[a], cB], writes=[tpB])
                DVE.op(lambda h: h.tensor_copy(out=oT[:, os_, hh, :], in_=tp[:, 0:256]), reads=[tpB], writes=[oTB[os_]])

            def back(i):
                hh, kb = steps[i]
                D = 2 * t - kb
                ps = info.pop(i)
                for a in range(2):
                    dd = D + a
                    if dd < 0 or dd > 16:
                        continue
                    first = (kb == kbs[0]) or (dd == 16)
                    lastk = (dd == 0)
                    PE.group([lambda h: h.matmul(bank[OB[a]][:, 0:129], pT4[:, ps, a * 128:(a + 1) * 128], va[:, kb, hh, 0:129], start=first, stop=lastk)],
                             reads=[pB4[ps], vB], writes=[bankB[OB[a]]])
                if kb == kbs[-1]:
                    par = hh % 2
                    for a in range(2):
                        DVE.op(lambda h: h.tensor_copy(out=oraw[:, par, a, 0:129], in_=bank[OB[a]][:, 0:129]), reads=[bankB[OB[a]]], writes=[orawB[par][a]])
                    pending.append((i + min(4, 2 * len(kbs) - 1), hh, par))

            LA = 3
            n = len(steps)
            for i in range(min(LA, n)):
                front(i)
            for i in range(n):
                if i + LA < n:
                    front(i + LA)
                back(i)
                while pending and pending[0][0] <= i:
                    _, hh_, par_ = pending.pop(0)
                    epi2(hh_, par_)
            while pending:
                _, hh_, par_ = pending.pop(0)
                epi2(hh_, par_)
            SP.dma(ov[:, :, T0:T0 + TQ], oT[:, os_], st_o[os_], reads=[oTB[os_]], writes=[outB])

        load_norm(0)
        for t in range(ntile):
            proj(t)
            if t + 1 < ntile:
                load_norm(t + 1)
            attention(t)
        SP.wait([Tok(s_.sem, s_.count) for s_ in st_o if s_.count > 0])
        toks = [Tok(e.sem, e.count) for e in (PE, ACT, DVE, POOL) if e.count > 0]
        toks += [Tok(s.sem, s.count) for s in [ld_x, ld_c, ld_w] + st_o if s.count > 0]
        for e in (PE, ACT, DVE, POOL, SP, GQ):
            e.wait(toks)


TQ = 256
NTILE = 4096 // TQ


def lstm_consts_np():
    s = np.arange(128)[:, None]
    l = np.arange(128)[None, :]
    tri = ((s // 64 == l // 64) & (s <= l)).astype(np.float32)
    sel = np.zeros((128, 2), np.float32)
    sel[:64, 0] = 1.0
    sel[64:, 1] = 1.0
    return tri, sel


def phase_lstm(nc, K, d, tag="ls", ntile=NTILE):
    PE, ACT, DVE, POOL, SP, GQ = K.PE, K.ACT, K.DVE, K.POOL, K.SP, K.GQ
    bankB = K.bankB
    with ExitStack() as es:
        sb = lambda name, shape, dt: es.enter_context(nc.sbuf_tensor(f"sb_{tag}_{name}", shape, dt))
        xt = sb("xt", [128, 16, TQ], F32)
        ht = sb("ht", [128, 2, 16, TQ], BF16)
        win = sb("win", [128, 3, 16, 512], BF16)
        wg = sb("wg", [128, 16, 2], BF16)
        zq = sb("zq", [128, 4, TQ + 3], F32)
        acc = sb("acc", [128, 4, TQ], F32)
        qk = sb("qk", [128, 2, 4, TQ], BF16)
        ktm = sb("ktm", [128, 2, 2, 256], BF16)
        vp = sb("vp", [128, 4, 512], BF16)
        osg = sb("osg", [128, 4, 512], BF16)
        hb = sb("hb", [128, 512], F32)
        hjunk = sb("hjunk", [128, 512], BF16)
        htmp = sb("htmp", [128, 512], F32)
        hstm = sb("hstm", [128, 512], BF16)
        hsT = sb("hsT", [128, 2, 4, TQ], BF16)
        Sf = sb("Sf", [128, 2, 512], F32)
        Sb = sb("Sb", [128, 2, 512], BF16)
        nf = sb("nf", [128, 2], F32)
        nbf = sb("nbf", [128, 2], BF16)
        qts = sb("qts", [128, 2, 2, 64], BF16)
        WT = sb("WT", [128, 4, 128], BF16)
        gsb = sb("gsb", [128, 4, 2], F32)
        t1 = sb("t1", [128, 4, 1], F32)
        lf = sb("lf", [128, 4, 1], F32)
        lfm = sb("lfm", [128, 4, 2], F32)
        el = sb("el", [128, 4, 1], F32)
        amb = sb("amb", [128, 4, 1], F32)
        af = sb("af", [128, 4, 1], F32)
        abf = sb("abf", [128, 4, 1], BF16)
        dvec = sb("dvec", [128, 4, 2], F32)
        sm = sb("sm", [128, 8], F32)
        rt = sb("rt", [128, TQ], F32)
        rstd = sb("rstd", [128, TQ], F32)
        gn = sb("gn", [128, 16], F32)
        cw = sb("cw", [128, 4, 4], F32)
        cb = sb("cb", [128, 4], F32)
        hg = sb("hg", [128, 512], F32)
        gbias = sb("gbias", [128, 2], F32)
        tri = sb("tri", [128, 128], F32)
        onesf = sb("onesf", [128, 128], F32)
        sel = sb("sel", [128, 2], F32)
        ones = sb("ones", [128, 128], BF16)
        ident = sb("ident", [128, 128], BF16)
        tp = es.enter_context(nc.psum_tensor(f"{tag}_tp", [128, 1024], BF16))
        bank = K.banks

        B = lambda: Buf()
        xtB, htB, wB, cB, onesB = B(), [B(), B()], B(), B(), B()
        zqB, accB, qkB, ktmB = [B() for _ in range(4)], [B() for _ in range(4)], [B(), B()], [B(), B()]
        vpB, osgB, hbB, hjB, htmpB, hstmB, hsTB = [B() for _ in range(4)], [B() for _ in range(4)], B(), B(), B(), B(), [B(), B()]
        SfB, SbB, nfB, nbfB, qtsB, WTB = B(), B(), B(), B(), [B(), B()], [B() for _ in range(4)]
        gB, smB, rtB, rstdB, outB = [B() for _ in range(4)], B(), B(), B(), B()
        tpB = K.bankB[7]
        SMB = 2
        gateR = cumR = denR = dnR = sTR = K.bankB[2]
        PR, NUMB, DC = [0, 1], [3, 4], [5, 6]
        ld_x, ld_c, ld_w = K.dsem(), K.dsem(), K.dsem()
        st_o = [K.dsem(), K.dsem()]
        ring = {"p": 0}

        for (t, src) in ((gn[:], d["gn"]), (cw[:], d["cw"]), (cb[:], d["cb"]), (hg[:], d["hg"]), (gbias[:], d["gbias"]),
                         (tri[:], d["tri"]), (sel[:], d["sel"]), (ident[:], d["ident"])):
            SP.dma(t, src, ld_c, writes=[cB])
        for i in range(3):
            GQ.dma(win[:, i], d["win"][:, i * 512:(i + 1) * 512].rearrange("(c p) n -> p c n", p=128), ld_w, writes=[wB])
        GQ.dma(wg[:], d["wg"].rearrange("(c p) n -> p c n", p=128), ld_w, writes=[wB])
        POOL.op(lambda h: h.memset(ones[:], 1.0), writes=[onesB])
        POOL.op(lambda h: h.memset(onesf[:], 1.0), writes=[onesB])
        POOL.op(lambda h: h.memset(zq[:], 0.0), writes=zqB)
        POOL.op(lambda h: h.memset(Sf[:], 0.0), writes=[SfB])
        POOL.op(lambda h: h.memset(nf[:], 0.0), writes=[nfB])

        if "xtile" in d:
            xtile = d["xtile"]
        else:
            xv = d["xfull"].rearrange("(c p) t -> p c t", p=128)
            xtile = lambda t: xv[:, :, t * TQ:(t + 1) * TQ]
        if "opad" in d:
            POOL.op(lambda h: h.memset(hstm[:, 0:16], 0.0), writes=[hstmB])
            SP.dma(d["opad"].rearrange("(h p) t -> p h t", p=128), hstm[:, 0:16].rearrange("p (h t) -> p h t", h=4), st_o[0], reads=[hstmB], writes=[outB])
        ov = d["hsT"].rearrange("(h p) t -> p h t", p=128)

        def mm_group(out_ap, lhs_fn, rhs_fn, nch, reads, writes):
            fns = [(lambda h, c=c: h.matmul(out_ap, lhs_fn(c), rhs_fn(c), start=(c == 0), stop=(c == nch - 1))) for c in range(nch)]
            return PE.group(fns, reads=reads, writes=writes)

        def nb(kind, lst):
            b = lst[ring[kind] % len(lst)]
            ring[kind] += 1
            return b

        def load_norm(t):
            T0 = t * TQ
            hs = t % 2
            SP.dma(xt[:], xtile(t), ld_x, writes=[xtB], extra=d.get("x_dep", ()))
            POOL.op(lambda h: h.tensor_tensor(out=ht[:, hs], in0=xt[:], in1=xt[:], op=ALU.mult), reads=[xtB], writes=[htB[hs]])
            mm_group(bank[PR[0]][:, 0:TQ], lambda c: ones[:], lambda c: ht[:, hs, c, :], 16, [htB[hs], onesB], [bankB[PR[0]]])
            DVE.op(lambda h: h.tensor_scalar(out=rt[:], in0=bank[PR[0]][:, 0:TQ], scalar1=1.0 / 2048.0, scalar2=1e-6, op0=ALU.mult, op1=ALU.add),
                   reads=[bankB[PR[0]]], writes=[rtB])
            ACT.op(lambda h: h.activation(out=rt[:], in_=rt[:], func=AF.Ln), reads=[rtB], writes=[rtB])
            ACT.op(lambda h: h.activation(out=rstd[:], in_=rt[:], func=AF.Exp, scale=-0.5), reads=[rtB], writes=[rstdB])
            for c in range(16):
                POOL.op(lambda h: h.tensor_scalar(out=xt[:, c, :], in0=xt[:, c, :], scalar1=gn[:, c:c + 1], scalar2=1.0, op0=ALU.mult, op1=ALU.mult),
                        reads=[xtB, cB], writes=[xtB])
                POOL.op(lambda h: h.tensor_tensor(out=ht[:, hs, c, :], in0=xt[:, c, :], in1=rstd[:], op=ALU.mult),
                        reads=[xtB, rstdB], writes=[htB[hs]])

        def proj_qk_f(t, f):
            hs = t % 2
            qs = t % 2
            if True:
                bk = nb("p", PR)
                mm_group(bank[bk][:, 0:TQ], lambda c: win[:, 0, c, f * 128:(f + 1) * 128], lambda c: ht[:, hs, c, :], 16, [wB, htB[hs]], [bankB[bk]])
                ACT.op(lambda h: h.activation(out=zq[:, f, 3:3 + TQ], in_=bank[bk][:, 0:TQ], func=AF.Copy), reads=[bankB[bk]], writes=[zqB[f]])
                DVE.op(lambda h: h.tensor_scalar(out=acc[:, f, :], in0=zq[:, f, 3:3 + TQ], scalar1=cw[:, f, 3:4], scalar2=cb[:, f:f + 1], op0=ALU.mult, op1=ALU.add),
                       reads=[zqB[f], cB], writes=[accB[f]])
                for j in (1, 2, 3):
                    DVE.op(lambda h: h.scalar_tensor_tensor(out=acc[:, f, :], in0=zq[:, f, 3 - j:3 - j + TQ], scalar=cw[:, f, 3 - j:4 - j], in1=acc[:, f, :],
                                                            op0=ALU.mult, op1=ALU.add), reads=[zqB[f], accB[f], cB], writes=[accB[f]])
                DVE.op(lambda h: h.tensor_copy(out=zq[:, f, 0:3], in_=zq[:, f, TQ:TQ + 3]), reads=[zqB[f]], writes=[zqB[f]])
                ACT.op(lambda h: h.activation(out=acc[:, f, :], in_=acc[:, f, :], func=AF.Silu), reads=[accB[f]], writes=[accB[f]])
                DVE.op(lambda h: h.tensor_scalar(out=qk[:, qs, f, :], in0=acc[:, f, :], scalar1=(1.0 if f < 2 else 0.0625), scalar2=None, op0=ALU.mult),
                       reads=[accB[f]], writes=[qkB[qs]])
        def ktrans(t):
            qs = t % 2
            for blk in range(2):
                for dkc in range(2):
                    col = (blk * 2 + dkc) * 128
                    PE.group([lambda h: h.transpose(tp[:, col:col + 128], qk[:, qs, 2 + dkc, blk * 128:(blk + 1) * 128], ident[:])],
                             reads=[qkB[qs], cB], writes=[tpB])
            DVE.op(lambda h: h.tensor_copy(out=ktm[:, qs].rearrange("p b k -> p (b k)"), in_=tp[:, 0:512]), reads=[tpB], writes=[ktmB[qs]])

        def oproj(t, blk):
            hs = t % 2
            bs = (t * 2 + blk) % 4
            tok = slice(blk * 128, (blk + 1) * 128)
            bk = nb("p", PR)
            mm_group(bank[bk][:, 0:512], lambda c: ht[:, hs, c, tok], lambda c: win[:, 2, c, :], 16, [wB, htB[hs]], [bankB[bk]])
            ACT.op(lambda h: h.activation(out=osg[:, bs, :], in_=bank[bk][:, 0:512], func=AF.Sigmoid), reads=[bankB[bk]], writes=[osgB[bs]])

        def gates(t, blk):
            hs = t % 2
            qs = t % 2
            bs = (t * 2 + blk) % 4
            tok = slice(blk * 128, (blk + 1) * 128)
            mm_group(bank[SMB][:, 0:2], lambda c: ht[:, hs, c, tok], lambda c: wg[:, c, :], 16, [wB, htB[hs]], [gateR])
            DVE.op(lambda h: h.tensor_tensor(out=gsb[:, bs, :], in0=bank[SMB][:, 0:2], in1=gbias[:], op=ALU.add), reads=[gateR, cB], writes=[gB[bs]])
            ACT.op(lambda h: h.activation(out=t1[:, bs, :], in_=gsb[:, bs, 1:2], func=AF.Exp, scale=-1.0), reads=[gB[bs]], writes=[gB[bs]])
            DVE.op(lambda h: h.tensor_scalar(out=t1[:, bs, :], in0=t1[:, bs, :], scalar1=1.0, scalar2=None, op0=ALU.add), reads=[gB[bs]], writes=[gB[bs]])
            ACT.op(lambda h: h.activation(out=t1[:, bs, :], in_=t1[:, bs, :], func=AF.Ln), reads=[gB[bs]], writes=[gB[bs]])
            DVE.op(lambda h: h.tensor_scalar(out=lf[:, bs, :], in0=t1[:, bs, :], scalar1=-1.0, scalar2=None, op0=ALU.mult), reads=[gB[bs]], writes=[gB[bs]])
            DVE.op(lambda h: h.tensor_scalar(out=lfm[:, bs, :], in0=sel[:], scalar1=lf[:, bs, 0:1], scalar2=None, op0=ALU.mult), reads=[gB[bs], cB], writes=[gB[bs]])
            PE.group([lambda h: h.matmul(bank[SMB][:, 4:5], tri[:], lf[:, bs, :], start=True, stop=True),
                      lambda h: h.matmul(bank[SMB][:, 6:8], onesf[:], lfm[:, bs, :], start=True, stop=True)],
                     reads=[gB[bs], cB, onesB], writes=[cumR])
            ACT.op(lambda h: h.activation(out=el[:, bs, :], in_=bank[SMB][:, 4:5], func=AF.Exp), reads=[cumR], writes=[gB[bs]])
            DVE.op(lambda h: h.tensor_tensor(out=amb[:, bs, :], in0=gsb[:, bs, 0:1], in1=bank[SMB][:, 4:5], op=ALU.subtract), reads=[cumR, gB[bs]], writes=[gB[bs]])
            ACT.op(lambda h: h.activation(out=af[:, bs, :], in_=amb[:, bs, :], func=AF.Exp), reads=[gB[bs]], writes=[gB[bs]])
            DVE.op(lambda h: h.tensor_copy(out=abf[:, bs, :], in_=af[:, bs, :]), reads=[gB[bs]], writes=[gB[bs]])
            ACT.op(lambda h: h.activation(out=dvec[:, bs, :], in_=bank[SMB][:, 6:8], func=AF.Exp), reads=[cumR], writes=[gB[bs]])

        def vproj_st(t, blk):
            hs = t % 2
            qs = t % 2
            bs = (t * 2 + blk) % 4
            tok = slice(blk * 128, (blk + 1) * 128)
            bk = nb("p", PR)
            mm_group(bank[bk][:, 0:512], lambda c: ht[:, hs, c, tok], lambda c: win[:, 1, c, :], 16, [wB, htB[hs]], [bankB[bk]])
            ACT.op(lambda h: h.activation(out=vp[:, bs, :], in_=bank[bk][:, 0:512], func=AF.Copy, scale=af[:, bs, :]), reads=[bankB[bk], gB[bs]], writes=[vpB[bs]])
            mm_group(bank[SMB][:, 128:256], lambda c: qk[:, qs, 2 + c, tok], lambda c: qk[:, qs, c, tok], 2, [qkB[qs]], [sTR])
            DVE.op(lambda h: h.tensor_tensor(out=WT[:, bs, :], in0=bank[SMB][:, 128:256], in1=tri[:], op=ALU.mult), reads=[sTR, cB], writes=[WTB[bs]])

        def chunk(t, blk, half):
            qs = t % 2
            bs = (t * 2 + blk) % 4
            pbs = (bs - 1) % 4
            nbk = NUMB[(t * 2 + blk) % 2]
            if True:
                c_idx = (t * 2 + blk) * 2 + half
                r0 = half * 64
                rows = slice(r0, r0 + 64)
                ctok = slice(blk * 128 + r0, blk * 128 + r0 + 64)
                dprev = dvec[:, bs, 0:1] if (half == 1 or c_idx == 0) else dvec[:, pbs, 1:2]
                if c_idx > 0:
                    ACT.op(lambda h: h.activation(out=qts[:, half], in_=qk[:, qs, 0:2, ctok], func=AF.Copy, scale=dprev),
                           reads=[qkB[qs], gB[bs], gB[pbs]], writes=[qtsB[half]])
                fns = []
                if c_idx > 0:
                    for dkc in range(2):
                        fns.append(lambda h, dkc=dkc: h.matmul(bank[nbk][rows, 0:512], qts[:, half, dkc, :], Sb[:, dkc, :], start=(dkc == 0), stop=False))
                fns.append(lambda h: h.matmul(bank[nbk][rows, 0:512], WT[rows, bs, r0:r0 + 64], vp[rows, bs, :], start=(c_idx == 0), stop=True))
                PE.group(fns, reads=[qtsB[half], SbB, WTB[bs], vpB[bs]], writes=[bankB[nbk]])
                fns = []
                if c_idx > 0:
                    for dkc in range(2):
                        fns.append(lambda h, dkc=dkc: h.matmul(bank[SMB][rows, 8:9], qts[:, half, dkc, :], nbf[:, dkc:dkc + 1], start=(dkc == 0), stop=False))
                fns.append(lambda h: h.matmul(bank[SMB][rows, 8:9], WT[rows, bs, r0:r0 + 64], abf[rows, bs, :], start=(c_idx == 0), stop=True))
                PE.group(fns, reads=[qtsB[half], nbfB, WTB[bs], gB[bs]], writes=[denR])
                for dkc in range(2):
                    PE.group([lambda h: h.matmul(bank[DC[dkc]][:, 0:512], ktm[rows, qs, blk, dkc * 128:(dkc + 1) * 128], vp[rows, bs, :], start=True, stop=True)],
                             reads=[ktmB[qs], vpB[bs]], writes=[bankB[DC[dkc]]])
                PE.group([(lambda h, dkc=dkc: h.matmul(bank[SMB][:, 12 + dkc:13 + dkc], ktm[rows, qs, blk, dkc * 128:(dkc + 1) * 128], abf[rows, bs, :], start=True, stop=True))
                          for dkc in range(2)], reads=[ktmB[qs], gB[bs]], writes=[dnR])
                dc = dprev
                for dkc in range(2):
                    DVE.op(lambda h: h.scalar_tensor_tensor(out=Sf[:, dkc, :], in0=Sf[:, dkc, :], scalar=dc, in1=bank[DC[dkc]][:, 0:512], op0=ALU.mult, op1=ALU.add),
                           reads=[bankB[DC[dkc]], gB[bs], gB[pbs], SfB], writes=[SfB])
                DVE.op(lambda h: h.scalar_tensor_tensor(out=nf[:], in0=nf[:], scalar=dc, in1=bank[SMB][:, 12:14], op0=ALU.mult, op1=ALU.add),
                       reads=[dnR, gB[bs], gB[pbs], nfB], writes=[nfB])
                ACT.op(lambda h: h.activation(out=Sb[:], in_=Sf[:], func=AF.Copy), reads=[SfB], writes=[SbB])
                DVE.op(lambda h: h.tensor_copy(out=nbf[:], in_=nf[:]), reads=[nfB], writes=[nbfB])

        def outstage(t, blk):
            bs = (t * 2 + blk) % 4
            nbk = NUMB[(t * 2 + blk) % 2]
            tok = slice(blk * 128, (blk + 1) * 128)
            DVE.op(lambda h: h.tensor_scalar(out=sm[:, 0:1], in0=bank[SMB][:, 8:9], scalar1=el[:, bs, :], scalar2=None, op0=ALU.mult),
                   reads=[denR, gB[bs]], writes=[smB])
            DVE.op(lambda h: h.tensor_scalar(out=sm[:, 6:7], in0=sm[:, 0:1], scalar1=-1.0, scalar2=1.0, op0=ALU.mult, op1=ALU.max), reads=[smB], writes=[smB])
            DVE.op(lambda h: h.tensor_tensor(out=sm[:, 0:1], in0=sm[:, 0:1], in1=sm[:, 6:7], op=ALU.max), reads=[smB], writes=[smB])
            DVE.op(lambda h: h.reciprocal(out=sm[:, 1:2], in_=sm[:, 0:1]), reads=[smB], writes=[smB])
            DVE.op(lambda h: h.tensor_tensor(out=sm[:, 2:3], in0=sm[:, 1:2], in1=el[:, bs, :], op=ALU.mult), reads=[smB, gB[bs]], writes=[smB])
            ACT.op(lambda h: h.activation(out=hb[:], in_=bank[nbk][:, 0:512], func=AF.Copy, scale=sm[:, 2:3]), reads=[bankB[nbk], smB], writes=[hbB])
            ACT.op(lambda h: h.activation(out=hjunk[:], in_=hb[:], func=AF.Square, accum_out=sm[:, 3:4]), reads=[hbB], writes=[hjB, smB])
            DVE.op(lambda h: h.tensor_scalar(out=sm[:, 4:5], in0=sm[:, 3:4], scalar1=1.0 / 512.0, scalar2=1e-6, op0=ALU.mult, op1=ALU.add), reads=[smB], writes=[smB])
            ACT.op(lambda h: h.activation(out=sm[:, 4:5], in_=sm[:, 4:5], func=AF.Ln), reads=[smB], writes=[smB])
            ACT.op(lambda h: h.activation(out=sm[:, 5:6], in_=sm[:, 4:5], func=AF.Exp, scale=-0.5), reads=[smB], writes=[smB])
            DVE.op(lambda h: h.scalar_tensor_tensor(out=htmp[:], in0=hb[:], scalar=sm[:, 5:6], in1=hg[:], op0=ALU.mult, op1=ALU.mult),
                   reads=[hbB, smB, cB], writes=[htmpB])
            DVE.op(lambda h: h.tensor_tensor(out=hstm[:], in0=htmp[:], in1=osg[:, bs, :], op=ALU.mult), reads=[htmpB, osgB[bs]], writes=[hstmB])
            os_ = t % 2
            for j in range(4):
                PE.group([lambda h: h.transpose(tp[:, 512 + j * 128:512 + (j + 1) * 128], hstm[:, j * 128:(j + 1) * 128], ident[:])],
                         reads=[hstmB, cB], writes=[tpB])
            DVE.op(lambda h: h.tensor_copy(out=hsT[:, os_, :, tok], in_=tp[:, 512:1024].rearrange("p (j t) -> p j t", j=4)), reads=[tpB], writes=[hsTB[os_]])

        def bulk_steps(t):
            st = []
            if t + 1 < ntile:
                st.append(lambda: load_norm(t + 1))
            for f in range(4):
                st.append(lambda f=f: proj_qk_f(t, f))
            st.append(lambda: ktrans(t))
            st.append(lambda: oproj(t, 0))
            st.append(lambda: oproj(t, 1))
            for blk in range(2):
                st.append(lambda blk=blk: gates(t, blk))
                st.append(lambda blk=blk: vproj_st(t, blk))
            return st

        def chain_steps(t):
            st = []
            for blk in range(2):
                st.append(lambda blk=blk: chunk(t, blk, 0))
                st.append(lambda blk=blk: chunk(t, blk, 1))
                st.append(lambda blk=blk: outstage(t, blk))
            return st

        load_norm(0)
        for f_ in bulk_steps(0):
            f_()
        for t in range(ntile):
            bs_ = bulk_steps(t + 1) if t + 1 < ntile else []
            for cs_ in chain_steps(t):
                cs_()
                for _ in range(2):
                    if bs_:
                        bs_.pop(0)()
            while bs_:
                bs_.pop(0)()
            SP.dma(ov[:, :, t * TQ:(t + 1) * TQ], hsT[:, t % 2], st_o[t % 2], reads=[hsTB[t % 2]], writes=[outB])
        SP.wait([Tok(s_.sem, s_.count) for s_ in st_o if s_.count > 0])
        toks = [Tok(e.sem, e.count) for e in (PE, ACT, DVE, POOL) if e.count > 0]
        toks += [Tok(s.sem, s.count) for s in [ld_x, ld_c, ld_w] + st_o if s.count > 0]
        for e in (PE, ACT, DVE, POOL, SP, GQ):
            e.wait(toks)


def _c(a):
    return np.ascontiguousarray(a)


def _build_attn():
    nc = bass.Bass("TRN2", target_bir_lowering=False)
    d = {}

    def inp(name, shape, dt=F32):
        d[name] = nc.dram_tensor(name, shape, dt, kind="ExternalInput").ap()
    inp("xfull", [2048, 4096]); inp("wqkv", [2048, 1536]); inp("gn", [128, 16]); inp("qg", [128, 1]); inp("kg", [128, 1])
    inp("mask", [128, ND * 128], BF16); inp("ident", [128, 128], BF16)
    d["oT"] = nc.dram_tensor("oT", [512, 4096], BF16, kind="ExternalOutput").ap()
    with ExitStack() as es:
        K = Kit(nc, es)
        phase_attn(nc, K, d)
    return nc


def _build_lstm():
    nc = bass.Bass("TRN2", target_bir_lowering=False)
    d = {}

    def inp(name, shape, dt=F32):
        d[name] = nc.dram_tensor(name, shape, dt, kind="ExternalInput").ap()
    inp("xfull", [2048, 4096]); inp("win", [2048, 1536]); inp("wg", [2048, 2]); inp("gn", [128, 16])
    inp("cw", [128, 4, 4]); inp("cb", [128, 4]); inp("hg", [128, 512]); inp("gbias", [128, 2])
    inp("tri", [128, 128]); inp("sel", [128, 2]); inp("ident", [128, 128], BF16)
    d["hsT"] = nc.dram_tensor("hsT", [512, 4096], BF16, kind="ExternalOutput").ap()
    with ExitStack() as es:
        K = Kit(nc, es)
        phase_lstm(nc, K, d)
    return nc


def _build_of(last):
    nc = bass.Bass("TRN2", target_bir_lowering=False)
    d = {}

    def inp(name, shape, dt=F32):
        d[name] = nc.dram_tensor(name, shape, dt, kind="ExternalInput").ap()
    inp("xin", [2048, NT]); inp("m", [2048, NT], BF16); inp("wo", [2048, 2048]); inp("wup", [2048, 2 * FFN]); inp("wdn", [FFN, 2048])
    inp("cw", [128, 88, 3]); inp("cb", [128, 88]); inp("gn", [128, 16]); inp("hv", [128, 1])
    d["xout"] = nc.dram_tensor("xout", [2048, NT], F32, kind="ExternalOutput").ap()
    with ExitStack() as es:
        K = Kit(nc, es)
        phase_of(nc, K, d, True, last)
    return nc


def _tok_shard(fullT, c):
    s0 = c * 1024
    out = np.zeros((fullT.shape[0], NT), fullT.dtype)
    lo = max(0, s0 - 4)
    out[:, 4 - (s0 - lo):] = fullT[:, lo:s0 + 1024]
    return out


def _of_inputs(z, L, xT_full, mT_full, wo):
    wup = _c(z["ffn_w_up"][L]); wdn = _c(z["ffn_w_down"][L])
    cw = _c(z["ffn_conv_w"][L].reshape(3, 88, 128).transpose(2, 1, 0))
    cb = _c(z["ffn_conv_b"][L].reshape(88, 128).T)
    gn = _c(z["ffn_norm"][L].reshape(16, 128).T)
    maps = []
    for core in range(8):
        b, c = core // 4, core % 4
        maps.append({"xin": _tok_shard(xT_full[b], c), "m": _tok_shard(mT_full[b], c), "wo": wo, "wup": wup, "wdn": wdn,
                     "cw": cw, "cb": cb, "gn": gn, "hv": np.full((128, 1), 0.0 if c == 0 else 1.0, np.float32)})
    return maps


def _lstm_inputs(z, xT_b, hd):
    w = z["lstm_w_in"][0]
    win = np.concatenate([w[:, hd * 256:(hd + 1) * 256], w[:, 1024 + hd * 256:1024 + (hd + 1) * 256],
                          w[:, 2048 + hd * 512:2048 + (hd + 1) * 512], w[:, 4096 + hd * 512:4096 + (hd + 1) * 512]], axis=1)
    wg = np.stack([w[:, 6144 + hd], w[:, 6148 + hd]], axis=1)
    cwf = z["lstm_conv_w"][0]; cbf = z["lstm_conv_b"][0]
    chans = np.concatenate([np.arange(hd * 256, (hd + 1) * 256), np.arange(1024 + hd * 256, 1024 + (hd + 1) * 256)])
    cw = cwf[:, chans].reshape(4, 4, 128).transpose(2, 1, 0)
    cb = cbf[chans].reshape(4, 128).T
    hg = np.broadcast_to(z["lstm_head_gain"][0][hd * 512:(hd + 1) * 512][None, :], (128, 512))
    gb = z["lstm_gate_bias"][0]
    gbias = np.broadcast_to(np.stack([gb[hd], gb[4 + hd]])[None, :], (128, 2))
    tri, sel = lstm_consts_np()
    return {"xfull": _c(xT_b), "win": _c(win), "wg": _c(wg), "gn": _c(z["lstm_norm"][0].reshape(16, 128).T), "cw": _c(cw), "cb": _c(cb),
            "hg": _c(hg), "gbias": _c(gbias), "tri": tri, "sel": sel, "ident": np.eye(128, dtype=np.float32).astype(ml_dtypes.bfloat16)}


def kernel(**inputs):
    z = {k: np.asarray(v, dtype=np.float32) for k, v in inputs.items()}
    x = z["x"]
    cores = list(range(8))
    xT = [_c(x[b].T) for b in range(2)]
    wqkv = z["attn_w_qkv"][0]
    gn = _c(z["attn_norm"][0].reshape(16, 128).T)
    qg = _c(z["attn_q_gain"][0].reshape(128, 1)); kg = _c(z["attn_k_gain"][0].reshape(128, 1))
    mask = attn_mask_np(); ident = np.eye(128, dtype=np.float32).astype(ml_dtypes.bfloat16)
    maps = []
    for core in cores:
        b, g = core // 4, core % 4
        w = np.concatenate([wqkv[:, i * 2048 + g * 512: i * 2048 + (g + 1) * 512] for i in range(3)], axis=1)
        maps.append({"xfull": xT[b], "wqkv": _c(w), "gn": gn, "qg": qg, "kg": kg, "mask": mask, "ident": ident})
    r1 = run_bass_kernel_spmd(_build_attn(), maps, core_ids=cores)
    oT = [np.concatenate([r1.results[b * 4 + g]["oT"] for g in range(4)], axis=0) for b in range(2)]
    r2 = run_bass_kernel_spmd(_build_of(False), _of_inputs(z, 0, xT, oT, _c(z["attn_w_o"][0])), core_ids=cores)
    x2T = [np.concatenate([r2.results[b * 4 + c]["xout"][:, 4:] for c in range(4)], axis=1) for b in range(2)]
    r3 = run_bass_kernel_spmd(_build_lstm(), [_lstm_inputs(z, x2T[core // 4], core % 4) for core in cores], core_ids=cores)
    hsT = [np.concatenate([r3.results[b * 4 + g]["hsT"] for g in range(4)], axis=0) for b in range(2)]
    r4 = run_bass_kernel_spmd(_build_of(True), _of_inputs(z, 1, x2T, hsT, _c(z["lstm_w_out"][0])), core_ids=cores)
    out = np.empty((2, 4096, 2048), np.float32)
    for core in cores:
        b, c = core // 4, core % 4
        out[b, c * 1024:(c + 1) * 1024, :] = r4.results[core]["xout"][:, 4:].T
    return out
```
